# Optimizing a Trainium2 kernel written in Bass

```python
import jax, jax.numpy as jnp
from jax import lax
import numpy as np

D_MODEL = 1024
BATCH = 32
SEQ = 2048
DEPTH = 1
DEC_BATCH = 16
DEC_SEQ = 16
PAST_LEN = 4096

CHUNK = 64
EPS = 1e-6
NEG_INF = -1e30
H_A = 8
NOPE_DIM = 64
ROPE_DIM = 32
V_DIM = 64
Q_RANK = 256
KV_RANK = 256
ROPE_THETA = 10000.0
MLA_SCALE = (NOPE_DIM + ROPE_DIM) ** -0.5
Q_BLOCK = 128
H_B = 8
HD_B = 64
N_PREV_CHUNKS = 8
BAND_PAST = N_PREV_CHUNKS * CHUNK
BAND = (N_PREV_CHUNKS + 1) * CHUNK
REL_CLIP = 256
N_REL = 2 * REL_CLIP + 1
BAND_SCALE = HD_B ** -0.5
MIX_A = H_A * V_DIM
MIX_B = H_B * HD_B
MIX_WIDTH = MIX_A + MIX_B
IN_COLS = Q_RANK + KV_RANK + ROPE_DIM + 3 * MIX_B
SPLITS = (Q_RANK, Q_RANK + KV_RANK, Q_RANK + KV_RANK + ROPE_DIM)
D_FF = -(-8 * D_MODEL // (3 * 256)) * 256

kernel_name = "hymba_mla_chunkband_stream_step"


def rmsnorm(x, g):
    xf = x.astype(jnp.float32)
    xf = xf * lax.rsqrt(jnp.mean(xf * xf, axis=-1, keepdims=True) + EPS)
    return xf.astype(x.dtype) * g


def rope(x, pos):
    inv = ROPE_THETA ** (-jnp.arange(0, ROPE_DIM, 2, dtype=jnp.float32) / ROPE_DIM)
    ang = pos.astype(jnp.float32)[:, None] * inv[None, :]
    cos = jnp.cos(ang)[None, :, None, :]
    sin = jnp.sin(ang)[None, :, None, :]
    x1, x2 = jnp.split(x.astype(jnp.float32), 2, axis=-1)
    return jnp.concatenate([x1 * cos - x2 * sin, x1 * sin + x2 * cos], axis=-1).astype(x.dtype)


def head_group_inputs(xn, pos, w_in, g_q, w_uq, g_kv):
    b, s, _ = xn.shape
    h = xn @ w_in
    c_q, c_kv, k_r, qkv = jnp.split(h, SPLITS, axis=-1)
    q = (rmsnorm(c_q, g_q) @ w_uq).reshape(b, s, H_A, NOPE_DIM + ROPE_DIM)
    q_nope = q[..., :NOPE_DIM]
    q_rope = rope(q[..., NOPE_DIM:], pos)
    c_kv = rmsnorm(c_kv, g_kv)
    k_rope = rope(k_r[:, :, None, :], pos)[:, :, 0, :]
    q_b, k_b, v_b = [t.reshape(b, s, H_B, HD_B) for t in jnp.split(qkv, 3, axis=-1)]
    return q_nope, q_rope, c_kv, k_rope, q_b, k_b, v_b


def mla_core(q_nope, q_rope, k_nope, k_rope, v, mask):
    s = (jnp.einsum('bqhd,bkhd->bhqk', q_nope, k_nope)
         + jnp.einsum('bqhr,bkr->bhqk', q_rope, k_rope)).astype(jnp.float32) * MLA_SCALE
    if mask is not None:
        s = jnp.where(mask, s, NEG_INF)
    p = jax.nn.softmax(s, axis=-1).astype(v.dtype)
    return jnp.einsum('bhqk,bkhd->bqhd', p, v)


def mla_prompt(q_nope, q_rope, c_kv, k_rope, w_uk, w_uv):
    b, s = q_nope.shape[:2]
    k_nope = jnp.einsum('bkc,chd->bkhd', c_kv, w_uk)
    v = jnp.einsum('bkc,chd->bkhd', c_kv, w_uv)
    nb = s // Q_BLOCK
    key_chunk = jnp.arange(s) // CHUNK

    def one_block(args):
        qn, qr, blk = args
        q_chunk = (blk * Q_BLOCK + jnp.arange(Q_BLOCK)) // CHUNK
        mask = key_chunk[None, :] <= q_chunk[:, None]
        return mla_core(qn, qr, k_nope, k_rope, v, mask)

    qn = q_nope.reshape(b, nb, Q_BLOCK, H_A, NOPE_DIM).swapaxes(0, 1)
    qr = q_rope.reshape(b, nb, Q_BLOCK, H_A, ROPE_DIM).swapaxes(0, 1)
    out = lax.map(one_block, (qn, qr, jnp.arange(nb)))
    return out.swapaxes(0, 1).reshape(b, s, MIX_A)


def mla_sample(q_nope, q_rope, c_kv, k_rope, cache_ckv, cache_kr, w_uk, w_uv):
    b, s = q_nope.shape[:2]
    ckv_all = jnp.concatenate([cache_ckv, c_kv], axis=1)
    kr_all = jnp.concatenate([cache_kr, k_rope], axis=1)
    k_nope = jnp.einsum('bkc,chd->bkhd', ckv_all, w_uk)
    v = jnp.einsum('bkc,chd->bkhd', ckv_all, w_uv)
    return mla_core(q_nope, q_rope, k_nope, kr_all, v, None).reshape(b, s, MIX_A)


def band_core(q, k, v, dist, valid, rel_bias):
    bias = rel_bias[:, jnp.clip(dist, -REL_CLIP, REL_CLIP) + REL_CLIP]
    s = jnp.einsum('...qhd,...khd->...hqk', q, k).astype(jnp.float32) * BAND_SCALE \
        + bias.astype(jnp.float32)
    if valid is not None:
        s = jnp.where(valid, s, NEG_INF)
    p = jax.nn.softmax(s, axis=-1).astype(v.dtype)
    return jnp.einsum('...hqk,...khd->...qhd', p, v)


def band_prompt(q, k, v, rel_bias):
    b, s = q.shape[:2]
    nc = s // CHUNK
    qc = q.reshape(b, nc, CHUNK, H_B, HD_B)
    pad = jnp.zeros((b, BAND_PAST, H_B, HD_B), k.dtype)
    kp = jnp.concatenate([pad, k], axis=1).reshape(b, nc + N_PREV_CHUNKS, CHUNK, H_B, HD_B)
    vp = jnp.concatenate([pad, v], axis=1).reshape(b, nc + N_PREV_CHUNKS, CHUNK, H_B, HD_B)
    idx = jnp.arange(nc)[:, None] + jnp.arange(N_PREV_CHUNKS + 1)[None, :]
    kb = kp[:, idx].reshape(b, nc, BAND, H_B, HD_B)
    vb = vp[:, idx].reshape(b, nc, BAND, H_B, HD_B)
    dist = jnp.arange(CHUNK)[:, None] - (jnp.arange(BAND) - BAND_PAST)[None, :]
    key_chunk = jnp.arange(nc)[:, None] - N_PREV_CHUNKS + (jnp.arange(BAND) // CHUNK)[None, :]
    valid = (key_chunk >= 0)[:, None, None, :]
    out = band_core(qc, kb, vb, dist, valid, rel_bias)
    return out.reshape(b, s, MIX_B)


def band_sample(q, k, v, cache_k, cache_v, rel_bias):
    b, t = q.shape[:2]
    n_c = cache_k.shape[1]
    k_all = jnp.concatenate([cache_k, k], axis=1)
    v_all = jnp.concatenate([cache_v, v], axis=1)
    kpos = jnp.concatenate([jnp.arange(n_c) - n_c, jnp.arange(t)])
    dist = jnp.arange(t)[:, None] - kpos[None, :]
    return band_core(q, k_all, v_all, dist, None, rel_bias).reshape(b, t, MIX_B)


def layer_forward(x, pos, cache, w_in, g_attn, g_q, w_uq, g_kv, w_uk, w_uv, rel_bias,
                  g_out_a, g_out_b, w_out, g_ffn, w_gate, w_up, w_down):
    xn = rmsnorm(x, g_attn)
    q_nope, q_rope, c_kv, k_rope, q_b, k_b, v_b = head_group_inputs(xn, pos, w_in, g_q, w_uq, g_kv)
    if cache is None:
        out_a = mla_prompt(q_nope, q_rope, c_kv, k_rope, w_uk, w_uv)
        out_b = band_prompt(q_b, k_b, v_b, rel_bias)
        n_keep = min(BAND_PAST, x.shape[1])
        new_state = (c_kv, k_rope, k_b[:, x.shape[1] - n_keep:], v_b[:, x.shape[1] - n_keep:])
    else:
        cache_ckv, cache_kr, cache_bk, cache_bv = cache
        out_a = mla_sample(q_nope, q_rope, c_kv, k_rope, cache_ckv, cache_kr, w_uk, w_uv)
        out_b = band_sample(q_b, k_b, v_b, cache_bk, cache_bv, rel_bias)
        new_state = (c_kv, k_rope, k_b, v_b)
    mix = jnp.concatenate([rmsnorm(out_a, g_out_a), rmsnorm(out_b, g_out_b)], axis=-1)
    x = x + mix @ w_out
    h = rmsnorm(x, g_ffn)
    x = x + (jax.nn.silu(h @ w_gate) * (h @ w_up)) @ w_down
    return x, new_state


def setup_inputs(seed: int = 0) -> dict:
    key = jax.random.key(seed)
    ks = jax.random.split(key, 24)
    f32 = jnp.float32

    def nrm(k, shape, scale=1.0):
        return jax.random.normal(k, shape, f32) * scale

    def gain(k, shape):
        return 1.0 + 0.02 * jax.random.normal(k, shape, f32)

    n_band = min(BAND_PAST, PAST_LEN)
    return {
        "x_prompt": nrm(ks[0], (BATCH, SEQ, D_MODEL)),
        "x_sample": nrm(ks[1], (DEC_BATCH, DEC_SEQ, D_MODEL)),
        "cache_mla_ckv": nrm(ks[2], (DEPTH, DEC_BATCH, PAST_LEN, KV_RANK)),
        "cache_mla_krope": nrm(ks[3], (DEPTH, DEC_BATCH, PAST_LEN, ROPE_DIM)),
        "cache_band_k": nrm(ks[4], (DEPTH, DEC_BATCH, n_band, H_B, HD_B)),
        "cache_band_v": nrm(ks[5], (DEPTH, DEC_BATCH, n_band, H_B, HD_B)),
        "w_in": nrm(ks[6], (DEPTH, D_MODEL, IN_COLS), D_MODEL ** -0.5),
        "g_attn": gain(ks[7], (DEPTH, D_MODEL)),
        "g_q": gain(ks[8], (DEPTH, Q_RANK)),
        "w_uq": nrm(ks[9], (DEPTH, Q_RANK, H_A * (NOPE_DIM + ROPE_DIM)), Q_RANK ** -0.5),
        "g_kv": gain(ks[10], (DEPTH, KV_RANK)),
        "w_uk": nrm(ks[11], (DEPTH, KV_RANK, H_A, NOPE_DIM), KV_RANK ** -0.5),
        "w_uv": nrm(ks[12], (DEPTH, KV_RANK, H_A, V_DIM), KV_RANK ** -0.5),
        "rel_bias": nrm(ks[13], (DEPTH, H_B, N_REL), 0.5),
        "g_out_a": gain(ks[14], (DEPTH, MIX_A)),
        "g_out_b": gain(ks[15], (DEPTH, MIX_B)),
        "w_out": nrm(ks[16], (DEPTH, MIX_WIDTH, D_MODEL), MIX_WIDTH ** -0.5),
        "g_ffn": gain(ks[17], (DEPTH, D_MODEL)),
        "w_gate": nrm(ks[18], (DEPTH, D_MODEL, D_FF), D_MODEL ** -0.5),
        "w_up": nrm(ks[19], (DEPTH, D_MODEL, D_FF), D_MODEL ** -0.5),
        "w_down": nrm(ks[20], (DEPTH, D_FF, D_MODEL), D_FF ** -0.5),
        "g_final": gain(ks[21], (D_MODEL,)),
    }


def reference(x_prompt, x_sample, cache_mla_ckv, cache_mla_krope, cache_band_k, cache_band_v,
              w_in, g_attn, g_q, w_uq, g_kv, w_uk, w_uv, rel_bias, g_out_a, g_out_b, w_out,
              g_ffn, w_gate, w_up, w_down, g_final):
    past_len = cache_mla_ckv.shape[2]
    pos_p = jnp.arange(x_prompt.shape[1], dtype=jnp.int32)
    pos_s = past_len + jnp.arange(x_sample.shape[1], dtype=jnp.int32)
    yp, ys = x_prompt, x_sample
    st_p, st_s = [], []
    for l in range(DEPTH):
        w = (w_in[l], g_attn[l], g_q[l], w_uq[l], g_kv[l], w_uk[l], w_uv[l], rel_bias[l],
             g_out_a[l], g_out_b[l], w_out[l], g_ffn[l], w_gate[l], w_up[l], w_down[l])
        yp, sp = layer_forward(yp, pos_p, None, *w)
        ys, ss = layer_forward(ys, pos_s, (cache_mla_ckv[l], cache_mla_krope[l],
                                           cache_band_k[l], cache_band_v[l]), *w)
        st_p.append(sp)
        st_s.append(ss)
    y_prompt = rmsnorm(yp, g_final)
    y_sample = rmsnorm(ys, g_final)
    new_ckv_prompt = jnp.stack([s[0] for s in st_p])
    new_kr_prompt = jnp.stack([s[1] for s in st_p])
    new_bk_prompt = jnp.stack([s[2] for s in st_p])
    new_bv_prompt = jnp.stack([s[3] for s in st_p])
    new_ckv_sample = jnp.stack([s[0] for s in st_s])
    new_kr_sample = jnp.stack([s[1] for s in st_s])
    new_bk_sample = jnp.stack([s[2] for s in st_s])
    new_bv_sample = jnp.stack([s[3] for s in st_s])
    return (y_prompt, y_sample, new_ckv_prompt, new_kr_prompt, new_bk_prompt, new_bv_prompt,
            new_ckv_sample, new_kr_sample, new_bk_sample, new_bv_sample)
```

```python
import contextlib
import math
import numpy as np
import concourse.bass as bass
import concourse.mybir as mybir
from concourse.bass_utils import run_bass_kernel_spmd

F32 = mybir.dt.float32
BF16 = mybir.dt.bfloat16
I32 = mybir.dt.int32
AF = mybir.ActivationFunctionType
ALU = mybir.AluOpType
AX = mybir.AxisListType


class Op:
    __slots__ = ("eng", "fn", "idx", "deps", "signal", "sig", "waits", "is_dma",
                 "dsem", "dval", "dwaits")

    def __init__(self, eng, fn, is_dma):
        self.eng = eng
        self.fn = fn
        self.is_dma = is_dma
        self.signal = False
        self.sig = 0
        self.waits = []
        self.dwaits = []
        self.deps = ()
        self.dsem = None
        self.dval = 0


class TileState:
    __slots__ = ("w", "r", "rd")

    def __init__(self):
        self.w = None
        self.r = {}
        self.rd = []


class Prog:
    ENGS = ("pe", "act", "dve", "pool", "sp")
    NDS = 20

    def __init__(self, nc, stack):
        self.nc = nc
        self.stack = stack
        self.eng_ops = {e: [] for e in self.ENGS}
        self.tiles = {}
        self.esem = {e: stack.enter_context(nc.semaphore("s_" + e)) for e in self.ENGS}
        self.nds = {"sp": 20, "pool": 44, "act": 1}
        self.dsems = {q: [stack.enter_context(nc.semaphore("d_%s%d" % (q, i)))
                          for i in range(self.nds[q])] for q in ("sp", "pool", "act")}
        self.dcount = {"sp": 0, "pool": 0, "act": 0}
        self.all_ops = []

    def _ts(self, key):
        t = self.tiles.get(key)
        if t is None:
            t = TileState()
            self.tiles[key] = t
        return t

    def op(self, eng, fn, reads=(), writes=(), dma=False):
        o = Op(eng, fn, dma)
        deps = []
        for k in reads:
            t = self._ts(k)
            if t.w is not None:
                deps.append(t.w)
            if isinstance(k, tuple) and k[0] == "ps":
                deps.extend(o2 for e2, o2 in t.r.items() if e2 != eng)
        for k in writes:
            t = self._ts(k)
            if t.w is not None:
                deps.append(t.w)
            deps.extend(t.r.values())
            deps.extend(t.rd)
        for k in reads:
            t = self._ts(k)
            if dma:
                t.rd.append(o)
            else:
                t.r[eng] = o
        for k in writes:
            t = self._ts(k)
            t.w = o
            t.r = {}
            t.rd = []
        o.deps = deps
        o.idx = len(self.eng_ops[eng])
        self.eng_ops[eng].append(o)
        self.all_ops.append(o)
        if dma:
            n = self.dcount[eng]
            self.dcount[eng] = n + 1
            o.dsem = self.dsems[eng][n % self.nds[eng]]
            o.dval = 16 * (n // self.nds[eng] + 1)
        return o

    def barrier(self):
        last = []
        for e in self.ENGS:
            for o in reversed(self.eng_ops[e]):
                if o.fn is not None and not o.is_dma:
                    last.append(o)
                    break
        pend = [o for o in self.all_ops if o.is_dma]
        for e in self.ENGS:
            o = Op(e, None, False)
            o.deps = [d for d in last if not d.is_dma] + pend
            o.idx = len(self.eng_ops[e])
            self.eng_ops[e].append(o)
            self.all_ops.append(o)
        self.tiles = {}

    def dma(self, q, out, in_, reads=(), writes=(), **kw):
        return self.op(q, lambda e: e.dma_start(out=out, in_=in_, **kw), reads, writes, dma=True)

    def resolve(self):
        waited = {e: {} for e in self.ENGS}
        dwaited = {e: {} for e in self.ENGS}
        for o in self.all_ops:
            E = o.eng
            need = {}
            dneed = {}
            for d in o.deps:
                if d is o:
                    continue
                if d.is_dma:
                    k = id(d.dsem)
                    if dneed.get(k, (None, 0))[1] < d.dval:
                        dneed[k] = (d.dsem, d.dval)
                else:
                    if d.eng == "pe" and E == "pe":
                        continue
                    c = need.get(d.eng)
                    if c is None or d.idx > c.idx:
                        need[d.eng] = d
            if o.is_dma and o.dval > 16:
                k = id(o.dsem)
                if dneed.get(k, (None, 0))[1] < o.dval - 16:
                    dneed[k] = (o.dsem, o.dval - 16)
            for P, d in need.items():
                if waited[E].get(P, -1) >= d.idx:
                    continue
                waited[E][P] = d.idx
                d.signal = True
                o.waits.append(d)
            for k, (sem, val) in dneed.items():
                if dwaited[E].get(k, 0) >= val:
                    continue
                dwaited[E][k] = val
                o.dwaits.append((sem, val))
        for e in self.ENGS:
            c = 0
            for o in self.eng_ops[e]:
                if o.is_dma:
                    continue
                if o.signal:
                    c += 1
                    o.sig = c

    def _emit_eng(self, name, e):
        for o in self.eng_ops[name]:
            for d in o.waits:
                e.wait_ge(self.esem[d.eng], d.sig)
            for sem, val in o.dwaits:
                e.wait_ge(sem, val)
            if o.fn is None:
                continue
            ins = o.fn(e)
            if o.is_dma:
                ins.then_inc(o.dsem, 16)
            elif o.signal:
                ins.then_inc(self.esem[name], 1)

    def finish(self):
        self.resolve()
        nc = self.nc
        with nc.Block() as block:
            @block.tensor
            def _(e):
                self._emit_eng("pe", e)

            @block.scalar
            def _(e):
                self._emit_eng("act", e)

            @block.vector
            def _(e):
                self._emit_eng("dve", e)

            @block.gpsimd
            def _(e):
                self._emit_eng("pool", e)
                for q in ("pool",):
                    n = self.dcount[q]
                    for i in range(min(n, self.nds[q])):
                        cnt = (n - 1 - i) // self.nds[q] + 1
                        e.wait_ge(self.dsems[q][i], 16 * cnt)

            @block.sync
            def _(e):
                self._emit_eng("sp", e)
                for q in ("sp",):
                    n = self.dcount[q]
                    for i in range(min(n, self.nds[q])):
                        cnt = (n - 1 - i) // self.nds[q] + 1
                        e.wait_ge(self.dsems[q][i], 16 * cnt)


D = 1024
DFF = 2816
NFC = 22
EPS = 1e-6
MLA_SCALE = 96 ** -0.5
PAST_FULL = 4096
SL_A, SL_V, SL_KB, SL_QB, SL_S, SL_O0, SL_O1 = 0, 1, 2, 3, 4, 5, 6
SL_G0 = 7
SL_D0 = 19
NSLOT = 25
S_KR, S_UQ, S_UK, S_UV = 0, 256, 1792, 2816
G_CHUNKS = [[0, 1], [2, 3], [4, 5], [6, 7], [8, 9], [10], [11, 12], [13, 14], [15, 16], [17, 18], [19, 20], [21]]
D_CHUNKS = [[0, 1, 2, 3], [4, 5, 6, 7], [8, 9, 10], [11, 12, 13, 14], [15, 16, 17, 18], [19, 20, 21]]
BAND_T = [((0, 0), 1, 1), ((0, 2), 3, 1), ((0, 4), 5, 1), ((0, 6), 7, 1),
          ((1, 7), 0, 0), ((3, 7), 2, 0), ((5, 7), 4, 0), ((7, 7), 6, 0)]


def build_program(NSEQ, S, NSS, PAST):
    nc = bass.Bass("TRN2", target_bir_lowering=False)
    NT = S // 128
    NST = S // 512
    NKT = PAST // 128
    dt_in = lambda name, shape: nc.dram_tensor(name, shape, F32, kind="ExternalInput")
    dt_out = lambda name, shape: nc.dram_tensor(name, shape, F32, kind="ExternalOutput")
    NSEQd, NSSd = max(NSEQ, 1), max(NSS, 1)
    x_p = dt_in("x_prompt", [NSEQd, S, D])
    x_s = dt_in("x_sample", [NSSd * 16, D])
    c_ckv = dt_in("cache_mla_ckv", [NSSd, PAST, 256])
    c_kr = dt_in("cache_mla_krope", [NSSd, PAST, 32])
    c_bk = dt_in("cache_band_k", [NSSd, 512, 512])
    c_bv = dt_in("cache_band_v", [NSSd, 512, 512])
    w_in = dt_in("w_in", [D, 2080])
    g_attn = dt_in("g_attn", [D])
    g_q = dt_in("g_q", [256])
    w_uq = dt_in("w_uq", [256, 768])
    g_kv = dt_in("g_kv", [256])
    w_uk = dt_in("w_uk", [256, 512])
    w_uv = dt_in("w_uv", [256, 512])
    rel_bias = dt_in("rel_bias", [8, 513])
    g_out_a = dt_in("g_out_a", [512])
    g_out_b = dt_in("g_out_b", [512])
    w_out = dt_in("w_out", [D, D])
    g_ffn = dt_in("g_ffn", [D])
    w_gate = dt_in("w_gate", [D, DFF])
    w_up = dt_in("w_up", [D, DFF])
    w_down = dt_in("w_down", [DFF, D])
    g_final = dt_in("g_final", [D])
    y_p = dt_out("y_prompt", [NSEQd, S, D])
    y_s = dt_out("y_sample", [NSSd * 16, D])
    o_ckv_p = dt_out("new_ckv_prompt", [NSEQd, S, 256])
    o_kr_p = dt_out("new_kr_prompt", [NSEQd, S, 32])
    o_bk_p = dt_out("new_bk_prompt", [NSEQd, 512, 512])
    o_bv_p = dt_out("new_bv_prompt", [NSEQd, 512, 512])
    o_ckv_s = dt_out("new_ckv_sample", [NSSd * 16, 256])
    o_kr_s = dt_out("new_kr_sample", [NSSd * 16, 32])
    o_bk_s = dt_out("new_bk_sample", [NSSd * 16, 512])
    o_bv_s = dt_out("new_bv_sample", [NSSd * 16, 512])
    wsc = nc.dram_tensor("wsc", [NSLOT, 128, 4096], BF16, kind="Internal")
    rbp = nc.dram_tensor("rbp", [8, 641], F32, kind="Internal")

    with contextlib.ExitStack() as st:
        P = Prog(nc, st)
        sbt = lambda name, shape, dt: st.enter_context(nc.sbuf_tensor(name, shape, dt))
        NTT = max(NT, 1) + 1

        ident = sbt("ident", [128, 128], BF16)
        identf = sbt("identf", [128, 128], F32)
        gat = sbt("gat", [128, 8], F32)
        gqT = sbt("gqT", [128, 2], F32)
        gmix = sbt("gmix", [128, 8], F32)
        gffn = sbt("gffn", [128, 8], F32)
        gkv_bc = sbt("gkv_bc", [128, 256], F32)
        gfin_bc = sbt("gfin_bc", [128, D], F32)
        cbias = sbt("cbias", [128, 8], F32)
        toep = sbt("toep", [128, 8, 384], F32)
        cs = sbt("cs", [128, NTT, 2, 16], F32)
        cst = sbt("cst", [128, 4], F32)
        stat = sbt("stat", [128, 64], F32)
        PS = st.enter_context(nc.psum_tensor("PS", [128, 8, 512], F32))

        def psf(b, rows=128):
            return PS[0:rows, b, :]

        def psb(b, rows=128):
            return PS[0:rows, b, :].bitcast(BF16)

        ARENA = 83360
        arena = sbt("arena", [128, ARENA], BF16)
        apos = [0]

        def carve(n_el, dt=BF16):
            n16 = n_el * (2 if dt == F32 else 1)
            n16 = (n16 + 15) // 16 * 16
            a = apos[0]
            apos[0] += n16
            assert apos[0] <= ARENA, ("arena overflow", apos[0])
            v = arena[:, a:a + n_el * (2 if dt == F32 else 1)]
            return v.bitcast(F32) if dt == F32 else v

        def ACT(out, in_, func, r, w, **kw):
            P.op("act", lambda e: e.activation(out=out, in_=in_, func=func, **kw), r, w)

        def TT(out, in0, in1, op, r, w, eng="dve"):
            P.op(eng, lambda e: e.tensor_tensor(out=out, in0=in0, in1=in1, op=op), r, w)

        def TS(out, in0, s1, op0, r, w, s2=None, op1=None, eng="dve"):
            if op1 is None:
                P.op(eng, lambda e: e.tensor_scalar(out=out, in0=in0, scalar1=s1, scalar2=None, op0=op0), r, w)
            else:
                P.op(eng, lambda e: e.tensor_scalar(out=out, in0=in0, scalar1=s1, scalar2=s2, op0=op0, op1=op1), r, w)

        def CP(out, in_, r, w, eng="dve"):
            if eng == "act":
                P.op("act", lambda e: e.copy(out=out, in_=in_), r, w)
            else:
                P.op(eng, lambda e: e.tensor_copy(out=out, in_=in_), r, w)

        def MM(out, lhsT, rhs, start, stop, r, w, sg=False):
            P.op("pe", lambda e: e.matmul(out, lhsT=lhsT, rhs=rhs, start=start, stop=stop, skip_group_check=sg), r, w)

        def TR(out, in_, idn, r, w):
            P.op("pe", lambda e: e.transpose(out=out, in_=in_, identity=idn), r, w)

        def MEMSET(ap, val, w, eng="pool"):
            P.op(eng, lambda e: e.memset(ap, val), (), w)

        rr = {"mm": 0, "sc": 0, "acc": 0, "misc": 0}
        POOLS = {"mm": [2, 3, 4, 5, 6, 7], "sc": [2, 3, 4, 5], "acc": [0, 1], "misc": [6, 7]}

        def bank(pool):
            lst = POOLS[pool]
            b = lst[rr[pool] % len(lst)]
            rr[pool] += 1
            return b

        MEMSET(identf[:], 0.0, ["identf"])
        P.op("pool", lambda e: e.affine_select(out=identf[:], in_=identf[:], pattern=[[-1, 128]],
                                                 compare_op=ALU.not_equal, fill=1.0, base=0,
                                                 channel_multiplier=1), ["identf"], ["identf"])
        CP(ident[:], identf[:], ["identf"], ["ident"])
        MEMSET(cst[:, 0:1], EPS, ["cst"])
        P.dma("sp", gat[:], g_attn.ap().rearrange("(c p) -> p c", p=128), writes=["gains"], allow_slow_non_contiguous=True)
        P.dma("sp", gffn[:], g_ffn.ap().rearrange("(c p) -> p c", p=128), writes=["gains"], allow_slow_non_contiguous=True)
        P.dma("sp", gqT[:], g_q.ap().rearrange("(c p) -> p c", p=128), writes=["gains"], allow_slow_non_contiguous=True)
        P.dma("sp", gmix[:, 0:4], g_out_a.ap().rearrange("(c p) -> p c", p=128), writes=["gains"], allow_slow_non_contiguous=True)
        P.dma("sp", gmix[:, 4:8], g_out_b.ap().rearrange("(c p) -> p c", p=128), writes=["gains"], allow_slow_non_contiguous=True)
        P.dma("sp", gkv_bc[:], bass.AP(g_kv, 0, [[0, 128], [1, 256]]), writes=["gkv_bc"])
        P.dma("sp", gfin_bc[:], bass.AP(g_final, 0, [[0, 128], [1, D]]), writes=["gfin_bc"])
        P.dma("sp", cbias[:].unsqueeze(2), bass.AP(rel_bias, 512, [[0, 128], [513, 8], [1, 1]]), writes=["cbias"], allow_slow_non_contiguous=True)
        P.dma("sp", rbp.ap()[:, 0:513], rel_bias.ap(), writes=["rbp"])
        P.dma("sp", rbp.ap()[:, 513:641].unsqueeze(2), bass.AP(rel_bias, 512, [[513, 8], [0, 128], [1, 1]]), writes=["rbp"], allow_slow_non_contiguous=True)
        def load_toeplitz():
            for k in range(128):
                P.dma("sp", toep[k:k + 1, :, :], rbp.ap()[:, 256 - k:640 - k].unsqueeze(0), reads=["rbp"], writes=["toep"])

        toep_state = {"done": False}
        posi = sbt("posi", [128, NTT], I32)
        posf = sbt("posf", [128, NTT], F32)
        inv = sbt("inv", [128, 16], F32)
        aT = carve(11 * 512).rearrange("p (c s) -> p c s", c=11)
        aTf = aT[:].rearrange("p c s -> p (c s)")
        ang = aTf[:, 0:2 * NTT * 16].bitcast(F32).rearrange("p (t j) -> p t j", t=NTT)
        angi = aTf[:, 1024:1024 + 2 * NTT * 16].bitcast(I32).rearrange("p (t j) -> p t j", t=NTT)
        angr = aTf[:, 2048:2048 + 2 * NTT * 16].bitcast(F32).rearrange("p (t j) -> p t j", t=NTT)
        AK = [("aT", i) for i in range(11)]
        P.op("pool", lambda e: e.iota(out=posi[:, 0:NTT - 1], pattern=[[128, NTT - 1]], base=0, channel_multiplier=1), (), ["posi"])
        P.op("pool", lambda e: e.iota(out=posi[:, NTT - 1:NTT], pattern=[[0, 1]], base=0, channel_multiplier=1), (), ["posi"])
        P.op("dve", lambda e: e.tensor_single_scalar(out=posi[:, NTT - 1:NTT], in_=posi[:, NTT - 1:NTT], scalar=15, op=ALU.bitwise_and),
             ["posi"], ["posi"])
        P.op("dve", lambda e: e.tensor_single_scalar(out=posi[:, NTT - 1:NTT], in_=posi[:, NTT - 1:NTT], scalar=PAST, op=ALU.add),
             ["posi"], ["posi"])
        CP(posf[:], posi[:], ["posi"], ["posf"])
        for j in range(16):
            MEMSET(inv[:, j:j + 1], float(np.float32(10000.0) ** np.float32(-(2 * j) / 32.0)), ["inv"])
        for t in range(NTT):
            TS(ang[:, t, :], inv[:], posf[:, t:t + 1], ALU.mult, ["inv", "posf"], ["ang"])
        angf = ang[:].rearrange("p t j -> p (t j)")
        angif = angi[:].rearrange("p t j -> p (t j)")
        angrf = angr[:].rearrange("p t j -> p (t j)")
        TS(angf, angf, 1.0 / (2 * math.pi), ALU.mult, ["ang"], ["ang"])
        CP(angif, angf, ["ang"], ["angi"])
        CP(angrf, angif, ["angi"], ["angr"])
        TT(angf, angf, angrf, ALU.subtract, ["ang", "angr"], ["ang"])
        for t in range(NTT):
            ACT(cs[:, t, 1, :], ang[:, t, :], AF.Sin, ["ang"], ["cs"], scale=2 * math.pi)
        TS(angf, angf, 0.25, ALU.add, ["ang"], ["ang"])
        TS(angrf, angf, 0.5, ALU.is_gt, ["ang"], ["angr"])
        TT(angf, angf, angrf, ALU.subtract, ["ang", "angr"], ["ang"])
        for t in range(NTT):
            ACT(cs[:, t, 0, :], ang[:, t, :], AF.Sin, ["ang"], ["cs"], scale=2 * math.pi)
        P.op("dve", lambda e: e.memset(aT[:, 0, 0:16], 0.0), ["ang", "angi", "angr"], AK + ["ang", "angi", "angr"])

        def wview(slot, off, kc, n):
            return wsc.ap()[slot, :, off:off + kc * n].rearrange("p (k n) -> p k n", k=kc)

        def cast_cols(slot, off, src, c0, n, kc):
            P.dma("pool", wview(slot, off, kc, n), src.ap()[:, c0:c0 + n].rearrange("(k p) n -> p k n", p=128),
                  writes=[("wsc", slot)])

        casted = set()

        def cast_slot(slot):
            if slot in casted:
                return
            casted.add(slot)
            if slot == SL_A:
                cast_cols(SL_A, 0, w_in, 0, 512, 8)
            elif slot == SL_V:
                cast_cols(SL_V, 0, w_in, 1568, 512, 8)
            elif slot == SL_KB:
                cast_cols(SL_KB, 0, w_in, 1056, 512, 8)
            elif slot == SL_QB:
                cast_cols(SL_QB, 0, w_in, 544, 512, 8)
            elif slot == SL_S:
                cast_cols(SL_S, S_KR, w_in, 512, 32, 8)
                cast_cols(SL_S, S_UQ, w_uq, 0, 768, 2)
                cast_cols(SL_S, S_UK, w_uk, 0, 512, 2)
                cast_cols(SL_S, S_UV, w_uv, 0, 512, 2)
            elif slot == SL_O0:
                cast_cols(SL_O0, 0, w_out, 0, 512, 8)
            elif slot == SL_O1:
                cast_cols(SL_O1, 0, w_out, 512, 512, 8)
            elif slot < SL_D0:
                gi = slot - SL_G0
                chunks = G_CHUNKS[gi]
                n = len(chunks)
                c0 = chunks[0] * 128
                dst = wsc.ap()[SL_G0 + gi, :, :].rearrange("p (k n) -> p k n", k=8)
                P.dma("pool", dst[:, :, 0:n * 128], w_gate.ap()[:, c0:c0 + n * 128].rearrange("(k p) n -> p k n", p=128),
                      writes=[("wsc", SL_G0 + gi)])
                P.dma("pool", dst[:, :, 256:256 + n * 128], w_up.ap()[:, c0:c0 + n * 128].rearrange("(k p) n -> p k n", p=128),
                      writes=[("wsc", SL_G0 + gi)])
            else:
                di = slot - SL_D0
                chunks = D_CHUNKS[di]
                f0 = chunks[0] * 128
                n = len(chunks)
                P.dma("pool", wsc.ap()[SL_D0 + di, :, 0:n * 1024].rearrange("p (c n) -> p c n", c=n),
                      w_down.ap()[f0:f0 + n * 128, :].rearrange("(c p) n -> p c n", p=128),
                      writes=[("wsc", SL_D0 + di)])

        NRING = 3
        ring = [sbt("ring%d" % i, [128, 4096], BF16) for i in range(NRING)]
        wq = {"seq": [], "loaded": 0, "used": 0}

        def slot_elems(slot):
            if slot == SL_S:
                return 3840
            if SL_G0 <= slot < SL_D0:
                return 8 * 512
            if slot >= SL_D0:
                return len(D_CHUNKS[slot - SL_D0]) * 1024
            return 4096

        def wload_upto(n):
            while wq["loaded"] < min(n, len(wq["seq"])):
                i = wq["loaded"]
                slot = wq["seq"][i]
                for j_ in range(i, min(i + 10, len(wq["seq"]), 25)):
                    cast_slot(wq["seq"][j_])
                ne = slot_elems(slot)
                if slot in (SL_G0 + 5, SL_G0 + 11):
                    P.dma("sp", ring[i % NRING][:, 0:4096].rearrange("p (k a n) -> p k a n", k=8, a=2)[:, :, :, 0:128],
                          wsc.ap()[slot, :, :].rearrange("p (k a n) -> p k a n", k=8, a=2)[:, :, :, 0:128],
                          reads=[("wsc", slot)], writes=[("wr", i % NRING)])
                else:
                    P.dma("sp", ring[i % NRING][:, 0:ne], wsc.ap()[slot, :, 0:ne],
                          reads=[("wsc", slot)], writes=[("wr", i % NRING)])
                wq["loaded"] += 1

        def wnext(slot, hold=0):
            i = wq["used"]
            assert wq["seq"][i] == slot, (i, wq["seq"][i], slot)
            wload_upto(i + NRING - hold)
            wq["used"] += 1
            return ring[i % NRING], ("wr", i % NRING)

        PASS_SLOTS = [SL_A, SL_V, SL_KB, SL_QB, SL_S, SL_O0, SL_O1]
        FFN_SLOTS = []
        for g in range(2):
            FFN_SLOTS += [SL_G0 + 6 * g + i for i in range(6)] + [SL_D0 + 3 * g + i for i in range(3)]
        n_pass = NSEQ * NST + (1 if NSS else 0)
        wq["seq"] = (PASS_SLOTS + FFN_SLOTS) * n_pass

        KT = carve(8 * S).rearrange("p (h s) -> p h s", h=8)
        VmF = carve(NT * 520 + 64)
        Vm = VmF[:, 0:NT * 520].rearrange("p (t h d) -> p t h d", t=NT, h=8)
        KbT = carve(4 * 1024).rearrange("p (j s) -> p j s", j=4)
        VbF = carve(8 * 520 + 64)
        Vb = VbF[:, 0:8 * 520].rearrange("p (t h d) -> p t h d", t=8, h=8)
        persist_end = apos[0]
        actT = carve(8 * 512).rearrange("p (c s) -> p c s", c=8)
        QT = carve(8 * 512).rearrange("p (h s) -> p h s", h=8)
        qb_off = apos[0]
        QbT = carve(4 * 512).rearrange("p (j s) -> p j s", j=4)
        cqnT = carve(2 * 512).rearrange("p (c s) -> p c s", c=2)
        ckvnT = carve(2 * 512).rearrange("p (c s) -> p c s", c=2)
        assert apos[0] == qb_off + 4096
        stgB = arena[:, qb_off:qb_off + 4096].bitcast(F32).rearrange("p (k d) -> p k d", k=2)
        xs = carve(1024)
        x1 = carve(4 * 1024, F32).rearrange("p (t d) -> p t d", t=4)
        oa = carve(4 * 1024).rearrange("p (t d) -> p t d", t=4)
        NPT = 6
        PTall = carve(NPT * 512)
        PT = [PTall[:, i * 512:(i + 1) * 512] for i in range(NPT)]
        xsb = [(xs, ["xs"]), (PTall[:, 0:1024], [("PT", 0), ("PT", 1)])]
        junk = PTall[:, 1024:2048]
        JK = [("PT", 2), ("PT", 3)]
        oaF = oa[:].rearrange("p t d -> p (t d)").bitcast(F32).rearrange("p (k d) -> p k d", k=2)
        okeys = lambda k: [("oa", 2 * k, 0), ("oa", 2 * k, 1), ("oa", 2 * k + 1, 0), ("oa", 2 * k + 1, 1)]

        def xstage(t, rows):
            if t < 2:
                return oaF[0:rows, t, :], okeys(t)
            if t == 2:
                return stgB[0:rows, 0, :], [("QbT", j) for j in range(4)]
            return stgB[0:rows, 1, :], [("cqnT", i) for i in range(4)] + [("ckvnT", i) for i in range(4)]
        stmp = [carve(512, F32) for _ in range(2)]
        oT = [carve(512, F32) for _ in range(2)]
        sg = [carve(512) for _ in range(2)]
        ostage = [carve(512, F32) for _ in range(2)]
        cq_bf = [carve(256) for _ in range(2)]
        ckv_f = [carve(256, F32) for _ in range(2)]
        ckv_bf = [carve(256) for _ in range(2)]
        kr_f = [carve(32, F32) for _ in range(2)]
        kr_t = [carve(64, F32) for _ in range(2)]
        krpad = [carve(96) for _ in range(2)]
        qf = [carve(768, F32).rearrange("p (h d) -> p h d", h=8) for _ in range(2)]
        q_bf = [carve(768).rearrange("p (h d) -> p h d", h=8) for _ in range(2)]
        q_t1 = carve(4 * 128, F32).rearrange("p (a h d) -> p a h d", a=4, h=8)
        q_t = [q_t1, q_t1]
        krT_sb = carve(512)
        prompt_end = apos[0]

        if NSEQ:
            MEMSET(VmF[:, :], 1.0, [("Vm", i) for i in range(NT)])
            MEMSET(KT[96:128, :, :].rearrange("p h s -> p (h s)"), 0.0, ["KTpad"])
        MEMSET(QT[96:128, :, :].rearrange("p h s -> p (h s)"), 0.0, ["QTpad"])
        MEMSET(VbF[:, :], 1.0, [("Vb", i) for i in range(8)])
        for i in range(2):
            MEMSET(krpad[i], 0.0, [("krpad", i)])

        cnt = {"tile": 0, "blk": 0, "ft": 0, "xs": 0, "pt": 0}

        def rstd_from_ss(col, n, R, key_in, key_out):
            ACT(stat[0:R, col + 1:col + 2], stat[0:R, col:col + 1], AF.Ln, [key_in, "cst"], [key_out + "_ln"],
                scale=1.0 / n, bias=cst[0:R, 0:1])
            ACT(stat[0:R, col + 1:col + 2], stat[0:R, col + 1:col + 2], AF.Exp, [key_out + "_ln"], [key_out], scale=-0.5)

        def to_featT(src_bf, R, t, gcol, dstT, dkey, nchunk, rkeys, scale_ap):
            b = bank("misc")
            pv = psb(b)
            for c in range(nchunk):
                TR(pv[:, c * R:(c + 1) * R], src_bf[0:R, c * 128:(c + 1) * 128], ident[0:R, 0:R], rkeys + ["ident"], [("ps", b)])
            o = dstT[:, 0:nchunk, t * R:(t + 1) * R]
            i = pv[:, 0:nchunk * R].rearrange("p (c r) -> p c r", c=nchunk)
            if scale_ap is None:
                use_act = (cnt["ft"] % 2 == 1)
                cnt["ft"] += 1
                CP(o, i, [("ps", b)], [(dkey, t)], eng=("act" if use_act else "dve"))
            else:
                g = scale_ap[:, gcol:gcol + nchunk].unsqueeze(2).broadcast_to([128, nchunk, R])
                TT(o, i, g, ALU.mult, [("ps", b), "gains"], [(dkey, t)])

        def norm_to_featT(cx, t, src_f32, junk_ap, junk_keys, n, gains, rkeys, sc, kp="x"):
            R = cx["R"]
            xb, xk = xsb[cnt["xs"] % 2]
            cnt["xs"] += 1
            ACT(junk_ap, src_f32, AF.Square, rkeys, junk_keys + [("ss" + kp, t)], accum_out=stat[0:R, sc:sc + 1])
            rstd_from_ss(sc, n, R, ("ss" + kp, t), "rs%s%d" % (kp, t))
            TS(xb[0:R, :], src_f32, stat[0:R, sc + 1:sc + 2], ALU.mult, rkeys + ["rs%s%d" % (kp, t)], xk)
            to_featT(xb, R, t, 0, actT, "actT", 8, xk, gains)

        def phase_P(cx):
            R, T, C = cx["R"], cx["T"], cx["T"] * cx["R"]
            last = cx["last"]
            xdone = cx.get("xdone", set())
            for t in range(T):
                if t not in xdone:
                    P.dma("pool", x1[0:R, t, :], cx["x_src"](t), writes=[("x1", t)])
            W, wkA = wnext(SL_A)
            WvA = W[:, 0:4096].rearrange("p (k n) -> p k n", k=8)
            tinfo = []

            xn_done = set(xdone)

            def xnorm(t):
                if t not in xn_done:
                    xn_done.add(t)
                    norm_to_featT(cx, t, x1[0:R, t, :], junk[0:R, :], JK, D, gat, [("x1", t)], 2 * t)

            def A_tile(t):
                if t + 1 < T:
                    xnorm(t + 1)
                b = bank("mm")
                for c in range(8):
                    MM(psf(b, R), actT[:, c, t * R:(t + 1) * R], WvA[:, c, :], c == 0, c == 7, [("actT", t), wkA], [("ps", b)])
                i2 = cnt["tile"] % 2
                cnt["tile"] += 1
                tinfo.append(i2)
                if t == 0:
                    for tt in sorted(xdone):
                        sa, sk = xstage(tt, R)
                        CP(x1[0:R, tt, :], sa, sk, [("x1", tt)])
                s0 = 8 + 4 * t
                ACT(cq_bf[i2][0:R, :], psf(b, R)[:, 0:256], AF.Square, [("ps", b)], [("cq_bf", i2), ("ssq", t)],
                    accum_out=stat[0:R, s0:s0 + 1])
                ACT(ckv_bf[i2][0:R, :], psf(b, R)[:, 256:512], AF.Square, [("ps", b)], [("ckv_bf", i2), ("sskv", t)],
                    accum_out=stat[0:R, s0 + 2:s0 + 3])
                rstd_from_ss(s0, 256, R, ("ssq", t), "rsq%d" % t)
                rstd_from_ss(s0 + 2, 256, R, ("sskv", t), "rskv%d" % t)
                ACT(cq_bf[i2][0:R, :], psf(b, R)[:, 0:256], AF.Copy, [("ps", b), "rsq%d" % t], [("cq_bf", i2)],
                    scale=stat[0:R, s0 + 1:s0 + 2])
                ACT(ckv_f[i2][0:R, :], psf(b, R)[:, 256:512], AF.Copy, [("ps", b), "rskv%d" % t], [("ckv_f", i2)],
                    scale=stat[0:R, s0 + 3:s0 + 4])
                TT(ckv_f[i2][0:R, :], ckv_f[i2][0:R, :], gkv_bc[0:R, :], ALU.mult, [("ckv_f", i2), "gkv_bc"], [("ckv_f", i2)], eng="pool")
                P.dma("pool", cx["o_ckv"](t), ckv_f[i2][0:R, :], reads=[("ckv_f", i2)])
                CP(ckv_bf[i2][0:R, :], ckv_f[i2][0:R, :], [("ckv_f", i2)], [("ckv_bf", i2)], eng="pool")

            def TRcq(tt):
                j2 = tinfo[tt]
                to_featT(cq_bf[j2], R, tt, 0, cqnT, "cqnT", 2, [("cq_bf", j2)], gqT)
                to_featT(ckv_bf[j2], R, tt, 0, ckvnT, "ckvnT", 2, [("ckv_bf", j2)], None)

            xnorm(0)
            g1 = list(range(0, min(2, T)))
            g2 = list(range(2, T))
            for t in g1:
                A_tile(t)
            for t in range(T):
                xnorm(t)
            W, wk = wnext(SL_V, hold=1)
            Wv = W[:, 0:4096].rearrange("p (k n) -> p k n", k=8)
            for t in range(T):
                b = bank("mm")
                for c in range(8):
                    MM(psf(b, R), actT[:, c, t * R:(t + 1) * R], Wv[:, c, :], c == 0, c == 7, [("actT", t), wk], [("ps", b)])
                vdst, vkey = cx["vb_dst"](t)
                CP(vdst, psf(b, R).rearrange("p (h d) -> p h d", h=8), [("ps", b)], [vkey])
                if last:
                    i2 = cnt["tile"] % 2
                    cnt["tile"] += 1
                    CP(ostage[i2][0:R, :], psf(b, R), [("ps", b)], [("ostage", i2)], eng="act")
                    P.dma("pool", cx["o_bv"](t), ostage[i2][0:R, :], reads=[("ostage", i2)])
            for t in g1:
                TRcq(t)
            for t in g2:
                A_tile(t)
            W, wk = wnext(SL_KB)
            Wv = W[:, 0:4096].rearrange("p (k n) -> p k n", k=8)
            if last:
                for t in range(T):
                    b = bank("mm")
                    for c in range(8):
                        MM(psf(b, R), actT[:, c, t * R:(t + 1) * R], Wv[:, c, :], c == 0, c == 7, [("actT", t), wk], [("ps", b)])
                    i2 = cnt["tile"] % 2
                    cnt["tile"] += 1
                    CP(ostage[i2][0:R, :], psf(b, R), [("ps", b)], [("ostage", i2)], eng="act")
                    P.dma("pool", cx["o_bk"](t), ostage[i2][0:R, :], reads=[("ostage", i2)])
            for j in range(4):
                b = bank("mm")
                for c in range(8):
                    MM(psf(b)[:, 0:C], Wv[:, c, j * 128:(j + 1) * 128], actT[:, c, 0:C], c == 0, c == 7,
                       [("actT", t) for t in range(T)] + [wk], [("ps", b)])
                kdst, kkey = cx["kbT_dst"](j)
                if j % 2 == 0:
                    CP(kdst, psf(b)[:, 0:C], [("ps", b)], [kkey])
                else:
                    CP(kdst, psf(b)[:, 0:C], [("ps", b)], [kkey], eng="act")
            for t in g2:
                TRcq(t)
            W, wk = wnext(SL_QB)
            Wv = W[:, 0:4096].rearrange("p (k n) -> p k n", k=8)
            for j in range(4):
                b = bank("mm")
                for c in range(8):
                    MM(psf(b)[:, 0:C], Wv[:, c, j * 128:(j + 1) * 128], actT[:, c, 0:C], c == 0, c == 7,
                       [("actT", t) for t in range(T)] + [wk], [("ps", b)])
                if j % 2 == 0:
                    TS(QbT[:, j, 0:C], psf(b)[:, 0:C], 0.125, ALU.mult, [("ps", b)], [("QbT", j)])
                else:
                    ACT(QbT[:, j, 0:C], psf(b)[:, 0:C], AF.Copy, [("ps", b)], [("QbT", j)], scale=0.125)
            W, wk = wnext(SL_S)
            Wkr = W[:, S_KR:S_KR + 256].rearrange("p (k n) -> p k n", k=8)
            Wuq = W[:, S_UQ:S_UQ + 1536].rearrange("p (k n) -> p k n", k=2)
            Wuk = W[:, S_UK:S_UK + 1024].rearrange("p (k n) -> p k n", k=2)
            Wuv = W[:, S_UV:S_UV + 1024].rearrange("p (k n) -> p k n", k=2)
            sinfo = []
            for t in range(T):
                b = bank("mm")
                for c in range(2):
                    MM(psf(b, R), ckvnT[:, c, t * R:(t + 1) * R], Wuv[:, c, :], c == 0, c == 1, [("ckvnT", t), wk], [("ps", b)])
                vdst, vkey = cx["vm_dst"](t)
                CP(vdst, psf(b, R).rearrange("p (h d) -> p h d", h=8), [("ps", b)], [vkey], eng="act")
            for h in range(8):
                b = bank("mm")
                for c in range(2):
                    MM(psf(b, 64)[:, 0:C], Wuk[:, c, h * 64:(h + 1) * 64], ckvnT[:, c, 0:C], c == 0, c == 1,
                       [("ckvnT", t) for t in range(T)] + [wk], [("ps", b)])
                kd, kk = cx["ktn_dst"](h)
                if h % 2 == 0:
                    CP(kd, psf(b, 64)[:, 0:C], [("ps", b)], [kk])
                else:
                    CP(kd, psf(b, 64)[:, 0:C], [("ps", b)], [kk], eng="act")
            for t in range(T):
                ti = cx["pos_tile"](t)
                cosv = cs[0:R, ti, 0, :]
                sinv = cs[0:R, ti, 1, :]
                i2 = cnt["tile"] % 2
                cnt["tile"] += 1
                sinfo.append(i2)
                b = bank("mm")
                for c in range(8):
                    MM(psf(b, R)[:, 0:32], actT[:, c, t * R:(t + 1) * R], Wkr[:, c, :], c == 0, c == 7, [("actT", t), wk], [("ps", b)])
                CP(kr_t[i2][0:R, 0:32], psf(b, R)[:, 0:32], [("ps", b)], [("kr_t", i2)], eng="act")
                TT(kr_t[i2][0:R, 32:48], kr_t[i2][0:R, 0:16], cosv, ALU.mult, [("kr_t", i2), "cs"], [("kr_u", i2)])
                TT(kr_t[i2][0:R, 48:64], kr_t[i2][0:R, 16:32], sinv, ALU.mult, [("kr_t", i2), "cs"], [("kr_v", i2)])
                TT(kr_f[i2][0:R, 0:16], kr_t[i2][0:R, 32:48], kr_t[i2][0:R, 48:64], ALU.subtract,
                   [("kr_u", i2), ("kr_v", i2)], [("kr_f", i2)])
                TT(kr_t[i2][0:R, 32:48], kr_t[i2][0:R, 0:16], sinv, ALU.mult, [("kr_t", i2), "cs"], [("kr_u", i2)])
                TT(kr_t[i2][0:R, 48:64], kr_t[i2][0:R, 16:32], cosv, ALU.mult, [("kr_t", i2), "cs"], [("kr_v", i2)])
                TT(kr_f[i2][0:R, 16:32], kr_t[i2][0:R, 32:48], kr_t[i2][0:R, 48:64], ALU.add,
                   [("kr_u", i2), ("kr_v", i2)], [("kr_f", i2)])
                P.dma("pool", cx["o_kr"](t), kr_f[i2][0:R, :], reads=[("kr_f", i2)])
                CP(krpad[i2][0:R, 64:96], kr_f[i2][0:R, :], [("kr_f", i2)], [("krpad", i2)], eng="pool")
                qps = PS[0:R, 0:2, :].rearrange("p a n -> p (a n)")
                for c in range(2):
                    MM(qps[:, 0:512], cqnT[:, c, t * R:(t + 1) * R], Wuq[:, c, 0:512], c == 0, c == 1, [("cqnT", t), wk], [("ps", 0)])
                for c in range(2):
                    MM(qps[:, 512:768], cqnT[:, c, t * R:(t + 1) * R], Wuq[:, c, 512:768], c == 0, c == 1, [("cqnT", t), wk], [("ps", 1)])
                ACT(qf[i2][0:R].rearrange("p h d -> p (h d)"), qps[:, 0:768], AF.Copy, [("ps", 0), ("ps", 1)], [("qf", i2)],
                    scale=MLA_SCALE)
                CP(q_bf[i2][0:R, :, 0:64], qf[i2][0:R, :, 0:64], [("qf", i2)], [("q_bf", i2)], eng="pool")
                qa = qf[i2][0:R, :, 64:80]
                qb = qf[i2][0:R, :, 80:96]
                cb = cosv.unsqueeze(1).broadcast_to([R, 8, 16])
                sb_ = sinv.unsqueeze(1).broadcast_to([R, 8, 16])
                qt = q_t[i2]
                TT(qt[0:R, 0], qa, cb, ALU.mult, [("qf", i2), "cs"], ["q_t0"])
                TT(qt[0:R, 1], qb, sb_, ALU.mult, [("qf", i2), "cs"], ["q_t1"])
                TT(qt[0:R, 2], qa, sb_, ALU.mult, [("qf", i2), "cs"], ["q_t2"])
                TT(qt[0:R, 3], qb, cb, ALU.mult, [("qf", i2), "cs"], ["q_t3"])
                TT(q_bf[i2][0:R, :, 64:80], qt[0:R, 0], qt[0:R, 1], ALU.subtract, ["q_t0", "q_t1"], [("q_bf", i2)])
                TT(q_bf[i2][0:R, :, 80:96], qt[0:R, 2], qt[0:R, 3], ALU.add, ["q_t2", "q_t3"], [("q_bf", i2)])
                if t % 2 == 1 or t == T - 1:
                    for tt in range(t - (1 if t % 2 == 1 else 0), t + 1):
                        j2 = sinfo[tt]
                        bk_ = bank("misc")
                        TR(psb(bk_)[0:96, 0:R], krpad[j2][0:R, 0:96], ident[0:R, 0:R], [("krpad", j2), "ident"], [("ps", bk_)])
                        CP(krT_sb[64:96, tt * R:(tt + 1) * R], psb(bk_)[64:96, 0:R], [("ps", bk_)], ["krT_sb"])
                        bq = bank("misc")
                        for h in range(8):
                            TR(psb(bq)[0:96, h * R:(h + 1) * R], q_bf[j2][0:R, h, :], ident[0:R, 0:R], [("q_bf", j2), "ident"], [("ps", bq)])
                        CP(QT[0:96, :, tt * R:(tt + 1) * R], psb(bq)[0:96, 0:8 * R].rearrange("p (h r) -> p h r", h=8), [("ps", bq)],
                           [("QT", tt)])
            for h in range(8):
                kd, kk = cx["ktr_dst"](h)
                CP(kd, krT_sb[64:96, 0:C], ["krT_sb"], [kk], eng="pool")

        def finalize_head(cx, accb, h, mixer, tiles=None):
            R, T, C = cx["R"], cx["T"], cx["T"] * cx["R"]
            i2 = cnt["blk"] % 2
            cnt["blk"] += 1
            tl = list(range(T)) if tiles is None else tiles
            c0_, c1_ = tl[0] * R, (tl[-1] + 1) * R
            CP(oT[i2][0:65, c0_:c1_], psf(accb, 65)[:, c0_:c1_], [("ps", accb)], [("oT", i2)], eng=("dve" if mixer == 0 else "act"))
            b = bank("misc")
            for t in tl:
                TR(psf(b, R)[:, t * 65:(t + 1) * 65], oT[i2][0:65, t * R:(t + 1) * R], identf[0:65, 0:65], [("oT", i2), "identf"], [("ps", b)])
            pv = psf(b, R)[:, 0:T * 65].rearrange("p (t d) -> p t d", t=T)
            sc = 40 + 4 * i2
            t0_, t1_ = tl[0], tl[-1] + 1
            P.op("dve", lambda e: e.reciprocal(out=stat[0:R, sc + t0_:sc + t1_], in_=pv[:, t0_:t1_, 64]), [("ps", b)], [("rden", i2)])
            for t in tl:
                TS(oa[0:R, t, mixer * 512 + h * 64: mixer * 512 + (h + 1) * 64], pv[:, t, 0:64], stat[0:R, sc + t:sc + t + 1], ALU.mult,
                   [("ps", b), ("rden", i2)], [("oa", t, mixer)])

        def run_blocks(cx, blocks, LAG=2):
            n = len(blocks)
            for i in range(n + LAG):
                if i < n:
                    blocks[i]["score"]()
                j = i - LAG
                if 0 <= j < n:
                    blocks[j]["pv"]()
                    if blocks[j].get("fin"):
                        blocks[j]["fin"]()

        def phase_M(cx, st_i):
            blocks = []
            for h in range(8):
                accb = bank("acc")
                nkt = 4 * st_i + 4
                for kt in range(nkt):
                    d = kt - 4 * st_i
                    q0 = 0 if d < 0 else 128 * d
                    blk = {}
                    st_b = {}

                    def score(h=h, kt=kt, q0=q0, st_b=st_b):
                        b = bank("sc")
                        pi = cnt["pt"] % NPT
                        cnt["pt"] += 1
                        st_b["pi"] = pi
                        MM(psf(b)[:, q0:512], KT[:, h, kt * 128:(kt + 1) * 128], QT[:, h, q0:512], True, True,
                           [("KTn", h, kt // 4), ("KTr", h, kt // 4), "KTpad", "QTpad"] + [("QT", t) for t in range(4)], [("ps", b)])
                        ACT(PT[pi][:, q0:512], psf(b)[:, q0:512], AF.Exp, [("ps", b)], [("PT", pi)])
                        if kt >= 4 * st_i:
                            MEMSET(PT[pi][64:128, q0:q0 + 64], 0.0, [("PT", pi)])

                    def pv(h=h, kt=kt, d=d, q0=q0, accb=accb, st_b=st_b):
                        pi = st_b["pi"]
                        w0 = (kt * 8 + h) * 65
                        MM(psf(accb)[:, q0:512], VmF[:, w0:w0 + 128], PT[pi][:, q0:512], kt == 0, kt == 4 * st_i + 3,
                           [("Vm", kt), ("Vm", min(kt + 1, NT - 1)), ("PT", pi)], [("ps", accb)], sg=True)

                    blk["score"] = score
                    blk["pv"] = pv
                    if kt == nkt - 1:
                        blk["fin"] = (lambda h=h, accb=accb: finalize_head(cx, accb, h, 0))
                    blocks.append(blk)
            run_blocks(cx, blocks)

        def phase_B(cx, st_i):
            blocks = []
            for h in range(8):
                accb = bank("acc")
                hp, hj = (h % 2) * 64, h // 2
                tiles = [t for t in range(8) if 4 * st_i - 4 + t >= 0]
                for n_, t in enumerate(tiles):
                    m = 4 * st_i - 4 + t
                    slot = (m // 4) % 2
                    kcol = slot * 512 + (m % 4) * 128
                    vt = slot * 4 + (m % 4)
                    (c0, c1), ex, exh = BAND_T[t]
                    u0, u1 = min(c0, ex), max(c1, ex)
                    qa, qb_ = 64 * u0, 64 * (u1 + 1)
                    r0 = 64 * (u0 + 8 - 2 * t)
                    blk = {}
                    st_b = {}

                    def score(h=h, hp=hp, hj=hj, kcol=kcol, qa=qa, qb_=qb_, r0=r0, slot=slot, st_b=st_b, ex=ex, exh=exh):
                        b = bank("sc")
                        pi = cnt["pt"] % NPT
                        cnt["pt"] += 1
                        st_b["pi"] = pi
                        n = qb_ - qa
                        MM(psf(b)[:, qa:qb_], KbT[hp:hp + 64, hj, kcol:kcol + 128], QbT[hp:hp + 64, hj, qa:qb_], True, True,
                           [("KbT", hj, slot), ("QbT", hj)], [("ps", b)])
                        nb = max(0, min(384, r0 + n) - r0)
                        if nb < n:
                            ACT(PT[pi][:, qa + nb:qb_], psf(b)[:, qa + nb:qb_], AF.Exp, [("ps", b), "cbias"], [("PT", pi)],
                                bias=cbias[:, h:h + 1])
                        if nb > 0:
                            si = cnt["blk"] % 2
                            TT(stmp[si][:, 0:nb], psf(b)[:, qa:qa + nb], toep[:, h, r0:r0 + nb], ALU.add, [("ps", b), "toep"], [("stmp", si)])
                            ACT(PT[pi][:, qa:qa + nb], stmp[si][:, 0:nb], AF.Exp, [("stmp", si)], [("PT", pi)])
                        zp = 64 * (1 - exh)
                        MEMSET(PT[pi][zp:zp + 64, 64 * ex:64 * (ex + 1)], 0.0, [("PT", pi)])

                    def pv(h=h, vt=vt, c0=c0, c1=c1, ex=ex, exh=exh, accb=accb, first=(n_ == 0), st_b=st_b, slot=slot,
                           lastb=(n_ == len(tiles) - 1)):
                        pi = st_b["pi"]
                        u0, u1 = min(c0, ex), max(c1, ex)
                        w0 = (vt * 8 + h) * 65
                        MM(psf(accb)[:, 64 * u0:64 * (u1 + 1)], VbF[:, w0:w0 + 128], PT[pi][:, 64 * u0:64 * (u1 + 1)], first, lastb,
                           [("Vb", vt), ("Vb", min(vt + 1, 7)), ("PT", pi)], [("ps", accb)], sg=True)

                    blk["score"] = score
                    blk["pv"] = pv
                    if n_ == len(tiles) - 1:
                        blk["fin"] = (lambda h=h, accb=accb: finalize_head(cx, accb, h, 1))
                    blocks.append(blk)
            run_blocks(cx, blocks, LAG=3)

        def phase_O(cx):
            R, T = cx["R"], cx["T"]
            for t in range(T):
                s0 = 24 + 4 * t
                xb, xk = xsb[cnt["xs"] % 2]
                cnt["xs"] += 1
                ACT(sg[0][0:R, 0:512], oa[0:R, t, 0:512], AF.Square, [("oa", t, 0)], [("sg", 0), ("ssa", t)], accum_out=stat[0:R, s0:s0 + 1])
                ACT(sg[1][0:R, 0:512], oa[0:R, t, 512:1024], AF.Square, [("oa", t, 1)], [("sg", 1), ("ssb", t)],
                    accum_out=stat[0:R, s0 + 2:s0 + 3])
                rstd_from_ss(s0, 512, R, ("ssa", t), "rsa%d" % t)
                rstd_from_ss(s0 + 2, 512, R, ("ssb", t), "rsb%d" % t)
                TS(xb[0:R, 0:512], oa[0:R, t, 0:512], stat[0:R, s0 + 1:s0 + 2], ALU.mult, [("oa", t, 0), "rsa%d" % t], xk)
                TS(xb[0:R, 512:1024], oa[0:R, t, 512:1024], stat[0:R, s0 + 3:s0 + 4], ALU.mult, [("oa", t, 1), "rsb%d" % t], xk)
                to_featT(xb, R, t, 0, actT, "actT", 8, xk, gmix)
            W0, wk0 = wnext(SL_O0)
            W1, wk1 = wnext(SL_O1, hold=1)
            Wvs = [(W0[:, 0:4096].rearrange("p (k n) -> p k n", k=8), wk0), (W1[:, 0:4096].rearrange("p (k n) -> p k n", k=8), wk1)]

            def wout(t):
                for half in range(2):
                    Wv, wk = Wvs[half]
                    b = bank("mm")
                    for c in range(8):
                        MM(psf(b, R), actT[:, c, t * R:(t + 1) * R], Wv[:, c, :], c == 0, c == 7, [("actT", t), wk], [("ps", b)])
                    xv = x1[0:R, t, half * 512:(half + 1) * 512]
                    TT(xv, psf(b, R), xv, ALU.add, [("ps", b), ("x1", t)], [("x1", t)])

            wout(0)
            for t in range(T):
                if t + 1 < T:
                    wout(t + 1)
                norm_to_featT(cx, t, x1[0:R, t, :], junk[0:R, :], JK, D, gffn, [("x1", t)], 2 * t)

        def phase_F(cx, nxt=None):
            R, T, C = cx["R"], cx["T"], cx["T"] * cx["R"]
            akeys = [("actT", t) for t in range(T)]
            if nxt is not None:
                for t in range(nxt["T"]):
                    sa, sk = xstage(t, nxt["R"])
                    P.dma("pool", sa, nxt["x_src"](t), writes=sk)
            for g in range(2):
                for gi in range(6):
                    W, wk = wnext(SL_G0 + 6 * g + gi)
                    Wv = W[:, 0:4096].rearrange("p (k n) -> p k n", k=8)
                    for ci, fc in enumerate(G_CHUNKS[6 * g + gi]):
                        lc = fc - 11 * g
                        bg = bank("mm")
                        bu = bank("mm")
                        for c in range(8):
                            MM(psf(bg)[:, 0:C], Wv[:, c, ci * 128:ci * 128 + 128], actT[:, c, 0:C], c == 0, c == 7, akeys + [wk], [("ps", bg)])
                        for c in range(8):
                            MM(psf(bu)[:, 0:C], Wv[:, c, 256 + ci * 128:256 + ci * 128 + 128], actT[:, c, 0:C], c == 0, c == 7, akeys + [wk], [("ps", bu)])
                        i2 = cnt["blk"] % 2
                        cnt["blk"] += 1
                        ACT(sg[i2][:, 0:C], psf(bg)[:, 0:C], AF.Silu, [("ps", bg)], [("sg", i2)])
                        TT(aT[:, lc, 0:C], psf(bu)[:, 0:C], sg[i2][:, 0:C], ALU.mult, [("ps", bu), ("sg", i2)], [("aT", lc)])
                if g == 1 and nxt is not None:
                    nxt["xdone"] = set()
                    for t in range(nxt["T"]):
                        sa, sk = xstage(t, nxt["R"])
                        norm_to_featT(nxt, t, sa, junk[0:nxt["R"], :], JK, D, gat, sk, 48 + 2 * t, kp="p")
                        nxt["xdone"].add(t)
                nacc = 2 * T
                for di in range(3):
                    W, wk = wnext(SL_D0 + 3 * g + di)
                    chunks = D_CHUNKS[3 * g + di]
                    Wv = W[:, 0:len(chunks) * 1024].rearrange("p (c n) -> p c n", c=len(chunks))
                    for t in range(T):
                        for half in range(2):
                            b = t * 2 + half
                            for ci, fc in enumerate(chunks):
                                lc = fc - 11 * g
                                MM(psf(b, R), aT[:, lc, t * R:(t + 1) * R], Wv[:, ci, half * 512:(half + 1) * 512],
                                   (di == 0 and ci == 0), (di == 2 and ci == len(chunks) - 1), [("aT", lc), wk], [("ps", b)])
                for t in range(T):
                    for half in range(2):
                        b = t * 2 + half
                        xv = x1[0:R, t, half * 512:(half + 1) * 512]
                        TT(xv, psf(b, R), xv, ALU.add, [("ps", b), ("x1", t)], [("x1", t)])

        def phase_Y(cx):
            R, T = cx["R"], cx["T"]
            for t in range(T):
                sc = 2 * t
                ACT(junk[0:R, :], x1[0:R, t, :], AF.Square, [("x1", t)], JK + [("ssx", t)], accum_out=stat[0:R, sc:sc + 1])
                rstd_from_ss(sc, D, R, ("ssx", t), "rsx%d" % t)
                P.op("dve", lambda e, t=t, sc=sc: e.scalar_tensor_tensor(
                    out=x1[0:R, t, :], in0=x1[0:R, t, :], scalar=stat[0:R, sc + 1:sc + 2], in1=gfin_bc[0:R, :],
                    op0=ALU.mult, op1=ALU.mult), [("x1", t), "rsx%d" % t, "gfin_bc"], [("x1", t)])
                P.dma("sp", cx["y_dst"](t), x1[0:R, t, :], reads=[("x1", t)])

        passes = []
        for sq in range(NSEQ):
            for st_i in range(NST):
                r0 = st_i * 512
                slot = st_i % 2
                cx = dict(
                    st_i=st_i,
                    R=128, T=4, last=(st_i == NST - 1),
                    x_src=lambda t, sq=sq, r0=r0: x_p.ap()[sq, r0 + t * 128:r0 + (t + 1) * 128, :],
                    o_ckv=lambda t, sq=sq, r0=r0: o_ckv_p.ap()[sq, r0 + t * 128:r0 + (t + 1) * 128, :],
                    o_kr=lambda t, sq=sq, r0=r0: o_kr_p.ap()[sq, r0 + t * 128:r0 + (t + 1) * 128, :],
                    o_bk=lambda t, sq=sq: o_bk_p.ap()[sq, t * 128:(t + 1) * 128, :],
                    o_bv=lambda t, sq=sq: o_bv_p.ap()[sq, t * 128:(t + 1) * 128, :],
                    y_dst=lambda t, sq=sq, r0=r0: y_p.ap()[sq, r0 + t * 128:r0 + (t + 1) * 128, :],
                    pos_tile=lambda t, st_i=st_i: st_i * 4 + t,
                    vb_dst=lambda t, slot=slot: (Vb[:, slot * 4 + t, :, 0:64], ("Vb", slot * 4 + t)),
                    vm_dst=lambda t, st_i=st_i: (Vm[:, st_i * 4 + t, :, 0:64], ("Vm", st_i * 4 + t)),
                    kbT_dst=lambda j, slot=slot: (KbT[:, j, slot * 512:(slot + 1) * 512], ("KbT", j, slot)),
                    ktr_dst=lambda h, r0=r0, st_i=st_i: (KT[64:96, h, r0:r0 + 512], ("KTr", h, st_i)),
                    ktn_dst=lambda h, r0=r0, st_i=st_i: (KT[0:64, h, r0:r0 + 512], ("KTn", h, st_i)),
                )
                passes.append(cx)
        for pi_, cx in enumerate(passes):
            phase_P(cx)
            if not toep_state["done"]:
                load_toeplitz()
                toep_state["done"] = True
            phase_M(cx, cx["st_i"])
            phase_B(cx, cx["st_i"])
            phase_O(cx)
            phase_F(cx, passes[pi_ + 1] if pi_ + 1 < len(passes) else None)
            phase_Y(cx)

        if NSS:
            P.barrier()
            apos[0] = 0
            R = 16
            ckvT = carve(2 * PAST).rearrange("p (c s) -> p c s", c=2)
            Vs_h = carve(NKT * 65).rearrange("p (t d) -> p t d", t=NKT)
            KTs = carve(PAST)[0:96, :]
            stg_f = carve(2048, F32)
            stg_b = carve(2048)
            krp_s = carve(NKT * 96).rearrange("p (t d) -> p t d", t=NKT)
            wukv = carve(2048)
            KTnew = carve(8 * 32)[0:96, :].rearrange("p (h s) -> p h s", h=8)
            Vnew = carve(NSS * 520).rearrange("p (t h d) -> p t h d", t=NSS, h=8)
            KbTnew = carve(4 * 32).rearrange("p (j s) -> p j s", j=4)
            Vbnew = carve(NSS * 520).rearrange("p (t h d) -> p t h d", t=NSS, h=8)
            KbTs = carve(4 * 512).rearrange("p (j s) -> p j s", j=4)
            Vbs = carve(4 * 520).rearrange("p (t h d) -> p t h d", t=4, h=8)
            assert apos[0] <= persist_end, (apos[0], persist_end)
            Wuk_s = wukv[:, 0:1024].rearrange("p (k n) -> p k n", k=2)
            Wuv_s = wukv[:, 1024:2048].rearrange("p (k n) -> p k n", k=2)
            if not toep_state["done"]:
                load_toeplitz()
                toep_state["done"] = True
            cast_slot(SL_S)
            P.dma("sp", wukv[:, 0:2048], wsc.ap()[SL_S, :, S_UK:S_UK + 2048], reads=[("wsc", SL_S)], writes=["wukv"])
            MEMSET(Vs_h[:].rearrange("p t d -> p (t d)"), 1.0, ["Vs_h"])
            MEMSET(Vnew[:].rearrange("p t h d -> p (t h d)"), 1.0, [("Vnew", t) for t in range(NSS)])
            MEMSET(Vbnew[:].rearrange("p t h d -> p (t h d)"), 1.0, [("Vbnew", t) for t in range(NSS)])
            MEMSET(Vbs[:].rearrange("p t h d -> p (t h d)"), 1.0, ["Vbs"])
            MEMSET(krp_s[:].rearrange("p t d -> p (t d)"), 0.0, ["krp_s"])
            cx = dict(
                R=16, T=NSS, last=True,
                x_src=lambda t: x_s.ap()[t * 16:(t + 1) * 16, :],
                o_ckv=lambda t: o_ckv_s.ap()[t * 16:(t + 1) * 16, :],
                o_kr=lambda t: o_kr_s.ap()[t * 16:(t + 1) * 16, :],
                o_bk=lambda t: o_bk_s.ap()[t * 16:(t + 1) * 16, :],
                o_bv=lambda t: o_bv_s.ap()[t * 16:(t + 1) * 16, :],
                y_dst=lambda t: y_s.ap()[t * 16:(t + 1) * 16, :],
                pos_tile=lambda t: NTT - 1,
                vb_dst=lambda t: (Vbnew[0:16, t, :, 0:64], ("Vbnew", t)),
                vm_dst=lambda t: (Vnew[0:16, t, :, 0:64], ("Vnew", t)),
                kbT_dst=lambda j: (KbTnew[:, j, 0:16 * NSS], ("KbTnew", j)),
                ktr_dst=lambda h: (KTnew[64:96, h, 0:16 * NSS], ("KTnew_r", h)),
                ktn_dst=lambda h: (KTnew[0:64, h, 0:16 * NSS], ("KTnew_n", h)),
            )
            phase_P(cx)
            NQ = 16
            KCH = min(8, NKT)
            for bsq in range(NSS):
                qc = slice(bsq * 16, (bsq + 1) * 16)
                for k0 in range(0, NKT, KCH):
                    sf = stg_f[:, 0:KCH * 256].rearrange("p (t c) -> p t c", t=KCH)
                    sbv = stg_b[:, 0:KCH * 256].rearrange("p (t c) -> p t c", t=KCH)
                    P.dma("sp", sf, c_ckv.ap()[bsq, k0 * 128:(k0 + KCH) * 128, :].rearrange("(t p) c -> p t c", p=128),
                          writes=["stg_f"])
                    CP(sbv, sf, ["stg_f"], ["stg_b"])
                    for k4 in range(0, KCH, 4):
                        b = bank("misc")
                        pv = psb(b)
                        for kk in range(4):
                            for c in range(2):
                                TR(pv[:, (kk * 2 + c) * 128:(kk * 2 + c + 1) * 128], sbv[:, k4 + kk, c * 128:(c + 1) * 128], ident[:],
                                   ["stg_b", "ident"], [("ps", b)])
                        pv4 = pv.rearrange("p (k c n) -> p k c n", k=4, c=2)
                        for c in range(2):
                            dst = ckvT[:, c, (k0 + k4) * 128:(k0 + k4 + 4) * 128].rearrange("p (k n) -> p k n", k=4)
                            CP(dst, pv4[:, :, c, :], [("ps", b)], [("ckvT", (k0 + k4) // 4)], eng=("act" if c else "dve"))
                krf = stg_f[:, 0:NKT * 32].rearrange("p (t c) -> p t c", t=NKT)
                P.dma("sp", krf, c_kr.ap()[bsq, :, :].rearrange("(t p) c -> p t c", p=128), writes=["stg_f"])
                CP(krp_s[:, :, 64:96], krf, ["stg_f"], ["krp_s"])
                for k8 in range(0, NKT, 8):
                    nk = min(8, NKT - k8)
                    b = bank("misc")
                    for kk in range(nk):
                        TR(psb(b)[0:96, kk * 128:(kk + 1) * 128], krp_s[:, k8 + kk, :], ident[:], ["krp_s", "ident"], [("ps", b)])
                    CP(KTs[64:96, k8 * 128:(k8 + nk) * 128], psb(b)[64:96, 0:nk * 128], [("ps", b)], ["KTs_r"])
                sf = stg_f[:, 0:2048].rearrange("p (t c) -> p t c", t=4)
                sbv = stg_b[:, 0:2048].rearrange("p (t c) -> p t c", t=4)
                P.dma("sp", sf, c_bk.ap()[bsq, :, :].rearrange("(t p) c -> p t c", p=128), writes=["stg_f"])
                CP(sbv, sf, ["stg_f"], ["stg_b"])
                for j2 in range(0, 4, 2):
                    b = bank("misc")
                    for jj in range(2):
                        for m in range(4):
                            TR(psb(b)[:, (jj * 4 + m) * 128:(jj * 4 + m + 1) * 128], sbv[:, m, (j2 + jj) * 128:(j2 + jj + 1) * 128], ident[:],
                               ["stg_b", "ident"], [("ps", b)])
                    CP(KbTs[:, j2:j2 + 2, :], psb(b).rearrange("p (j s) -> p j s", j=2), [("ps", b)], ["KbTs"])
                P.dma("sp", sf, c_bv.ap()[bsq, :, :].rearrange("(t p) c -> p t c", p=128), writes=["stg_f"])
                for m in range(4):
                    CP(Vbs[:, m, :, 0:64], sf[:, m, :].rearrange("p (h d) -> p h d", h=8), ["stg_f"], ["Vbs"], eng=("pool" if m % 2 else "dve"))
                for h in range(8):
                    hp, hj = (h % 2) * 64, h // 2
                    sb_ = bank("sc")
                    for m in range(4):
                        MM(psf(sb_)[:, m * 16:(m + 1) * 16], KbTs[hp:hp + 64, hj, m * 128:(m + 1) * 128], QbT[hp:hp + 64, hj, qc], True, True,
                           ["KbTs", ("QbT", hj)], [("ps", sb_)])
                    MM(psf(sb_, 16)[:, 64:80], KbTnew[hp:hp + 64, hj, qc], QbT[hp:hp + 64, hj, qc], True, True,
                       [("KbTnew", hj), ("QbT", hj)], [("ps", sb_)])
                    pi = cnt["blk"] % 4
                    cnt["blk"] += 1
                    si = cnt["blk"] % 2
                    ACT(PT[pi][:, 0:32], psf(sb_)[:, 0:32], AF.Exp, [("ps", sb_), "cbias"], [("PT", pi)], bias=cbias[:, h:h + 1])
                    TT(stmp[si][:, 0:16], psf(sb_)[:, 32:48], toep[:, h, 256:272], ALU.add, [("ps", sb_), "toep"], [("stmp", si)])
                    TT(stmp[si][:, 16:32], psf(sb_)[:, 48:64], toep[:, h, 128:144], ALU.add, [("ps", sb_), "toep"], [("stmp", si)])
                    TT(stmp[si][0:16, 32:48], psf(sb_, 16)[:, 64:80], toep[0:16, h, 0:16], ALU.add, [("ps", sb_), "toep"], [("stmp", si)])
                    ACT(PT[pi][:, 32:64], stmp[si][:, 0:32], AF.Exp, [("stmp", si)], [("PT", pi)])
                    ACT(PT[pi][0:16, 64:80], stmp[si][0:16, 32:48], AF.Exp, [("stmp", si)], [("PT", pi)])
                    accb = bank("acc")
                    for m in range(4):
                        MM(psf(accb, 65)[:, qc], Vbs[:, m, h, :], PT[pi][:, m * 16:(m + 1) * 16], m == 0, False,
                           ["Vbs", ("PT", pi)], [("ps", accb)], sg=True)
                    MM(psf(accb, 65)[:, qc], Vbnew[0:16, bsq, h, :], PT[pi][0:16, 64:80], False, True,
                       [("Vbnew", bsq), ("PT", pi)], [("ps", accb)], sg=True)
                    finalize_head(cx, accb, h, 1, tiles=[bsq])
                NSC = (NKT * 16 + 511) // 512
                for h in range(8):
                    for k0 in range(0, PAST, 512):
                        b = bank("mm")
                        for c in range(2):
                            MM(psf(b, 64), Wuk_s[:, c, h * 64:(h + 1) * 64], ckvT[:, c, k0:k0 + 512], c == 0, c == 1,
                               ["wukv", ("ckvT", k0 // 512)], [("ps", b)])
                        CP(KTs[0:64, k0:k0 + 512], psf(b, 64), [("ps", b)], ["KTs_n"], eng=("act" if (k0 // 512) % 2 else "dve"))
                    for k8 in range(0, NKT, 8):
                        nk = min(8, NKT - k8)
                        b = bank("mm")
                        for kk in range(nk):
                            for c in range(2):
                                MM(psf(b)[:, kk * 64:(kk + 1) * 64], ckvT[:, c, (k8 + kk) * 128:(k8 + kk + 1) * 128], Wuv_s[:, c, h * 64:(h + 1) * 64],
                                   c == 0, c == 1, ["wukv", ("ckvT", (k8 + kk) // 4)], [("ps", b)])
                        CP(Vs_h[:, k8:k8 + nk, 0:64], psf(b)[:, 0:nk * 64].rearrange("p (t d) -> p t d", t=nk), [("ps", b)], ["Vs_h"],
                           eng=("act" if (k8 // 8) % 2 else "dve"))
                    pts = []
                    for s_ in range(NSC):
                        sb_ = bank("sc")
                        kts = list(range(s_ * 32, min(NKT, (s_ + 1) * 32)))
                        for i_, kt in enumerate(kts):
                            MM(psf(sb_)[:, i_ * 16:(i_ + 1) * 16], KTs[:, kt * 128:(kt + 1) * 128], QT[0:96, h, qc], True, True,
                               ["KTs_n", "KTs_r"] + [("QT", t) for t in range(NSS)], [("ps", sb_)])
                        pi = cnt["blk"] % 4
                        cnt["blk"] += 1
                        ACT(PT[pi][:, 0:len(kts) * 16], psf(sb_)[:, 0:len(kts) * 16], AF.Exp, [("ps", sb_)], [("PT", pi)])
                        pts.append((pi, kts))
                    sb_ = bank("sc")
                    MM(psf(sb_, 16)[:, 0:16], KTnew[:, h, qc], QT[0:96, h, qc], True, True,
                       [("KTnew_n", h), ("KTnew_r", h)] + [("QT", t) for t in range(NSS)], [("ps", sb_)])
                    si = cnt["blk"] % 2
                    cnt["blk"] += 1
                    ACT(sg[si][0:16, 0:16], psf(sb_, 16)[:, 0:16], AF.Exp, [("ps", sb_)], [("sg", si)])
                    accb = bank("acc")
                    first = True
                    for pi, kts in pts:
                        for i_, kt in enumerate(kts):
                            MM(psf(accb, 65)[:, qc], Vs_h[:, kt, :], PT[pi][:, i_ * 16:(i_ + 1) * 16], first, False,
                               ["Vs_h", ("PT", pi)], [("ps", accb)], sg=True)
                            first = False
                    MM(psf(accb, 65)[:, qc], Vnew[0:16, bsq, h, :], sg[si][0:16, 0:16], False, True,
                       [("Vnew", bsq), ("sg", si)], [("ps", accb)], sg=True)
                    finalize_head(cx, accb, h, 0, tiles=[bsq])
            phase_O(cx)
            phase_F(cx)
            phase_Y(cx)

        P.finish()
    return nc


def core_inputs(inp, core, nseq, nss):
    f = lambda a: np.ascontiguousarray(a, dtype=np.float32)
    ps, ss = slice(core * nseq, (core + 1) * nseq), slice(core * nss, (core + 1) * nss)
    return dict(
        x_prompt=f(inp["x_prompt"][ps]),
        x_sample=f(inp["x_sample"][ss].reshape(nss * 16, 1024)),
        cache_mla_ckv=f(inp["cache_mla_ckv"][0, ss]),
        cache_mla_krope=f(inp["cache_mla_krope"][0, ss]),
        cache_band_k=f(inp["cache_band_k"][0, ss].reshape(nss, 512, 512)),
        cache_band_v=f(inp["cache_band_v"][0, ss].reshape(nss, 512, 512)),
        w_in=f(inp["w_in"][0]), g_attn=f(inp["g_attn"][0]), g_q=f(inp["g_q"][0]), w_uq=f(inp["w_uq"][0]),
        g_kv=f(inp["g_kv"][0]), w_uk=f(inp["w_uk"][0].reshape(256, 512)), w_uv=f(inp["w_uv"][0].reshape(256, 512)),
        rel_bias=f(inp["rel_bias"][0]), g_out_a=f(inp["g_out_a"][0]), g_out_b=f(inp["g_out_b"][0]),
        w_out=f(inp["w_out"][0]), g_ffn=f(inp["g_ffn"][0]), w_gate=f(inp["w_gate"][0]), w_up=f(inp["w_up"][0]),
        w_down=f(inp["w_down"][0]), g_final=f(inp["g_final"]),
    )


_NC_CACHE = {}


def kernel(**inputs):
    n_cores = 8
    nb, S = inputs["x_prompt"].shape[0], inputs["x_prompt"].shape[1]
    nsb = inputs["x_sample"].shape[0]
    past = inputs["cache_mla_ckv"].shape[2]
    nseq, nss = nb // n_cores, nsb // n_cores
    key = (nseq, S, nss, past)
    if key not in _NC_CACHE:
        _NC_CACHE[key] = build_program(nseq, S, nss, past)
    nc = _NC_CACHE[key]
    inp = {k: np.asarray(v) for k, v in inputs.items()}
    in_maps = [core_inputs(inp, c, nseq, nss) for c in range(n_cores)]
    res = run_bass_kernel_spmd(nc, in_maps, core_ids=list(range(n_cores)))
    r = res.results
    cat = lambda name: np.concatenate([np.asarray(r[c][name]) for c in range(n_cores)], axis=0)
    y_p = cat("y_prompt")
    y_s = cat("y_sample").reshape(nsb, 16, 1024)
    return (
        y_p, y_s,
        cat("new_ckv_prompt")[None], cat("new_kr_prompt")[None],
        cat("new_bk_prompt").reshape(nb, 512, 8, 64)[None], cat("new_bv_prompt").reshape(nb, 512, 8, 64)[None],
        cat("new_ckv_sample").reshape(nsb, 16, 256)[None], cat("new_kr_sample").reshape(nsb, 16, 32)[None],
        cat("new_bk_sample").reshape(nsb, 16, 8, 64)[None], cat("new_bv_sample").reshape(nsb, 16, 8, 64)[None],
    )
```

```python
import contextlib
import math
import numpy as np
import concourse.bass as bass
import concourse.mybir as mybir
from concourse.bass_utils import run_bass_kernel_spmd

F32 = mybir.dt.float32
BF16 = mybir.dt.bfloat16
I32 = mybir.dt.int32
AF = mybir.ActivationFunctionType
ALU = mybir.AluOpType
AX = mybir.AxisListType


class Op:
    __slots__ = ("eng", "fn", "idx", "deps", "signal", "sig", "waits", "is_dma",
                 "dsem", "dval", "dwaits")

    def __init__(self, eng, fn, is_dma):
        self.eng = eng
        self.fn = fn
        self.is_dma = is_dma
        self.signal = False
        self.sig = 0
        self.waits = []
        self.dwaits = []
        self.deps = ()
        self.dsem = None
        self.dval = 0


class TileState:
    __slots__ = ("w", "r", "rd")

    def __init__(self):
        self.w = None
        self.r = {}
        self.rd = []


class Prog:
    ENGS = ("pe", "act", "dve", "pool", "sp")
    NDS = 20

    def __init__(self, nc, stack):
        self.nc = nc
        self.stack = stack
        self.eng_ops = {e: [] for e in self.ENGS}
        self.tiles = {}
        self.esem = {e: stack.enter_context(nc.semaphore("s_" + e)) for e in self.ENGS}
        self.nds = {"sp": 20, "pool": 44, "act": 1}
        self.dsems = {q: [stack.enter_context(nc.semaphore("d_%s%d" % (q, i)))
                          for i in range(self.nds[q])] for q in ("sp", "pool", "act")}
        self.dcount = {"sp": 0, "pool": 0, "act": 0}
        self.all_ops = []

    def _ts(self, key):
        t = self.tiles.get(key)
        if t is None:
            t = TileState()
            self.tiles[key] = t
        return t

    def op(self, eng, fn, reads=(), writes=(), dma=False):
        o = Op(eng, fn, dma)
        deps = []
        for k in reads:
            t = self._ts(k)
            if t.w is not None:
                deps.append(t.w)
            if isinstance(k, tuple) and k[0] == "ps":
                deps.extend(o2 for e2, o2 in t.r.items() if e2 != eng)
        for k in writes:
            t = self._ts(k)
            if t.w is not None:
                deps.append(t.w)
            deps.extend(t.r.values())
            deps.extend(t.rd)
        for k in reads:
            t = self._ts(k)
            if dma:
                t.rd.append(o)
            else:
                t.r[eng] = o
        for k in writes:
            t = self._ts(k)
            t.w = o
            t.r = {}
            t.rd = []
        o.deps = deps
        o.idx = len(self.eng_ops[eng])
        self.eng_ops[eng].append(o)
        self.all_ops.append(o)
        if dma:
            n = self.dcount[eng]
            self.dcount[eng] = n + 1
            o.dsem = self.dsems[eng][n % self.nds[eng]]
            o.dval = 16 * (n // self.nds[eng] + 1)
        return o

    def barrier(self):
        last = []
        for e in self.ENGS:
            for o in reversed(self.eng_ops[e]):
                if o.fn is not None and not o.is_dma:
                    last.append(o)
                    break
        pend = [o for o in self.all_ops if o.is_dma]
        for e in self.ENGS:
            o = Op(e, None, False)
            o.deps = [d for d in last if not d.is_dma] + pend
            o.idx = len(self.eng_ops[e])
            self.eng_ops[e].append(o)
            self.all_ops.append(o)
        self.tiles = {}

    def dma(self, q, out, in_, reads=(), writes=(), **kw):
        return self.op(q, lambda e: e.dma_start(out=out, in_=in_, **kw), reads, writes, dma=True)

    def resolve(self):
        waited = {e: {} for e in self.ENGS}
        dwaited = {e: {} for e in self.ENGS}
        for o in self.all_ops:
            E = o.eng
            need = {}
            dneed = {}
            for d in o.deps:
                if d is o:
                    continue
                if d.is_dma:
                    k = id(d.dsem)
                    if dneed.get(k, (None, 0))[1] < d.dval:
                        dneed[k] = (d.dsem, d.dval)
                else:
                    if d.eng == "pe" and E == "pe":
                        continue
                    c = need.get(d.eng)
                    if c is None or d.idx > c.idx:
                        need[d.eng] = d
            if o.is_dma and o.dval > 16:
                k = id(o.dsem)
                if dneed.get(k, (None, 0))[1] < o.dval - 16:
                    dneed[k] = (o.dsem, o.dval - 16)
            for P, d in need.items():
                if waited[E].get(P, -1) >= d.idx:
                    continue
                waited[E][P] = d.idx
                d.signal = True
                o.waits.append(d)
            for k, (sem, val) in dneed.items():
                if dwaited[E].get(k, 0) >= val:
                    continue
                dwaited[E][k] = val
                o.dwaits.append((sem, val))
        for e in self.ENGS:
            c = 0
            for o in self.eng_ops[e]:
                if o.is_dma:
                    continue
                if o.signal:
                    c += 1
                    o.sig = c

    def _emit_eng(self, name, e):
        for o in self.eng_ops[name]:
            for d in o.waits:
                e.wait_ge(self.esem[d.eng], d.sig)
            for sem, val in o.dwaits:
                e.wait_ge(sem, val)
            if o.fn is None:
                continue
            ins = o.fn(e)
            if o.is_dma:
                ins.then_inc(o.dsem, 16)
            elif o.signal:
                ins.then_inc(self.esem[name], 1)

    def finish(self):
        self.resolve()
        nc = self.nc
        with nc.Block() as block:
            @block.tensor
            def _(e):
                self._emit_eng("pe", e)

            @block.scalar
            def _(e):
                self._emit_eng("act", e)

            @block.vector
            def _(e):
                self._emit_eng("dve", e)

            @block.gpsimd
            def _(e):
                self._emit_eng("pool", e)
                for q in ("pool",):
                    n = self.dcount[q]
                    for i in range(min(n, self.nds[q])):
                        cnt = (n - 1 - i) // self.nds[q] + 1
                        e.wait_ge(self.dsems[q][i], 16 * cnt)

            @block.sync
            def _(e):
                self._emit_eng("sp", e)
                for q in ("sp",):
                    n = self.dcount[q]
                    for i in range(min(n, self.nds[q])):
                        cnt = (n - 1 - i) // self.nds[q] + 1
                        e.wait_ge(self.dsems[q][i], 16 * cnt)


D = 1024
DFF = 2816
NFC = 22
EPS = 1e-6
MLA_SCALE = 96 ** -0.5
PAST_FULL = 4096
SL_A, SL_V, SL_KB, SL_QB, SL_S, SL_O0, SL_O1 = 0, 1, 2, 3, 4, 5, 6
SL_G0 = 7
SL_D0 = 19
NSLOT = 25
S_KR, S_UQ, S_UK, S_UV = 0, 256, 1792, 2816
G_CHUNKS = [[0, 1], [2, 3], [4, 5], [6, 7], [8, 9], [10], [11, 12], [13, 14], [15, 16], [17, 18], [19, 20], [21]]
D_CHUNKS = [[0, 1, 2, 3], [4, 5, 6, 7], [8, 9, 10], [11, 12, 13, 14], [15, 16, 17, 18], [19, 20, 21]]
BAND_T = [((0, 0), 1, 1), ((0, 2), 3, 1), ((0, 4), 5, 1), ((0, 6), 7, 1),
          ((1, 7), 0, 0), ((3, 7), 2, 0), ((5, 7), 4, 0), ((7, 7), 6, 0)]


def build_program(NSEQ, S, NSS, PAST):
    nc = bass.Bass("TRN2", target_bir_lowering=False)
    NT = S // 128
    NST = S // 512
    NKT = PAST // 128
    dt_in = lambda name, shape: nc.dram_tensor(name, shape, F32, kind="ExternalInput")
    dt_out = lambda name, shape: nc.dram_tensor(name, shape, F32, kind="ExternalOutput")
    NSEQd, NSSd = max(NSEQ, 1), max(NSS, 1)
    x_p = dt_in("x_prompt", [NSEQd, S, D])
    x_s = dt_in("x_sample", [NSSd * 16, D])
    c_ckv = dt_in("cache_mla_ckv", [NSSd, PAST, 256])
    c_kr = dt_in("cache_mla_krope", [NSSd, PAST, 32])
    c_bk = dt_in("cache_band_k", [NSSd, 512, 512])
    c_bv = dt_in("cache_band_v", [NSSd, 512, 512])
    w_in = dt_in("w_in", [D, 2080])
    g_attn = dt_in("g_attn", [D])
    g_q = dt_in("g_q", [256])
    w_uq = dt_in("w_uq", [256, 768])
    g_kv = dt_in("g_kv", [256])
    w_uk = dt_in("w_uk", [256, 512])
    w_uv = dt_in("w_uv", [256, 512])
    rel_bias = dt_in("rel_bias", [8, 513])
    g_out_a = dt_in("g_out_a", [512])
    g_out_b = dt_in("g_out_b", [512])
    w_out = dt_in("w_out", [D, D])
    g_ffn = dt_in("g_ffn", [D])
    w_gate = dt_in("w_gate", [D, DFF])
    w_up = dt_in("w_up", [D, DFF])
    w_down = dt_in("w_down", [DFF, D])
    g_final = dt_in("g_final", [D])
    y_p = dt_out("y_prompt", [NSEQd, S, D])
    y_s = dt_out("y_sample", [NSSd * 16, D])
    o_ckv_p = dt_out("new_ckv_prompt", [NSEQd, S, 256])
    o_kr_p = dt_out("new_kr_prompt", [NSEQd, S, 32])
    o_bk_p = dt_out("new_bk_prompt", [NSEQd, 512, 512])
    o_bv_p = dt_out("new_bv_prompt", [NSEQd, 512, 512])
    o_ckv_s = dt_out("new_ckv_sample", [NSSd * 16, 256])
    o_kr_s = dt_out("new_kr_sample", [NSSd * 16, 32])
    o_bk_s = dt_out("new_bk_sample", [NSSd * 16, 512])
    o_bv_s = dt_out("new_bv_sample", [NSSd * 16, 512])
    wsc = nc.dram_tensor("wsc", [NSLOT, 128, 4096], BF16, kind="Internal")
    rbp = nc.dram_tensor("rbp", [8, 641], F32, kind="Internal")

    with contextlib.ExitStack() as st:
        P = Prog(nc, st)
        sbt = lambda name, shape, dt: st.enter_context(nc.sbuf_tensor(name, shape, dt))
        NTT = max(NT, 1) + 1

        ident = sbt("ident", [128, 128], BF16)
        identf = sbt("identf", [128, 128], F32)
        gat = sbt("gat", [128, 8], F32)
        gqT = sbt("gqT", [128, 2], F32)
        gmix = sbt("gmix", [128, 8], F32)
        gffn = sbt("gffn", [128, 8], F32)
        gkv_bc = sbt("gkv_bc", [128, 256], F32)
        gfin_bc = sbt("gfin_bc", [128, D], F32)
        cbias = sbt("cbias", [128, 8], F32)
        toep = sbt("toep", [128, 8, 384], F32)
        cs = sbt("cs", [128, NTT, 2, 16], F32)
        cst = sbt("cst", [128, 4], F32)
        stat = sbt("stat", [128, 64], F32)
        PS = st.enter_context(nc.psum_tensor("PS", [128, 8, 512], F32))

        def psf(b, rows=128):
            return PS[0:rows, b, :]

        def psb(b, rows=128):
            return PS[0:rows, b, :].bitcast(BF16)

        ARENA = 82336
        arena = sbt("arena", [128, ARENA], BF16)
        apos = [0]

        def carve(n_el, dt=BF16):
            n16 = n_el * (2 if dt == F32 else 1)
            n16 = (n16 + 15) // 16 * 16
            a = apos[0]
            apos[0] += n16
            assert apos[0] <= ARENA, ("arena overflow", apos[0])
            v = arena[:, a:a + n_el * (2 if dt == F32 else 1)]
            return v.bitcast(F32) if dt == F32 else v

        def ACT(out, in_, func, r, w, **kw):
            P.op("act", lambda e: e.activation(out=out, in_=in_, func=func, **kw), r, w)

        def TT(out, in0, in1, op, r, w, eng="dve"):
            P.op(eng, lambda e: e.tensor_tensor(out=out, in0=in0, in1=in1, op=op), r, w)

        def TS(out, in0, s1, op0, r, w, s2=None, op1=None, eng="dve"):
            if op1 is None:
                P.op(eng, lambda e: e.tensor_scalar(out=out, in0=in0, scalar1=s1, scalar2=None, op0=op0), r, w)
            else:
                P.op(eng, lambda e: e.tensor_scalar(out=out, in0=in0, scalar1=s1, scalar2=s2, op0=op0, op1=op1), r, w)

        def CP(out, in_, r, w, eng="dve"):
            if eng == "act":
                P.op("act", lambda e: e.copy(out=out, in_=in_), r, w)
            else:
                P.op(eng, lambda e: e.tensor_copy(out=out, in_=in_), r, w)

        def MM(out, lhsT, rhs, start, stop, r, w, sg=False):
            P.op("pe", lambda e: e.matmul(out, lhsT=lhsT, rhs=rhs, start=start, stop=stop, skip_group_check=sg), r, w)

        def TR(out, in_, idn, r, w):
            P.op("pe", lambda e: e.transpose(out=out, in_=in_, identity=idn), r, w)

        def MEMSET(ap, val, w, eng="pool"):
            P.op(eng, lambda e: e.memset(ap, val), (), w)

        rr = {"mm": 0, "sc": 0, "acc": 0, "misc": 0}
        POOLS = {"mm": [2, 3, 4, 5, 6, 7], "sc": [2, 3, 4, 5], "acc": [0, 1], "misc": [6, 7]}

        def bank(pool):
            lst = POOLS[pool]
            b = lst[rr[pool] % len(lst)]
            rr[pool] += 1
            return b

        MEMSET(identf[:], 0.0, ["identf"])
        P.op("pool", lambda e: e.affine_select(out=identf[:], in_=identf[:], pattern=[[-1, 128]],
                                                 compare_op=ALU.not_equal, fill=1.0, base=0,
                                                 channel_multiplier=1), ["identf"], ["identf"])
        CP(ident[:], identf[:], ["identf"], ["ident"])
        MEMSET(cst[:, 0:1], EPS, ["cst"])
        P.dma("sp", gat[:], g_attn.ap().rearrange("(c p) -> p c", p=128), writes=["gains"], allow_slow_non_contiguous=True)
        P.dma("sp", gffn[:], g_ffn.ap().rearrange("(c p) -> p c", p=128), writes=["gains"], allow_slow_non_contiguous=True)
        P.dma("sp", gqT[:], g_q.ap().rearrange("(c p) -> p c", p=128), writes=["gains"], allow_slow_non_contiguous=True)
        P.dma("sp", gmix[:, 0:4], g_out_a.ap().rearrange("(c p) -> p c", p=128), writes=["gains"], allow_slow_non_contiguous=True)
        P.dma("sp", gmix[:, 4:8], g_out_b.ap().rearrange("(c p) -> p c", p=128), writes=["gains"], allow_slow_non_contiguous=True)
        P.dma("sp", gkv_bc[:], bass.AP(g_kv, 0, [[0, 128], [1, 256]]), writes=["gkv_bc"])
        P.dma("sp", gfin_bc[:], bass.AP(g_final, 0, [[0, 128], [1, D]]), writes=["gfin_bc"])
        P.dma("sp", cbias[:].unsqueeze(2), bass.AP(rel_bias, 512, [[0, 128], [513, 8], [1, 1]]), writes=["cbias"], allow_slow_non_contiguous=True)
        P.dma("sp", rbp.ap()[:, 0:513], rel_bias.ap(), writes=["rbp"])
        P.dma("sp", rbp.ap()[:, 513:641].unsqueeze(2), bass.AP(rel_bias, 512, [[513, 8], [0, 128], [1, 1]]), writes=["rbp"], allow_slow_non_contiguous=True)
        def load_toeplitz():
            for k in range(128):
                P.dma("sp", toep[k:k + 1, :, :], rbp.ap()[:, 256 - k:640 - k].unsqueeze(0), reads=["rbp"], writes=["toep"])

        toep_state = {"done": False}
        posi = sbt("posi", [128, NTT], I32)
        posf = sbt("posf", [128, NTT], F32)
        inv = sbt("inv", [128, 16], F32)
        aT = carve(11 * 512).rearrange("p (c s) -> p c s", c=11)
        aTf = aT[:].rearrange("p c s -> p (c s)")
        ang = aTf[:, 0:2 * NTT * 16].bitcast(F32).rearrange("p (t j) -> p t j", t=NTT)
        angi = aTf[:, 1024:1024 + 2 * NTT * 16].bitcast(I32).rearrange("p (t j) -> p t j", t=NTT)
        angr = aTf[:, 2048:2048 + 2 * NTT * 16].bitcast(F32).rearrange("p (t j) -> p t j", t=NTT)
        AK = [("aT", i) for i in range(11)]
        P.op("pool", lambda e: e.iota(out=posi[:, 0:NTT - 1], pattern=[[128, NTT - 1]], base=0, channel_multiplier=1), (), ["posi"])
        P.op("pool", lambda e: e.iota(out=posi[:, NTT - 1:NTT], pattern=[[0, 1]], base=0, channel_multiplier=1), (), ["posi"])
        P.op("dve", lambda e: e.tensor_single_scalar(out=posi[:, NTT - 1:NTT], in_=posi[:, NTT - 1:NTT], scalar=15, op=ALU.bitwise_and),
             ["posi"], ["posi"])
        P.op("dve", lambda e: e.tensor_single_scalar(out=posi[:, NTT - 1:NTT], in_=posi[:, NTT - 1:NTT], scalar=PAST, op=ALU.add),
             ["posi"], ["posi"])
        CP(posf[:], posi[:], ["posi"], ["posf"])
        for j in range(16):
            MEMSET(inv[:, j:j + 1], float(np.float32(10000.0) ** np.float32(-(2 * j) / 32.0)), ["inv"])
        for t in range(NTT):
            TS(ang[:, t, :], inv[:], posf[:, t:t + 1], ALU.mult, ["inv", "posf"], ["ang"])
        angf = ang[:].rearrange("p t j -> p (t j)")
        angif = angi[:].rearrange("p t j -> p (t j)")
        angrf = angr[:].rearrange("p t j -> p (t j)")
        TS(angf, angf, 1.0 / (2 * math.pi), ALU.mult, ["ang"], ["ang"])
        CP(angif, angf, ["ang"], ["angi"])
        CP(angrf, angif, ["angi"], ["angr"])
        TT(angf, angf, angrf, ALU.subtract, ["ang", "angr"], ["ang"])
        for t in range(NTT):
            ACT(cs[:, t, 1, :], ang[:, t, :], AF.Sin, ["ang"], ["cs"], scale=2 * math.pi)
        TS(angf, angf, 0.25, ALU.add, ["ang"], ["ang"])
        TS(angrf, angf, 0.5, ALU.is_gt, ["ang"], ["angr"])
        TT(angf, angf, angrf, ALU.subtract, ["ang", "angr"], ["ang"])
        for t in range(NTT):
            ACT(cs[:, t, 0, :], ang[:, t, :], AF.Sin, ["ang"], ["cs"], scale=2 * math.pi)
        P.op("dve", lambda e: e.memset(aT[:, 0, 0:16], 0.0), ["ang", "angi", "angr"], AK + ["ang", "angi", "angr"])

        def wview(slot, off, kc, n):
            return wsc.ap()[slot, :, off:off + kc * n].rearrange("p (k n) -> p k n", k=kc)

        def cast_cols(slot, off, src, c0, n, kc):
            P.dma("pool", wview(slot, off, kc, n), src.ap()[:, c0:c0 + n].rearrange("(k p) n -> p k n", p=128),
                  writes=[("wsc", slot)])

        casted = set()

        def cast_slot(slot):
            if slot in casted:
                return
            casted.add(slot)
            if slot == SL_A:
                cast_cols(SL_A, 0, w_in, 0, 512, 8)
            elif slot == SL_V:
                cast_cols(SL_V, 0, w_in, 1568, 512, 8)
            elif slot == SL_KB:
                cast_cols(SL_KB, 0, w_in, 1056, 512, 8)
            elif slot == SL_QB:
                cast_cols(SL_QB, 0, w_in, 544, 512, 8)
            elif slot == SL_S:
                cast_cols(SL_S, S_KR, w_in, 512, 32, 8)
                cast_cols(SL_S, S_UQ, w_uq, 0, 768, 2)
                cast_cols(SL_S, S_UK, w_uk, 0, 512, 2)
                cast_cols(SL_S, S_UV, w_uv, 0, 512, 2)
            elif slot == SL_O0:
                cast_cols(SL_O0, 0, w_out, 0, 512, 8)
            elif slot == SL_O1:
                cast_cols(SL_O1, 0, w_out, 512, 512, 8)
            elif slot < SL_D0:
                gi = slot - SL_G0
                chunks = G_CHUNKS[gi]
                n = len(chunks)
                c0 = chunks[0] * 128
                dst = wsc.ap()[SL_G0 + gi, :, :].rearrange("p (k n) -> p k n", k=8)
                P.dma("pool", dst[:, :, 0:n * 128], w_gate.ap()[:, c0:c0 + n * 128].rearrange("(k p) n -> p k n", p=128),
                      writes=[("wsc", SL_G0 + gi)])
                P.dma("pool", dst[:, :, 256:256 + n * 128], w_up.ap()[:, c0:c0 + n * 128].rearrange("(k p) n -> p k n", p=128),
                      writes=[("wsc", SL_G0 + gi)])
            else:
                di = slot - SL_D0
                chunks = D_CHUNKS[di]
                f0 = chunks[0] * 128
                n = len(chunks)
                P.dma("pool", wsc.ap()[SL_D0 + di, :, 0:n * 1024].rearrange("p (c n) -> p c n", c=n),
                      w_down.ap()[f0:f0 + n * 128, :].rearrange("(c p) n -> p c n", p=128),
                      writes=[("wsc", SL_D0 + di)])

        NRING = 3
        ring = [sbt("ring%d" % i, [128, 4096], BF16) for i in range(NRING)]
        wq = {"seq": [], "loaded": 0, "used": 0}

        def slot_elems(slot):
            if slot == SL_S:
                return 3840
            if SL_G0 <= slot < SL_D0:
                return 8 * 512
            if slot >= SL_D0:
                return len(D_CHUNKS[slot - SL_D0]) * 1024
            return 4096

        def wload_upto(n):
            while wq["loaded"] < min(n, len(wq["seq"])):
                i = wq["loaded"]
                slot = wq["seq"][i]
                for j_ in range(i, min(i + 10, len(wq["seq"]), 25)):
                    cast_slot(wq["seq"][j_])
                ne = slot_elems(slot)
                if slot in (SL_G0 + 5, SL_G0 + 11):
                    P.dma("sp", ring[i % NRING][:, 0:4096].rearrange("p (k a n) -> p k a n", k=8, a=2)[:, :, :, 0:128],
                          wsc.ap()[slot, :, :].rearrange("p (k a n) -> p k a n", k=8, a=2)[:, :, :, 0:128],
                          reads=[("wsc", slot)], writes=[("wr", i % NRING)])
                else:
                    P.dma("sp", ring[i % NRING][:, 0:ne], wsc.ap()[slot, :, 0:ne],
                          reads=[("wsc", slot)], writes=[("wr", i % NRING)])
                wq["loaded"] += 1

        def wnext(slot, hold=0):
            i = wq["used"]
            assert wq["seq"][i] == slot, (i, wq["seq"][i], slot)
            wload_upto(i + NRING - hold)
            wq["used"] += 1
            return ring[i % NRING], ("wr", i % NRING)

        PASS_SLOTS = [SL_A, SL_V, SL_KB, SL_QB, SL_S, SL_O0, SL_O1]
        FFN_SLOTS = []
        for g in range(2):
            FFN_SLOTS += [SL_G0 + 6 * g + i for i in range(6)] + [SL_D0 + 3 * g + i for i in range(3)]
        n_pass = NSEQ * NST + (1 if NSS else 0)
        wq["seq"] = (PASS_SLOTS + FFN_SLOTS) * n_pass

        KT = carve(8 * S).rearrange("p (h s) -> p h s", h=8)
        VmF = carve(NT * 520 + 64)
        Vm = VmF[:, 0:NT * 520].rearrange("p (t h d) -> p t h d", t=NT, h=8)
        KbT = carve(4 * 1024).rearrange("p (j s) -> p j s", j=4)
        VbF = carve(8 * 520 + 64)
        Vb = VbF[:, 0:8 * 520].rearrange("p (t h d) -> p t h d", t=8, h=8)
        persist_end = apos[0]
        actT = carve(8 * 512).rearrange("p (c s) -> p c s", c=8)
        QT = carve(8 * 512).rearrange("p (h s) -> p h s", h=8)
        qb_off = apos[0]
        QbT = carve(4 * 512).rearrange("p (j s) -> p j s", j=4)
        cqnT = carve(2 * 512).rearrange("p (c s) -> p c s", c=2)
        ckvnT = carve(2 * 512).rearrange("p (c s) -> p c s", c=2)
        assert apos[0] == qb_off + 4096
        stgB = arena[:, qb_off:qb_off + 4096].bitcast(F32).rearrange("p (k d) -> p k d", k=2)
        xs = carve(1024)
        x1 = carve(4 * 1024, F32).rearrange("p (t d) -> p t d", t=4)
        oa = carve(4 * 1024).rearrange("p (t d) -> p t d", t=4)
        PTall = carve(2048)
        PT = [PTall[:, i * 512:(i + 1) * 512] for i in range(4)]
        xsb = [(xs, ["xs"]), (PTall[:, 0:1024], [("PT", 0), ("PT", 1)])]
        junk = PTall[:, 1024:2048]
        JK = [("PT", 2), ("PT", 3)]
        oaF = oa[:].rearrange("p t d -> p (t d)").bitcast(F32).rearrange("p (k d) -> p k d", k=2)
        okeys = lambda k: [("oa", 2 * k, 0), ("oa", 2 * k, 1), ("oa", 2 * k + 1, 0), ("oa", 2 * k + 1, 1)]

        def xstage(t, rows):
            if t < 2:
                return oaF[0:rows, t, :], okeys(t)
            if t == 2:
                return stgB[0:rows, 0, :], [("QbT", j) for j in range(4)]
            return stgB[0:rows, 1, :], [("cqnT", i) for i in range(4)] + [("ckvnT", i) for i in range(4)]
        stmp = [carve(512, F32) for _ in range(2)]
        oT = [carve(512, F32) for _ in range(2)]
        sg = [carve(512) for _ in range(2)]
        ostage = [carve(512, F32) for _ in range(2)]
        cq_bf = [carve(256) for _ in range(2)]
        ckv_f = [carve(256, F32) for _ in range(2)]
        ckv_bf = [carve(256) for _ in range(2)]
        kr_f = [carve(32, F32) for _ in range(2)]
        kr_t = [carve(64, F32) for _ in range(2)]
        krpad = [carve(96) for _ in range(2)]
        qf = [carve(768, F32).rearrange("p (h d) -> p h d", h=8) for _ in range(2)]
        q_bf = [carve(768).rearrange("p (h d) -> p h d", h=8) for _ in range(2)]
        q_t1 = carve(4 * 128, F32).rearrange("p (a h d) -> p a h d", a=4, h=8)
        q_t = [q_t1, q_t1]
        krT_sb = carve(512)
        prompt_end = apos[0]

        if NSEQ:
            MEMSET(VmF[:, :], 1.0, [("Vm", i) for i in range(NT)])
            MEMSET(KT[96:128, :, :].rearrange("p h s -> p (h s)"), 0.0, ["KTpad"])
        MEMSET(QT[96:128, :, :].rearrange("p h s -> p (h s)"), 0.0, ["QTpad"])
        MEMSET(VbF[:, :], 1.0, [("Vb", i) for i in range(8)])
        for i in range(2):
            MEMSET(krpad[i], 0.0, [("krpad", i)])

        cnt = {"tile": 0, "blk": 0, "ft": 0, "xs": 0, "st": 0}

        def rstd_from_ss(col, n, R, key_in, key_out):
            ACT(stat[0:R, col + 1:col + 2], stat[0:R, col:col + 1], AF.Ln, [key_in, "cst"], [key_out + "_ln"],
                scale=1.0 / n, bias=cst[0:R, 0:1])
            ACT(stat[0:R, col + 1:col + 2], stat[0:R, col + 1:col + 2], AF.Exp, [key_out + "_ln"], [key_out], scale=-0.5)

        def to_featT(src_bf, R, t, gcol, dstT, dkey, nchunk, rkeys, scale_ap):
            b = bank("misc")
            pv = psb(b)
            for c in range(nchunk):
                TR(pv[:, c * R:(c + 1) * R], src_bf[0:R, c * 128:(c + 1) * 128], ident[0:R, 0:R], rkeys + ["ident"], [("ps", b)])
            o = dstT[:, 0:nchunk, t * R:(t + 1) * R]
            i = pv[:, 0:nchunk * R].rearrange("p (c r) -> p c r", c=nchunk)
            if scale_ap is None:
                use_act = (cnt["ft"] % 2 == 1)
                cnt["ft"] += 1
                CP(o, i, [("ps", b)], [(dkey, t)], eng=("act" if use_act else "dve"))
            else:
                g = scale_ap[:, gcol:gcol + nchunk].unsqueeze(2).broadcast_to([128, nchunk, R])
                TT(o, i, g, ALU.mult, [("ps", b), "gains"], [(dkey, t)])

        def norm_to_featT(cx, t, src_f32, junk_ap, junk_keys, n, gains, rkeys, sc, kp="x"):
            R = cx["R"]
            xb, xk = xsb[cnt["xs"] % 2]
            cnt["xs"] += 1
            ACT(junk_ap, src_f32, AF.Square, rkeys, junk_keys + [("ss" + kp, t)], accum_out=stat[0:R, sc:sc + 1])
            rstd_from_ss(sc, n, R, ("ss" + kp, t), "rs%s%d" % (kp, t))
            TS(xb[0:R, :], src_f32, stat[0:R, sc + 1:sc + 2], ALU.mult, rkeys + ["rs%s%d" % (kp, t)], xk)
            to_featT(xb, R, t, 0, actT, "actT", 8, xk, gains)

        def phase_P(cx):
            R, T, C = cx["R"], cx["T"], cx["T"] * cx["R"]
            last = cx["last"]
            xdone = cx.get("xdone", set())
            for t in range(T):
                if t not in xdone:
                    P.dma("pool", x1[0:R, t, :], cx["x_src"](t), writes=[("x1", t)])
            W, wkA = wnext(SL_A)
            WvA = W[:, 0:4096].rearrange("p (k n) -> p k n", k=8)
            tinfo = []

            xn_done = set(xdone)

            def xnorm(t):
                if t not in xn_done:
                    xn_done.add(t)
                    norm_to_featT(cx, t, x1[0:R, t, :], junk[0:R, :], JK, D, gat, [("x1", t)], 2 * t)

            def A_tile(t):
                if t + 1 < T:
                    xnorm(t + 1)
                b = bank("mm")
                for c in range(8):
                    MM(psf(b, R), actT[:, c, t * R:(t + 1) * R], WvA[:, c, :], c == 0, c == 7, [("actT", t), wkA], [("ps", b)])
                i2 = cnt["tile"] % 2
                cnt["tile"] += 1
                tinfo.append(i2)
                if t == 0:
                    for tt in sorted(xdone):
                        sa, sk = xstage(tt, R)
                        CP(x1[0:R, tt, :], sa, sk, [("x1", tt)])
                s0 = 8 + 4 * t
                ACT(cq_bf[i2][0:R, :], psf(b, R)[:, 0:256], AF.Square, [("ps", b)], [("cq_bf", i2), ("ssq", t)],
                    accum_out=stat[0:R, s0:s0 + 1])
                ACT(ckv_bf[i2][0:R, :], psf(b, R)[:, 256:512], AF.Square, [("ps", b)], [("ckv_bf", i2), ("sskv", t)],
                    accum_out=stat[0:R, s0 + 2:s0 + 3])
                rstd_from_ss(s0, 256, R, ("ssq", t), "rsq%d" % t)
                rstd_from_ss(s0 + 2, 256, R, ("sskv", t), "rskv%d" % t)
                ACT(cq_bf[i2][0:R, :], psf(b, R)[:, 0:256], AF.Copy, [("ps", b), "rsq%d" % t], [("cq_bf", i2)],
                    scale=stat[0:R, s0 + 1:s0 + 2])
                ACT(ckv_f[i2][0:R, :], psf(b, R)[:, 256:512], AF.Copy, [("ps", b), "rskv%d" % t], [("ckv_f", i2)],
                    scale=stat[0:R, s0 + 3:s0 + 4])
                TT(ckv_f[i2][0:R, :], ckv_f[i2][0:R, :], gkv_bc[0:R, :], ALU.mult, [("ckv_f", i2), "gkv_bc"], [("ckv_f", i2)], eng="pool")
                P.dma("pool", cx["o_ckv"](t), ckv_f[i2][0:R, :], reads=[("ckv_f", i2)])
                CP(ckv_bf[i2][0:R, :], ckv_f[i2][0:R, :], [("ckv_f", i2)], [("ckv_bf", i2)], eng="pool")

            def TRcq(tt):
                j2 = tinfo[tt]
                to_featT(cq_bf[j2], R, tt, 0, cqnT, "cqnT", 2, [("cq_bf", j2)], gqT)
                to_featT(ckv_bf[j2], R, tt, 0, ckvnT, "ckvnT", 2, [("ckv_bf", j2)], None)

            xnorm(0)
            g1 = list(range(0, min(2, T)))
            g2 = list(range(2, T))
            for t in g1:
                A_tile(t)
            for t in range(T):
                xnorm(t)
            W, wk = wnext(SL_V, hold=1)
            Wv = W[:, 0:4096].rearrange("p (k n) -> p k n", k=8)
            for t in range(T):
                b = bank("mm")
                for c in range(8):
                    MM(psf(b, R), actT[:, c, t * R:(t + 1) * R], Wv[:, c, :], c == 0, c == 7, [("actT", t), wk], [("ps", b)])
                vdst, vkey = cx["vb_dst"](t)
                CP(vdst, psf(b, R).rearrange("p (h d) -> p h d", h=8), [("ps", b)], [vkey])
                if last:
                    i2 = cnt["tile"] % 2
                    cnt["tile"] += 1
                    CP(ostage[i2][0:R, :], psf(b, R), [("ps", b)], [("ostage", i2)], eng="act")
                    P.dma("pool", cx["o_bv"](t), ostage[i2][0:R, :], reads=[("ostage", i2)])
            for t in g1:
                TRcq(t)
            for t in g2:
                A_tile(t)
            W, wk = wnext(SL_KB)
            Wv = W[:, 0:4096].rearrange("p (k n) -> p k n", k=8)
            if last:
                for t in range(T):
                    b = bank("mm")
                    for c in range(8):
                        MM(psf(b, R), actT[:, c, t * R:(t + 1) * R], Wv[:, c, :], c == 0, c == 7, [("actT", t), wk], [("ps", b)])
                    i2 = cnt["tile"] % 2
                    cnt["tile"] += 1
                    CP(ostage[i2][0:R, :], psf(b, R), [("ps", b)], [("ostage", i2)], eng="act")
                    P.dma("pool", cx["o_bk"](t), ostage[i2][0:R, :], reads=[("ostage", i2)])
            for j in range(4):
                b = bank("mm")
                for c in range(8):
                    MM(psf(b)[:, 0:C], Wv[:, c, j * 128:(j + 1) * 128], actT[:, c, 0:C], c == 0, c == 7,
                       [("actT", t) for t in range(T)] + [wk], [("ps", b)])
                kdst, kkey = cx["kbT_dst"](j)
                if j % 2 == 0:
                    CP(kdst, psf(b)[:, 0:C], [("ps", b)], [kkey])
                else:
                    CP(kdst, psf(b)[:, 0:C], [("ps", b)], [kkey], eng="act")
            for t in g2:
                TRcq(t)
            W, wk = wnext(SL_QB)
            Wv = W[:, 0:4096].rearrange("p (k n) -> p k n", k=8)
            for j in range(4):
                b = bank("mm")
                for c in range(8):
                    MM(psf(b)[:, 0:C], Wv[:, c, j * 128:(j + 1) * 128], actT[:, c, 0:C], c == 0, c == 7,
                       [("actT", t) for t in range(T)] + [wk], [("ps", b)])
                if j % 2 == 0:
                    TS(QbT[:, j, 0:C], psf(b)[:, 0:C], 0.125, ALU.mult, [("ps", b)], [("QbT", j)])
                else:
                    ACT(QbT[:, j, 0:C], psf(b)[:, 0:C], AF.Copy, [("ps", b)], [("QbT", j)], scale=0.125)
            W, wk = wnext(SL_S)
            Wkr = W[:, S_KR:S_KR + 256].rearrange("p (k n) -> p k n", k=8)
            Wuq = W[:, S_UQ:S_UQ + 1536].rearrange("p (k n) -> p k n", k=2)
            Wuk = W[:, S_UK:S_UK + 1024].rearrange("p (k n) -> p k n", k=2)
            Wuv = W[:, S_UV:S_UV + 1024].rearrange("p (k n) -> p k n", k=2)
            sinfo = []
            for t in range(T):
                b = bank("mm")
                for c in range(2):
                    MM(psf(b, R), ckvnT[:, c, t * R:(t + 1) * R], Wuv[:, c, :], c == 0, c == 1, [("ckvnT", t), wk], [("ps", b)])
                vdst, vkey = cx["vm_dst"](t)
                CP(vdst, psf(b, R).rearrange("p (h d) -> p h d", h=8), [("ps", b)], [vkey], eng="act")
            for h in range(8):
                b = bank("mm")
                for c in range(2):
                    MM(psf(b, 64)[:, 0:C], Wuk[:, c, h * 64:(h + 1) * 64], ckvnT[:, c, 0:C], c == 0, c == 1,
                       [("ckvnT", t) for t in range(T)] + [wk], [("ps", b)])
                kd, kk = cx["ktn_dst"](h)
                if h % 2 == 0:
                    CP(kd, psf(b, 64)[:, 0:C], [("ps", b)], [kk])
                else:
                    CP(kd, psf(b, 64)[:, 0:C], [("ps", b)], [kk], eng="act")
            for t in range(T):
                ti = cx["pos_tile"](t)
                cosv = cs[0:R, ti, 0, :]
                sinv = cs[0:R, ti, 1, :]
                i2 = cnt["tile"] % 2
                cnt["tile"] += 1
                sinfo.append(i2)
                b = bank("mm")
                for c in range(8):
                    MM(psf(b, R)[:, 0:32], actT[:, c, t * R:(t + 1) * R], Wkr[:, c, :], c == 0, c == 7, [("actT", t), wk], [("ps", b)])
                CP(kr_t[i2][0:R, 0:32], psf(b, R)[:, 0:32], [("ps", b)], [("kr_t", i2)], eng="act")
                TT(kr_t[i2][0:R, 32:48], kr_t[i2][0:R, 0:16], cosv, ALU.mult, [("kr_t", i2), "cs"], [("kr_u", i2)])
                TT(kr_t[i2][0:R, 48:64], kr_t[i2][0:R, 16:32], sinv, ALU.mult, [("kr_t", i2), "cs"], [("kr_v", i2)])
                TT(kr_f[i2][0:R, 0:16], kr_t[i2][0:R, 32:48], kr_t[i2][0:R, 48:64], ALU.subtract,
                   [("kr_u", i2), ("kr_v", i2)], [("kr_f", i2)])
                TT(kr_t[i2][0:R, 32:48], kr_t[i2][0:R, 0:16], sinv, ALU.mult, [("kr_t", i2), "cs"], [("kr_u", i2)])
                TT(kr_t[i2][0:R, 48:64], kr_t[i2][0:R, 16:32], cosv, ALU.mult, [("kr_t", i2), "cs"], [("kr_v", i2)])
                TT(kr_f[i2][0:R, 16:32], kr_t[i2][0:R, 32:48], kr_t[i2][0:R, 48:64], ALU.add,
                   [("kr_u", i2), ("kr_v", i2)], [("kr_f", i2)])
                P.dma("pool", cx["o_kr"](t), kr_f[i2][0:R, :], reads=[("kr_f", i2)])
                CP(krpad[i2][0:R, 64:96], kr_f[i2][0:R, :], [("kr_f", i2)], [("krpad", i2)], eng="pool")
                qps = PS[0:R, 0:2, :].rearrange("p a n -> p (a n)")
                for c in range(2):
                    MM(qps[:, 0:512], cqnT[:, c, t * R:(t + 1) * R], Wuq[:, c, 0:512], c == 0, c == 1, [("cqnT", t), wk], [("ps", 0)])
                for c in range(2):
                    MM(qps[:, 512:768], cqnT[:, c, t * R:(t + 1) * R], Wuq[:, c, 512:768], c == 0, c == 1, [("cqnT", t), wk], [("ps", 1)])
                ACT(qf[i2][0:R].rearrange("p h d -> p (h d)"), qps[:, 0:768], AF.Copy, [("ps", 0), ("ps", 1)], [("qf", i2)],
                    scale=MLA_SCALE)
                CP(q_bf[i2][0:R, :, 0:64], qf[i2][0:R, :, 0:64], [("qf", i2)], [("q_bf", i2)], eng="pool")
                qa = qf[i2][0:R, :, 64:80]
                qb = qf[i2][0:R, :, 80:96]
                cb = cosv.unsqueeze(1).broadcast_to([R, 8, 16])
                sb_ = sinv.unsqueeze(1).broadcast_to([R, 8, 16])
                qt = q_t[i2]
                TT(qt[0:R, 0], qa, cb, ALU.mult, [("qf", i2), "cs"], ["q_t0"])
                TT(qt[0:R, 1], qb, sb_, ALU.mult, [("qf", i2), "cs"], ["q_t1"])
                TT(qt[0:R, 2], qa, sb_, ALU.mult, [("qf", i2), "cs"], ["q_t2"])
                TT(qt[0:R, 3], qb, cb, ALU.mult, [("qf", i2), "cs"], ["q_t3"])
                TT(q_bf[i2][0:R, :, 64:80], qt[0:R, 0], qt[0:R, 1], ALU.subtract, ["q_t0", "q_t1"], [("q_bf", i2)])
                TT(q_bf[i2][0:R, :, 80:96], qt[0:R, 2], qt[0:R, 3], ALU.add, ["q_t2", "q_t3"], [("q_bf", i2)])
                if t % 2 == 1 or t == T - 1:
                    for tt in range(t - (1 if t % 2 == 1 else 0), t + 1):
                        j2 = sinfo[tt]
                        bk_ = bank("misc")
                        TR(psb(bk_)[0:96, 0:R], krpad[j2][0:R, 0:96], ident[0:R, 0:R], [("krpad", j2), "ident"], [("ps", bk_)])
                        CP(krT_sb[64:96, tt * R:(tt + 1) * R], psb(bk_)[64:96, 0:R], [("ps", bk_)], ["krT_sb"])
                        bq = bank("misc")
                        for h in range(8):
                            TR(psb(bq)[0:96, h * R:(h + 1) * R], q_bf[j2][0:R, h, :], ident[0:R, 0:R], [("q_bf", j2), "ident"], [("ps", bq)])
                        CP(QT[0:96, :, tt * R:(tt + 1) * R], psb(bq)[0:96, 0:8 * R].rearrange("p (h r) -> p h r", h=8), [("ps", bq)],
                           [("QT", tt)])
            for h in range(8):
                kd, kk = cx["ktr_dst"](h)
                CP(kd, krT_sb[64:96, 0:C], ["krT_sb"], [kk], eng="pool")

        def finalize_head(cx, accb, h, mixer, tiles=None):
            R, T, C = cx["R"], cx["T"], cx["T"] * cx["R"]
            i2 = cnt["blk"] % 2
            cnt["blk"] += 1
            tl = list(range(T)) if tiles is None else tiles
            c0_, c1_ = tl[0] * R, (tl[-1] + 1) * R
            CP(oT[i2][0:65, c0_:c1_], psf(accb, 65)[:, c0_:c1_], [("ps", accb)], [("oT", i2)], eng=("dve" if mixer == 0 else "act"))
            b = bank("misc")
            for t in tl:
                TR(psf(b, R)[:, t * 65:(t + 1) * 65], oT[i2][0:65, t * R:(t + 1) * R], identf[0:65, 0:65], [("oT", i2), "identf"], [("ps", b)])
            pv = psf(b, R)[:, 0:T * 65].rearrange("p (t d) -> p t d", t=T)
            sc = 40 + 4 * i2
            t0_, t1_ = tl[0], tl[-1] + 1
            P.op("dve", lambda e: e.reciprocal(out=stat[0:R, sc + t0_:sc + t1_], in_=pv[:, t0_:t1_, 64]), [("ps", b)], [("rden", i2)])
            for t in tl:
                TS(oa[0:R, t, mixer * 512 + h * 64: mixer * 512 + (h + 1) * 64], pv[:, t, 0:64], stat[0:R, sc + t:sc + t + 1], ALU.mult,
                   [("ps", b), ("rden", i2)], [("oa", t, mixer)])

        def run_blocks(cx, blocks, LAG=2):
            n = len(blocks)
            for i in range(n + LAG):
                if i < n:
                    blocks[i]["score"]()
                j = i - LAG
                if 0 <= j < n:
                    blocks[j]["pv"]()
                    if blocks[j].get("fin"):
                        blocks[j]["fin"]()

        def phase_M(cx, st_i):
            blocks = []
            for h in range(8):
                accb = bank("acc")
                nkt = 4 * st_i + 4
                for kt in range(nkt):
                    d = kt - 4 * st_i
                    q0 = 0 if d < 0 else 128 * d
                    blk = {}
                    st_b = {}

                    def score(h=h, kt=kt, q0=q0, st_b=st_b):
                        b = bank("sc")
                        pi = cnt["blk"] % 4
                        cnt["blk"] += 1
                        st_b["pi"] = pi
                        MM(psf(b)[:, q0:512], KT[:, h, kt * 128:(kt + 1) * 128], QT[:, h, q0:512], True, True,
                           [("KTn", h, kt // 4), ("KTr", h, kt // 4), "KTpad", "QTpad"] + [("QT", t) for t in range(4)], [("ps", b)])
                        ACT(PT[pi][:, q0:512], psf(b)[:, q0:512], AF.Exp, [("ps", b)], [("PT", pi)])
                        if kt >= 4 * st_i:
                            MEMSET(PT[pi][64:128, q0:q0 + 64], 0.0, [("PT", pi)])

                    def pv(h=h, kt=kt, d=d, q0=q0, accb=accb, st_b=st_b):
                        pi = st_b["pi"]
                        w0 = (kt * 8 + h) * 65
                        MM(psf(accb)[:, q0:512], VmF[:, w0:w0 + 128], PT[pi][:, q0:512], kt == 0, kt == 4 * st_i + 3,
                           [("Vm", kt), ("Vm", min(kt + 1, NT - 1)), ("PT", pi)], [("ps", accb)], sg=True)

                    blk["score"] = score
                    blk["pv"] = pv
                    if kt == nkt - 1:
                        blk["fin"] = (lambda h=h, accb=accb: finalize_head(cx, accb, h, 0))
                    blocks.append(blk)
            run_blocks(cx, blocks)

        def phase_B(cx, st_i):
            blocks = []
            late = {"fn": None}

            def run_late():
                f = late["fn"]
                late["fn"] = None
                if f is not None:
                    f()

            for h in range(8):
                accb = bank("acc")
                hp, hj = (h % 2) * 64, h // 2
                tiles = [t for t in range(8) if 4 * st_i - 4 + t >= 0]
                for n_, t in enumerate(tiles):
                    m = 4 * st_i - 4 + t
                    slot = (m // 4) % 2
                    kcol = slot * 512 + (m % 4) * 128
                    vt = slot * 4 + (m % 4)
                    (c0, c1), ex, exh = BAND_T[t]
                    u0, u1 = min(c0, ex), max(c1, ex)
                    qa, qb_ = 64 * u0, 64 * (u1 + 1)
                    r0 = 64 * (u0 + 8 - 2 * t)
                    blk = {}
                    st_b = {}

                    def score(h=h, hp=hp, hj=hj, kcol=kcol, qa=qa, qb_=qb_, r0=r0, slot=slot, st_b=st_b, ex=ex, exh=exh):
                        b = bank("sc")
                        pi = cnt["blk"] % 4
                        cnt["blk"] += 1
                        st_b["pi"] = pi
                        n = qb_ - qa
                        MM(psf(b)[:, qa:qb_], KbT[hp:hp + 64, hj, kcol:kcol + 128], QbT[hp:hp + 64, hj, qa:qb_], True, True,
                           [("KbT", hj, slot), ("QbT", hj)], [("ps", b)])
                        nb = max(0, min(384, r0 + n) - r0)
                        if nb < n:
                            ACT(PT[pi][:, qa + nb:qb_], psf(b)[:, qa + nb:qb_], AF.Exp, [("ps", b), "cbias"], [("PT", pi)],
                                bias=cbias[:, h:h + 1])
                        si = cnt["st"] % 2
                        if nb > 0:
                            cnt["st"] += 1
                            TT(stmp[si][:, 0:nb], psf(b)[:, qa:qa + nb], toep[:, h, r0:r0 + nb], ALU.add, [("ps", b), "toep"], [("stmp", si)])
                        run_late()

                        def mylate():
                            if nb > 0:
                                ACT(PT[pi][:, qa:qa + nb], stmp[si][:, 0:nb], AF.Exp, [("stmp", si)], [("PT", pi)])
                            zp = 64 * (1 - exh)
                            MEMSET(PT[pi][zp:zp + 64, 64 * ex:64 * (ex + 1)], 0.0, [("PT", pi)])
                            st_b["late_done"] = True

                        late["fn"] = mylate

                    def pv(h=h, vt=vt, c0=c0, c1=c1, ex=ex, exh=exh, accb=accb, first=(n_ == 0), st_b=st_b, slot=slot,
                           lastb=(n_ == len(tiles) - 1)):
                        if not st_b.get("late_done"):
                            run_late()
                        pi = st_b["pi"]
                        u0, u1 = min(c0, ex), max(c1, ex)
                        w0 = (vt * 8 + h) * 65
                        MM(psf(accb)[:, 64 * u0:64 * (u1 + 1)], VbF[:, w0:w0 + 128], PT[pi][:, 64 * u0:64 * (u1 + 1)], first, lastb,
                           [("Vb", vt), ("Vb", min(vt + 1, 7)), ("PT", pi)], [("ps", accb)], sg=True)

                    blk["score"] = score
                    blk["pv"] = pv
                    if n_ == len(tiles) - 1:
                        blk["fin"] = (lambda h=h, accb=accb: finalize_head(cx, accb, h, 1))
                    blocks.append(blk)
            run_blocks(cx, blocks)

        def phase_O(cx):
            R, T = cx["R"], cx["T"]
            for t in range(T):
                s0 = 24 + 4 * t
                xb, xk = xsb[cnt["xs"] % 2]
                cnt["xs"] += 1
                ACT(sg[0][0:R, 0:512], oa[0:R, t, 0:512], AF.Square, [("oa", t, 0)], [("sg", 0), ("ssa", t)], accum_out=stat[0:R, s0:s0 + 1])
                ACT(sg[1][0:R, 0:512], oa[0:R, t, 512:1024], AF.Square, [("oa", t, 1)], [("sg", 1), ("ssb", t)],
                    accum_out=stat[0:R, s0 + 2:s0 + 3])
                rstd_from_ss(s0, 512, R, ("ssa", t), "rsa%d" % t)
                rstd_from_ss(s0 + 2, 512, R, ("ssb", t), "rsb%d" % t)
                TS(xb[0:R, 0:512], oa[0:R, t, 0:512], stat[0:R, s0 + 1:s0 + 2], ALU.mult, [("oa", t, 0), "rsa%d" % t], xk)
                TS(xb[0:R, 512:1024], oa[0:R, t, 512:1024], stat[0:R, s0 + 3:s0 + 4], ALU.mult, [("oa", t, 1), "rsb%d" % t], xk)
                to_featT(xb, R, t, 0, actT, "actT", 8, xk, gmix)
            W0, wk0 = wnext(SL_O0)
            W1, wk1 = wnext(SL_O1, hold=1)
            Wvs = [(W0[:, 0:4096].rearrange("p (k n) -> p k n", k=8), wk0), (W1[:, 0:4096].rearrange("p (k n) -> p k n", k=8), wk1)]

            def wout(t):
                for half in range(2):
                    Wv, wk = Wvs[half]
                    b = bank("mm")
                    for c in range(8):
                        MM(psf(b, R), actT[:, c, t * R:(t + 1) * R], Wv[:, c, :], c == 0, c == 7, [("actT", t), wk], [("ps", b)])
                    xv = x1[0:R, t, half * 512:(half + 1) * 512]
                    TT(xv, psf(b, R), xv, ALU.add, [("ps", b), ("x1", t)], [("x1", t)])

            wout(0)
            for t in range(T):
                if t + 1 < T:
                    wout(t + 1)
                norm_to_featT(cx, t, x1[0:R, t, :], junk[0:R, :], JK, D, gffn, [("x1", t)], 2 * t)

        def phase_F(cx, nxt=None):
            R, T, C = cx["R"], cx["T"], cx["T"] * cx["R"]
            akeys = [("actT", t) for t in range(T)]
            if nxt is not None:
                for t in range(nxt["T"]):
                    sa, sk = xstage(t, nxt["R"])
                    P.dma("pool", sa, nxt["x_src"](t), writes=sk)
            for g in range(2):
                for gi in range(6):
                    W, wk = wnext(SL_G0 + 6 * g + gi)
                    Wv = W[:, 0:4096].rearrange("p (k n) -> p k n", k=8)
                    for ci, fc in enumerate(G_CHUNKS[6 * g + gi]):
                        lc = fc - 11 * g
                        bg = bank("mm")
                        bu = bank("mm")
                        for c in range(8):
                            MM(psf(bg)[:, 0:C], Wv[:, c, ci * 128:ci * 128 + 128], actT[:, c, 0:C], c == 0, c == 7, akeys + [wk], [("ps", bg)])
                        for c in range(8):
                            MM(psf(bu)[:, 0:C], Wv[:, c, 256 + ci * 128:256 + ci * 128 + 128], actT[:, c, 0:C], c == 0, c == 7, akeys + [wk], [("ps", bu)])
                        i2 = cnt["blk"] % 2
                        cnt["blk"] += 1
                        ACT(sg[i2][:, 0:C], psf(bg)[:, 0:C], AF.Silu, [("ps", bg)], [("sg", i2)])
                        TT(aT[:, lc, 0:C], psf(bu)[:, 0:C], sg[i2][:, 0:C], ALU.mult, [("ps", bu), ("sg", i2)], [("aT", lc)])
                if g == 1 and nxt is not None:
                    nxt["xdone"] = set()
                    for t in range(nxt["T"]):
                        sa, sk = xstage(t, nxt["R"])
                        norm_to_featT(nxt, t, sa, junk[0:nxt["R"], :], JK, D, gat, sk, 48 + 2 * t, kp="p")
                        nxt["xdone"].add(t)
                nacc = 2 * T
                for di in range(3):
                    W, wk = wnext(SL_D0 + 3 * g + di)
                    chunks = D_CHUNKS[3 * g + di]
                    Wv = W[:, 0:len(chunks) * 1024].rearrange("p (c n) -> p c n", c=len(chunks))
                    for t in range(T):
                        for half in range(2):
                            b = t * 2 + half
                            for ci, fc in enumerate(chunks):
                                lc = fc - 11 * g
                                MM(psf(b, R), aT[:, lc, t * R:(t + 1) * R], Wv[:, ci, half * 512:(half + 1) * 512],
                                   (di == 0 and ci == 0), (di == 2 and ci == len(chunks) - 1), [("aT", lc), wk], [("ps", b)])
                for t in range(T):
                    for half in range(2):
                        b = t * 2 + half
                        xv = x1[0:R, t, half * 512:(half + 1) * 512]
                        TT(xv, psf(b, R), xv, ALU.add, [("ps", b), ("x1", t)], [("x1", t)])

        def phase_Y(cx):
            R, T = cx["R"], cx["T"]
            for t in range(T):
                sc = 2 * t
                ACT(junk[0:R, :], x1[0:R, t, :], AF.Square, [("x1", t)], JK + [("ssx", t)], accum_out=stat[0:R, sc:sc + 1])
                rstd_from_ss(sc, D, R, ("ssx", t), "rsx%d" % t)
                P.op("dve", lambda e, t=t, sc=sc: e.scalar_tensor_tensor(
                    out=x1[0:R, t, :], in0=x1[0:R, t, :], scalar=stat[0:R, sc + 1:sc + 2], in1=gfin_bc[0:R, :],
                    op0=ALU.mult, op1=ALU.mult), [("x1", t), "rsx%d" % t, "gfin_bc"], [("x1", t)])
                P.dma("sp", cx["y_dst"](t), x1[0:R, t, :], reads=[("x1", t)])

        passes = []
        for sq in range(NSEQ):
            for st_i in range(NST):
                r0 = st_i * 512
                slot = st_i % 2
                cx = dict(
                    st_i=st_i,
                    R=128, T=4, last=(st_i == NST - 1),
                    x_src=lambda t, sq=sq, r0=r0: x_p.ap()[sq, r0 + t * 128:r0 + (t + 1) * 128, :],
                    o_ckv=lambda t, sq=sq, r0=r0: o_ckv_p.ap()[sq, r0 + t * 128:r0 + (t + 1) * 128, :],
                    o_kr=lambda t, sq=sq, r0=r0: o_kr_p.ap()[sq, r0 + t * 128:r0 + (t + 1) * 128, :],
                    o_bk=lambda t, sq=sq: o_bk_p.ap()[sq, t * 128:(t + 1) * 128, :],
                    o_bv=lambda t, sq=sq: o_bv_p.ap()[sq, t * 128:(t + 1) * 128, :],
                    y_dst=lambda t, sq=sq, r0=r0: y_p.ap()[sq, r0 + t * 128:r0 + (t + 1) * 128, :],
                    pos_tile=lambda t, st_i=st_i: st_i * 4 + t,
                    vb_dst=lambda t, slot=slot: (Vb[:, slot * 4 + t, :, 0:64], ("Vb", slot * 4 + t)),
                    vm_dst=lambda t, st_i=st_i: (Vm[:, st_i * 4 + t, :, 0:64], ("Vm", st_i * 4 + t)),
                    kbT_dst=lambda j, slot=slot: (KbT[:, j, slot * 512:(slot + 1) * 512], ("KbT", j, slot)),
                    ktr_dst=lambda h, r0=r0, st_i=st_i: (KT[64:96, h, r0:r0 + 512], ("KTr", h, st_i)),
                    ktn_dst=lambda h, r0=r0, st_i=st_i: (KT[0:64, h, r0:r0 + 512], ("KTn", h, st_i)),
                )
                passes.append(cx)
        for pi_, cx in enumerate(passes):
            phase_P(cx)
            if not toep_state["done"]:
                load_toeplitz()
                toep_state["done"] = True
            phase_M(cx, cx["st_i"])
            phase_B(cx, cx["st_i"])
            phase_O(cx)
            phase_F(cx, passes[pi_ + 1] if pi_ + 1 < len(passes) else None)
            phase_Y(cx)

        if NSS:
            P.barrier()
            apos[0] = 0
            R = 16
            ckvT = carve(2 * PAST).rearrange("p (c s) -> p c s", c=2)
            Vs_h = carve(NKT * 65).rearrange("p (t d) -> p t d", t=NKT)
            KTs = carve(PAST)[0:96, :]
            stg_f = carve(2048, F32)
            stg_b = carve(2048)
            krp_s = carve(NKT * 96).rearrange("p (t d) -> p t d", t=NKT)
            wukv = carve(2048)
            KTnew = carve(8 * 32)[0:96, :].rearrange("p (h s) -> p h s", h=8)
            Vnew = carve(NSS * 520).rearrange("p (t h d) -> p t h d", t=NSS, h=8)
            KbTnew = carve(4 * 32).rearrange("p (j s) -> p j s", j=4)
            Vbnew = carve(NSS * 520).rearrange("p (t h d) -> p t h d", t=NSS, h=8)
            KbTs = carve(4 * 512).rearrange("p (j s) -> p j s", j=4)
            Vbs = carve(4 * 520).rearrange("p (t h d) -> p t h d", t=4, h=8)
            assert apos[0] <= persist_end, (apos[0], persist_end)
            Wuk_s = wukv[:, 0:1024].rearrange("p (k n) -> p k n", k=2)
            Wuv_s = wukv[:, 1024:2048].rearrange("p (k n) -> p k n", k=2)
            if not toep_state["done"]:
                load_toeplitz()
                toep_state["done"] = True
            cast_slot(SL_S)
            P.dma("sp", wukv[:, 0:2048], wsc.ap()[SL_S, :, S_UK:S_UK + 2048], reads=[("wsc", SL_S)], writes=["wukv"])
            MEMSET(Vs_h[:].rearrange("p t d -> p (t d)"), 1.0, ["Vs_h"])
            MEMSET(Vnew[:].rearrange("p t h d -> p (t h d)"), 1.0, [("Vnew", t) for t in range(NSS)])
            MEMSET(Vbnew[:].rearrange("p t h d -> p (t h d)"), 1.0, [("Vbnew", t) for t in range(NSS)])
            MEMSET(Vbs[:].rearrange("p t h d -> p (t h d)"), 1.0, ["Vbs"])
            MEMSET(krp_s[:].rearrange("p t d -> p (t d)"), 0.0, ["krp_s"])
            cx = dict(
                R=16, T=NSS, last=True,
                x_src=lambda t: x_s.ap()[t * 16:(t + 1) * 16, :],
                o_ckv=lambda t: o_ckv_s.ap()[t * 16:(t + 1) * 16, :],
                o_kr=lambda t: o_kr_s.ap()[t * 16:(t + 1) * 16, :],
                o_bk=lambda t: o_bk_s.ap()[t * 16:(t + 1) * 16, :],
                o_bv=lambda t: o_bv_s.ap()[t * 16:(t + 1) * 16, :],
                y_dst=lambda t: y_s.ap()[t * 16:(t + 1) * 16, :],
                pos_tile=lambda t: NTT - 1,
                vb_dst=lambda t: (Vbnew[0:16, t, :, 0:64], ("Vbnew", t)),
                vm_dst=lambda t: (Vnew[0:16, t, :, 0:64], ("Vnew", t)),
                kbT_dst=lambda j: (KbTnew[:, j, 0:16 * NSS], ("KbTnew", j)),
                ktr_dst=lambda h: (KTnew[64:96, h, 0:16 * NSS], ("KTnew_r", h)),
                ktn_dst=lambda h: (KTnew[0:64, h, 0:16 * NSS], ("KTnew_n", h)),
            )
            phase_P(cx)
            NQ = 16
            KCH = min(8, NKT)
            for bsq in range(NSS):
                qc = slice(bsq * 16, (bsq + 1) * 16)
                for k0 in range(0, NKT, KCH):
                    sf = stg_f[:, 0:KCH * 256].rearrange("p (t c) -> p t c", t=KCH)
                    sbv = stg_b[:, 0:KCH * 256].rearrange("p (t c) -> p t c", t=KCH)
                    P.dma("sp", sf, c_ckv.ap()[bsq, k0 * 128:(k0 + KCH) * 128, :].rearrange("(t p) c -> p t c", p=128),
                          writes=["stg_f"])
                    CP(sbv, sf, ["stg_f"], ["stg_b"])
                    for k4 in range(0, KCH, 4):
                        b = bank("misc")
                        pv = psb(b)
                        for kk in range(4):
                            for c in range(2):
                                TR(pv[:, (kk * 2 + c) * 128:(kk * 2 + c + 1) * 128], sbv[:, k4 + kk, c * 128:(c + 1) * 128], ident[:],
                                   ["stg_b", "ident"], [("ps", b)])
                        pv4 = pv.rearrange("p (k c n) -> p k c n", k=4, c=2)
                        for c in range(2):
                            dst = ckvT[:, c, (k0 + k4) * 128:(k0 + k4 + 4) * 128].rearrange("p (k n) -> p k n", k=4)
                            CP(dst, pv4[:, :, c, :], [("ps", b)], [("ckvT", (k0 + k4) // 4)], eng=("act" if c else "dve"))
                krf = stg_f[:, 0:NKT * 32].rearrange("p (t c) -> p t c", t=NKT)
                P.dma("sp", krf, c_kr.ap()[bsq, :, :].rearrange("(t p) c -> p t c", p=128), writes=["stg_f"])
                CP(krp_s[:, :, 64:96], krf, ["stg_f"], ["krp_s"])
                for k8 in range(0, NKT, 8):
                    nk = min(8, NKT - k8)
                    b = bank("misc")
                    for kk in range(nk):
                        TR(psb(b)[0:96, kk * 128:(kk + 1) * 128], krp_s[:, k8 + kk, :], ident[:], ["krp_s", "ident"], [("ps", b)])
                    CP(KTs[64:96, k8 * 128:(k8 + nk) * 128], psb(b)[64:96, 0:nk * 128], [("ps", b)], ["KTs_r"])
                sf = stg_f[:, 0:2048].rearrange("p (t c) -> p t c", t=4)
                sbv = stg_b[:, 0:2048].rearrange("p (t c) -> p t c", t=4)
                P.dma("sp", sf, c_bk.ap()[bsq, :, :].rearrange("(t p) c -> p t c", p=128), writes=["stg_f"])
                CP(sbv, sf, ["stg_f"], ["stg_b"])
                for j2 in range(0, 4, 2):
                    b = bank("misc")
                    for jj in range(2):
                        for m in range(4):
                            TR(psb(b)[:, (jj * 4 + m) * 128:(jj * 4 + m + 1) * 128], sbv[:, m, (j2 + jj) * 128:(j2 + jj + 1) * 128], ident[:],
                               ["stg_b", "ident"], [("ps", b)])
                    CP(KbTs[:, j2:j2 + 2, :], psb(b).rearrange("p (j s) -> p j s", j=2), [("ps", b)], ["KbTs"])
                P.dma("sp", sf, c_bv.ap()[bsq, :, :].rearrange("(t p) c -> p t c", p=128), writes=["stg_f"])
                for m in range(4):
                    CP(Vbs[:, m, :, 0:64], sf[:, m, :].rearrange("p (h d) -> p h d", h=8), ["stg_f"], ["Vbs"], eng=("pool" if m % 2 else "dve"))
                for h in range(8):
                    hp, hj = (h % 2) * 64, h // 2
                    sb_ = bank("sc")
                    for m in range(4):
                        MM(psf(sb_)[:, m * 16:(m + 1) * 16], KbTs[hp:hp + 64, hj, m * 128:(m + 1) * 128], QbT[hp:hp + 64, hj, qc], True, True,
                           ["KbTs", ("QbT", hj)], [("ps", sb_)])
                    MM(psf(sb_, 16)[:, 64:80], KbTnew[hp:hp + 64, hj, qc], QbT[hp:hp + 64, hj, qc], True, True,
                       [("KbTnew", hj), ("QbT", hj)], [("ps", sb_)])
                    pi = cnt["blk"] % 4
                    cnt["blk"] += 1
                    si = cnt["blk"] % 2
                    ACT(PT[pi][:, 0:32], psf(sb_)[:, 0:32], AF.Exp, [("ps", sb_), "cbias"], [("PT", pi)], bias=cbias[:, h:h + 1])
                    TT(stmp[si][:, 0:16], psf(sb_)[:, 32:48], toep[:, h, 256:272], ALU.add, [("ps", sb_), "toep"], [("stmp", si)])
                    TT(stmp[si][:, 16:32], psf(sb_)[:, 48:64], toep[:, h, 128:144], ALU.add, [("ps", sb_), "toep"], [("stmp", si)])
                    TT(stmp[si][0:16, 32:48], psf(sb_, 16)[:, 64:80], toep[0:16, h, 0:16], ALU.add, [("ps", sb_), "toep"], [("stmp", si)])
                    ACT(PT[pi][:, 32:64], stmp[si][:, 0:32], AF.Exp, [("stmp", si)], [("PT", pi)])
                    ACT(PT[pi][0:16, 64:80], stmp[si][0:16, 32:48], AF.Exp, [("stmp", si)], [("PT", pi)])
                    accb = bank("acc")
                    for m in range(4):
                        MM(psf(accb, 65)[:, qc], Vbs[:, m, h, :], PT[pi][:, m * 16:(m + 1) * 16], m == 0, False,
                           ["Vbs", ("PT", pi)], [("ps", accb)], sg=True)
                    MM(psf(accb, 65)[:, qc], Vbnew[0:16, bsq, h, :], PT[pi][0:16, 64:80], False, True,
                       [("Vbnew", bsq), ("PT", pi)], [("ps", accb)], sg=True)
                    finalize_head(cx, accb, h, 1, tiles=[bsq])
                NSC = (NKT * 16 + 511) // 512
                for h in range(8):
                    for k0 in range(0, PAST, 512):
                        b = bank("mm")
                        for c in range(2):
                            MM(psf(b, 64), Wuk_s[:, c, h * 64:(h + 1) * 64], ckvT[:, c, k0:k0 + 512], c == 0, c == 1,
                               ["wukv", ("ckvT", k0 // 512)], [("ps", b)])
                        CP(KTs[0:64, k0:k0 + 512], psf(b, 64), [("ps", b)], ["KTs_n"], eng=("act" if (k0 // 512) % 2 else "dve"))
                    for k8 in range(0, NKT, 8):
                        nk = min(8, NKT - k8)
                        b = bank("mm")
                        for kk in range(nk):
                            for c in range(2):
                                MM(psf(b)[:, kk * 64:(kk + 1) * 64], ckvT[:, c, (k8 + kk) * 128:(k8 + kk + 1) * 128], Wuv_s[:, c, h * 64:(h + 1) * 64],
                                   c == 0, c == 1, ["wukv", ("ckvT", (k8 + kk) // 4)], [("ps", b)])
                        CP(Vs_h[:, k8:k8 + nk, 0:64], psf(b)[:, 0:nk * 64].rearrange("p (t d) -> p t d", t=nk), [("ps", b)], ["Vs_h"],
                           eng=("act" if (k8 // 8) % 2 else "dve"))
                    pts = []
                    for s_ in range(NSC):
                        sb_ = bank("sc")
                        kts = list(range(s_ * 32, min(NKT, (s_ + 1) * 32)))
                        for i_, kt in enumerate(kts):
                            MM(psf(sb_)[:, i_ * 16:(i_ + 1) * 16], KTs[:, kt * 128:(kt + 1) * 128], QT[0:96, h, qc], True, True,
                               ["KTs_n", "KTs_r"] + [("QT", t) for t in range(NSS)], [("ps", sb_)])
                        pi = cnt["blk"] % 4
                        cnt["blk"] += 1
                        ACT(PT[pi][:, 0:len(kts) * 16], psf(sb_)[:, 0:len(kts) * 16], AF.Exp, [("ps", sb_)], [("PT", pi)])
                        pts.append((pi, kts))
                    sb_ = bank("sc")
                    MM(psf(sb_, 16)[:, 0:16], KTnew[:, h, qc], QT[0:96, h, qc], True, True,
                       [("KTnew_n", h), ("KTnew_r", h)] + [("QT", t) for t in range(NSS)], [("ps", sb_)])
                    si = cnt["blk"] % 2
                    cnt["blk"] += 1
                    ACT(sg[si][0:16, 0:16], psf(sb_, 16)[:, 0:16], AF.Exp, [("ps", sb_)], [("sg", si)])
                    accb = bank("acc")
                    first = True
                    for pi, kts in pts:
                        for i_, kt in enumerate(kts):
                            MM(psf(accb, 65)[:, qc], Vs_h[:, kt, :], PT[pi][:, i_ * 16:(i_ + 1) * 16], first, False,
                               ["Vs_h", ("PT", pi)], [("ps", accb)], sg=True)
                            first = False
                    MM(psf(accb, 65)[:, qc], Vnew[0:16, bsq, h, :], sg[si][0:16, 0:16], False, True,
                       [("Vnew", bsq), ("sg", si)], [("ps", accb)], sg=True)
                    finalize_head(cx, accb, h, 0, tiles=[bsq])
            phase_O(cx)
            phase_F(cx)
            phase_Y(cx)

        P.finish()
    return nc


def core_inputs(inp, core, nseq, nss):
    f = lambda a: np.ascontiguousarray(a, dtype=np.float32)
    ps, ss = slice(core * nseq, (core + 1) * nseq), slice(core * nss, (core + 1) * nss)
    return dict(
        x_prompt=f(inp["x_prompt"][ps]),
        x_sample=f(inp["x_sample"][ss].reshape(nss * 16, 1024)),
        cache_mla_ckv=f(inp["cache_mla_ckv"][0, ss]),
        cache_mla_krope=f(inp["cache_mla_krope"][0, ss]),
        cache_band_k=f(inp["cache_band_k"][0, ss].reshape(nss, 512, 512)),
        cache_band_v=f(inp["cache_band_v"][0, ss].reshape(nss, 512, 512)),
        w_in=f(inp["w_in"][0]), g_attn=f(inp["g_attn"][0]), g_q=f(inp["g_q"][0]), w_uq=f(inp["w_uq"][0]),
        g_kv=f(inp["g_kv"][0]), w_uk=f(inp["w_uk"][0].reshape(256, 512)), w_uv=f(inp["w_uv"][0].reshape(256, 512)),
        rel_bias=f(inp["rel_bias"][0]), g_out_a=f(inp["g_out_a"][0]), g_out_b=f(inp["g_out_b"][0]),
        w_out=f(inp["w_out"][0]), g_ffn=f(inp["g_ffn"][0]), w_gate=f(inp["w_gate"][0]), w_up=f(inp["w_up"][0]),
        w_down=f(inp["w_down"][0]), g_final=f(inp["g_final"]),
    )


_NC_CACHE = {}


def kernel(**inputs):
    n_cores = 8
    nb, S = inputs["x_prompt"].shape[0], inputs["x_prompt"].shape[1]
    nsb = inputs["x_sample"].shape[0]
    past = inputs["cache_mla_ckv"].shape[2]
    nseq, nss = nb // n_cores, nsb // n_cores
    key = (nseq, S, nss, past)
    if key not in _NC_CACHE:
        _NC_CACHE[key] = build_program(nseq, S, nss, past)
    nc = _NC_CACHE[key]
    inp = {k: np.asarray(v) for k, v in inputs.items()}
    in_maps = [core_inputs(inp, c, nseq, nss) for c in range(n_cores)]
    res = run_bass_kernel_spmd(nc, in_maps, core_ids=list(range(n_cores)))
    r = res.results
    cat = lambda name: np.concatenate([np.asarray(r[c][name]) for c in range(n_cores)], axis=0)
    y_p = cat("y_prompt")
    y_s = cat("y_sample").reshape(nsb, 16, 1024)
    return (
        y_p, y_s,
        cat("new_ckv_prompt")[None], cat("new_kr_prompt")[None],
        cat("new_bk_prompt").reshape(nb, 512, 8, 64)[None], cat("new_bv_prompt").reshape(nb, 512, 8, 64)[None],
        cat("new_ckv_sample").reshape(nsb, 16, 256)[None], cat("new_kr_sample").reshape(nsb, 16, 32)[None],
        cat("new_bk_sample").reshape(nsb, 16, 8, 64)[None], cat("new_bv_sample").reshape(nsb, 16, 8, 64)[None],
    )
```

```python
import contextlib
import math
import numpy as np
import concourse.bass as bass
import concourse.mybir as mybir
from concourse.bass_utils import run_bass_kernel_spmd

F32 = mybir.dt.float32
BF16 = mybir.dt.bfloat16
I32 = mybir.dt.int32
AF = mybir.ActivationFunctionType
ALU = mybir.AluOpType
AX = mybir.AxisListType


class Op:
    __slots__ = ("eng", "fn", "idx", "deps", "signal", "sig", "waits", "is_dma",
                 "dsem", "dval", "dwaits")

    def __init__(self, eng, fn, is_dma):
        self.eng = eng
        self.fn = fn
        self.is_dma = is_dma
        self.signal = False
        self.sig = 0
        self.waits = []
        self.dwaits = []
        self.deps = ()
        self.dsem = None
        self.dval = 0


class TileState:
    __slots__ = ("w", "r", "rd")

    def __init__(self):
        self.w = None
        self.r = {}
        self.rd = []


class Prog:
    ENGS = ("pe", "act", "dve", "pool", "sp")
    NDS = 20

    def __init__(self, nc, stack):
        self.nc = nc
        self.stack = stack
        self.eng_ops = {e: [] for e in self.ENGS}
        self.tiles = {}
        self.esem = {e: stack.enter_context(nc.semaphore("s_" + e)) for e in self.ENGS}
        self.nds = {"sp": 20, "pool": 44, "act": 1}
        self.dsems = {q: [stack.enter_context(nc.semaphore("d_%s%d" % (q, i)))
                          for i in range(self.nds[q])] for q in ("sp", "pool", "act")}
        self.dcount = {"sp": 0, "pool": 0, "act": 0}
        self.all_ops = []

    def _ts(self, key):
        t = self.tiles.get(key)
        if t is None:
            t = TileState()
            self.tiles[key] = t
        return t

    def op(self, eng, fn, reads=(), writes=(), dma=False):
        o = Op(eng, fn, dma)
        deps = []
        for k in reads:
            t = self._ts(k)
            if t.w is not None:
                deps.append(t.w)
            if isinstance(k, tuple) and k[0] == "ps":
                deps.extend(o2 for e2, o2 in t.r.items() if e2 != eng)
        for k in writes:
            t = self._ts(k)
            if t.w is not None:
                deps.append(t.w)
            deps.extend(t.r.values())
            deps.extend(t.rd)
        for k in reads:
            t = self._ts(k)
            if dma:
                t.rd.append(o)
            else:
                t.r[eng] = o
        for k in writes:
            t = self._ts(k)
            t.w = o
            t.r = {}
            t.rd = []
        o.deps = deps
        o.idx = len(self.eng_ops[eng])
        self.eng_ops[eng].append(o)
        self.all_ops.append(o)
        if dma:
            n = self.dcount[eng]
            self.dcount[eng] = n + 1
            o.dsem = self.dsems[eng][n % self.nds[eng]]
            o.dval = 16 * (n // self.nds[eng] + 1)
        return o

    def barrier(self):
        last = []
        for e in self.ENGS:
            for o in reversed(self.eng_ops[e]):
                if o.fn is not None and not o.is_dma:
                    last.append(o)
                    break
        pend = [o for o in self.all_ops if o.is_dma]
        for e in self.ENGS:
            o = Op(e, None, False)
            o.deps = [d for d in last if not d.is_dma] + pend
            o.idx = len(self.eng_ops[e])
            self.eng_ops[e].append(o)
            self.all_ops.append(o)
        self.tiles = {}

    def dma(self, q, out, in_, reads=(), writes=(), **kw):
        return self.op(q, lambda e: e.dma_start(out=out, in_=in_, **kw), reads, writes, dma=True)

    def resolve(self):
        waited = {e: {} for e in self.ENGS}
        dwaited = {e: {} for e in self.ENGS}
        for o in self.all_ops:
            E = o.eng
            need = {}
            dneed = {}
            for d in o.deps:
                if d is o:
                    continue
                if d.is_dma:
                    k = id(d.dsem)
                    if dneed.get(k, (None, 0))[1] < d.dval:
                        dneed[k] = (d.dsem, d.dval)
                else:
                    if d.eng == "pe" and E == "pe":
                        continue
                    c = need.get(d.eng)
                    if c is None or d.idx > c.idx:
                        need[d.eng] = d
            if o.is_dma and o.dval > 16:
                k = id(o.dsem)
                if dneed.get(k, (None, 0))[1] < o.dval - 16:
                    dneed[k] = (o.dsem, o.dval - 16)
            for P, d in need.items():
                if waited[E].get(P, -1) >= d.idx:
                    continue
                waited[E][P] = d.idx
                d.signal = True
                o.waits.append(d)
            for k, (sem, val) in dneed.items():
                if dwaited[E].get(k, 0) >= val:
                    continue
                dwaited[E][k] = val
                o.dwaits.append((sem, val))
        for e in self.ENGS:
            c = 0
            for o in self.eng_ops[e]:
                if o.is_dma:
                    continue
                if o.signal:
                    c += 1
                    o.sig = c

    def _emit_eng(self, name, e):
        for o in self.eng_ops[name]:
            for d in o.waits:
                e.wait_ge(self.esem[d.eng], d.sig)
            for sem, val in o.dwaits:
                e.wait_ge(sem, val)
            if o.fn is None:
                continue
            ins = o.fn(e)
            if o.is_dma:
                ins.then_inc(o.dsem, 16)
            elif o.signal:
                ins.then_inc(self.esem[name], 1)

    def finish(self):
        self.resolve()
        nc = self.nc
        with nc.Block() as block:
            @block.tensor
            def _(e):
                self._emit_eng("pe", e)

            @block.scalar
            def _(e):
                self._emit_eng("act", e)

            @block.vector
            def _(e):
                self._emit_eng("dve", e)

            @block.gpsimd
            def _(e):
                self._emit_eng("pool", e)
                for q in ("pool",):
                    n = self.dcount[q]
                    for i in range(min(n, self.nds[q])):
                        cnt = (n - 1 - i) // self.nds[q] + 1
                        e.wait_ge(self.dsems[q][i], 16 * cnt)

            @block.sync
            def _(e):
                self._emit_eng("sp", e)
                for q in ("sp",):
                    n = self.dcount[q]
                    for i in range(min(n, self.nds[q])):
                        cnt = (n - 1 - i) // self.nds[q] + 1
                        e.wait_ge(self.dsems[q][i], 16 * cnt)


D = 1024
DFF = 2816
NFC = 22
EPS = 1e-6
MLA_SCALE = 96 ** -0.5
PAST_FULL = 4096
SL_A, SL_V, SL_KB, SL_QB, SL_S, SL_O0, SL_O1 = 0, 1, 2, 3, 4, 5, 6
SL_G0 = 7
SL_D0 = 19
NSLOT = 25
S_KR, S_UQ, S_UK, S_UV = 0, 256, 1792, 2816
G_CHUNKS = [[0, 1], [2, 3], [4, 5], [6, 7], [8, 9], [10], [11, 12], [13, 14], [15, 16], [17, 18], [19, 20], [21]]
D_CHUNKS = [[0, 1, 2, 3], [4, 5, 6, 7], [8, 9, 10], [11, 12, 13, 14], [15, 16, 17, 18], [19, 20, 21]]
BAND_T = [((0, 0), 1, 1), ((0, 2), 3, 1), ((0, 4), 5, 1), ((0, 6), 7, 1),
          ((1, 7), 0, 0), ((3, 7), 2, 0), ((5, 7), 4, 0), ((7, 7), 6, 0)]


def build_program(NSEQ, S, NSS, PAST):
    nc = bass.Bass("TRN2", target_bir_lowering=False)
    NT = S // 128
    NST = S // 512
    NKT = PAST // 128
    dt_in = lambda name, shape: nc.dram_tensor(name, shape, F32, kind="ExternalInput")
    dt_out = lambda name, shape: nc.dram_tensor(name, shape, F32, kind="ExternalOutput")
    NSEQd, NSSd = max(NSEQ, 1), max(NSS, 1)
    x_p = dt_in("x_prompt", [NSEQd, S, D])
    x_s = dt_in("x_sample", [NSSd * 16, D])
    c_ckv = dt_in("cache_mla_ckv", [NSSd, PAST, 256])
    c_kr = dt_in("cache_mla_krope", [NSSd, PAST, 32])
    c_bk = dt_in("cache_band_k", [NSSd, 512, 512])
    c_bv = dt_in("cache_band_v", [NSSd, 512, 512])
    w_in = dt_in("w_in", [D, 2080])
    g_attn = dt_in("g_attn", [D])
    g_q = dt_in("g_q", [256])
    w_uq = dt_in("w_uq", [256, 768])
    g_kv = dt_in("g_kv", [256])
    w_uk = dt_in("w_uk", [256, 512])
    w_uv = dt_in("w_uv", [256, 512])
    rel_bias = dt_in("rel_bias", [8, 513])
    g_out_a = dt_in("g_out_a", [512])
    g_out_b = dt_in("g_out_b", [512])
    w_out = dt_in("w_out", [D, D])
    g_ffn = dt_in("g_ffn", [D])
    w_gate = dt_in("w_gate", [D, DFF])
    w_up = dt_in("w_up", [D, DFF])
    w_down = dt_in("w_down", [DFF, D])
    g_final = dt_in("g_final", [D])
    y_p = dt_out("y_prompt", [NSEQd, S, D])
    y_s = dt_out("y_sample", [NSSd * 16, D])
    o_ckv_p = dt_out("new_ckv_prompt", [NSEQd, S, 256])
    o_kr_p = dt_out("new_kr_prompt", [NSEQd, S, 32])
    o_bk_p = dt_out("new_bk_prompt", [NSEQd, 512, 512])
    o_bv_p = dt_out("new_bv_prompt", [NSEQd, 512, 512])
    o_ckv_s = dt_out("new_ckv_sample", [NSSd * 16, 256])
    o_kr_s = dt_out("new_kr_sample", [NSSd * 16, 32])
    o_bk_s = dt_out("new_bk_sample", [NSSd * 16, 512])
    o_bv_s = dt_out("new_bv_sample", [NSSd * 16, 512])
    wsc = nc.dram_tensor("wsc", [NSLOT, 128, 4096], BF16, kind="Internal")
    rbp = nc.dram_tensor("rbp", [8, 641], F32, kind="Internal")

    with contextlib.ExitStack() as st:
        P = Prog(nc, st)
        sbt = lambda name, shape, dt: st.enter_context(nc.sbuf_tensor(name, shape, dt))
        NTT = max(NT, 1) + 1

        ident = sbt("ident", [128, 128], BF16)
        identf = sbt("identf", [128, 128], F32)
        gat = sbt("gat", [128, 8], F32)
        gqT = sbt("gqT", [128, 2], F32)
        gmix = sbt("gmix", [128, 8], F32)
        gffn = sbt("gffn", [128, 8], F32)
        gkv_bc = sbt("gkv_bc", [128, 256], F32)
        gfin_bc = sbt("gfin_bc", [128, D], F32)
        cbias = sbt("cbias", [128, 8], F32)
        toep = sbt("toep", [128, 8, 384], F32)
        cs = sbt("cs", [128, NTT, 2, 16], F32)
        cst = sbt("cst", [128, 4], F32)
        stat = sbt("stat", [128, 64], F32)
        PS = st.enter_context(nc.psum_tensor("PS", [128, 8, 512], F32))

        def psf(b, rows=128):
            return PS[0:rows, b, :]

        def psb(b, rows=128):
            return PS[0:rows, b, :].bitcast(BF16)

        ARENA = 82336
        arena = sbt("arena", [128, ARENA], BF16)
        apos = [0]

        def carve(n_el, dt=BF16):
            n16 = n_el * (2 if dt == F32 else 1)
            n16 = (n16 + 15) // 16 * 16
            a = apos[0]
            apos[0] += n16
            assert apos[0] <= ARENA, ("arena overflow", apos[0])
            v = arena[:, a:a + n_el * (2 if dt == F32 else 1)]
            return v.bitcast(F32) if dt == F32 else v

        def ACT(out, in_, func, r, w, **kw):
            P.op("act", lambda e: e.activation(out=out, in_=in_, func=func, **kw), r, w)

        def TT(out, in0, in1, op, r, w, eng="dve"):
            P.op(eng, lambda e: e.tensor_tensor(out=out, in0=in0, in1=in1, op=op), r, w)

        def TS(out, in0, s1, op0, r, w, s2=None, op1=None, eng="dve"):
            if op1 is None:
                P.op(eng, lambda e: e.tensor_scalar(out=out, in0=in0, scalar1=s1, scalar2=None, op0=op0), r, w)
            else:
                P.op(eng, lambda e: e.tensor_scalar(out=out, in0=in0, scalar1=s1, scalar2=s2, op0=op0, op1=op1), r, w)

        def CP(out, in_, r, w, eng="dve"):
            if eng == "act":
                P.op("act", lambda e: e.copy(out=out, in_=in_), r, w)
            else:
                P.op(eng, lambda e: e.tensor_copy(out=out, in_=in_), r, w)

        def MM(out, lhsT, rhs, start, stop, r, w, sg=False):
            P.op("pe", lambda e: e.matmul(out, lhsT=lhsT, rhs=rhs, start=start, stop=stop, skip_group_check=sg), r, w)

        def TR(out, in_, idn, r, w):
            P.op("pe", lambda e: e.transpose(out=out, in_=in_, identity=idn), r, w)

        def MEMSET(ap, val, w, eng="pool"):
            P.op(eng, lambda e: e.memset(ap, val), (), w)

        rr = {"mm": 0, "sc": 0, "acc": 0, "misc": 0}
        POOLS = {"mm": [2, 3, 4, 5, 6, 7], "sc": [2, 3, 4, 5], "acc": [0, 1], "misc": [6, 7]}

        def bank(pool):
            lst = POOLS[pool]
            b = lst[rr[pool] % len(lst)]
            rr[pool] += 1
            return b

        MEMSET(identf[:], 0.0, ["identf"])
        P.op("pool", lambda e: e.affine_select(out=identf[:], in_=identf[:], pattern=[[-1, 128]],
                                                 compare_op=ALU.not_equal, fill=1.0, base=0,
                                                 channel_multiplier=1), ["identf"], ["identf"])
        CP(ident[:], identf[:], ["identf"], ["ident"])
        MEMSET(cst[:, 0:1], EPS, ["cst"])
        P.dma("sp", gat[:], g_attn.ap().rearrange("(c p) -> p c", p=128), writes=["gains"], allow_slow_non_contiguous=True)
        P.dma("sp", gffn[:], g_ffn.ap().rearrange("(c p) -> p c", p=128), writes=["gains"], allow_slow_non_contiguous=True)
        P.dma("sp", gqT[:], g_q.ap().rearrange("(c p) -> p c", p=128), writes=["gains"], allow_slow_non_contiguous=True)
        P.dma("sp", gmix[:, 0:4], g_out_a.ap().rearrange("(c p) -> p c", p=128), writes=["gains"], allow_slow_non_contiguous=True)
        P.dma("sp", gmix[:, 4:8], g_out_b.ap().rearrange("(c p) -> p c", p=128), writes=["gains"], allow_slow_non_contiguous=True)
        P.dma("sp", gkv_bc[:], bass.AP(g_kv, 0, [[0, 128], [1, 256]]), writes=["gkv_bc"])
        P.dma("sp", gfin_bc[:], bass.AP(g_final, 0, [[0, 128], [1, D]]), writes=["gfin_bc"])
        P.dma("sp", cbias[:].unsqueeze(2), bass.AP(rel_bias, 512, [[0, 128], [513, 8], [1, 1]]), writes=["cbias"], allow_slow_non_contiguous=True)
        P.dma("sp", rbp.ap()[:, 0:513], rel_bias.ap(), writes=["rbp"])
        P.dma("sp", rbp.ap()[:, 513:641].unsqueeze(2), bass.AP(rel_bias, 512, [[513, 8], [0, 128], [1, 1]]), writes=["rbp"], allow_slow_non_contiguous=True)
        def load_toeplitz():
            for k in range(128):
                P.dma("sp", toep[k:k + 1, :, :], rbp.ap()[:, 256 - k:640 - k].unsqueeze(0), reads=["rbp"], writes=["toep"])

        toep_state = {"done": False}
        posi = sbt("posi", [128, NTT], I32)
        posf = sbt("posf", [128, NTT], F32)
        inv = sbt("inv", [128, 16], F32)
        aT = carve(11 * 512).rearrange("p (c s) -> p c s", c=11)
        aTf = aT[:].rearrange("p c s -> p (c s)")
        ang = aTf[:, 0:2 * NTT * 16].bitcast(F32).rearrange("p (t j) -> p t j", t=NTT)
        angi = aTf[:, 1024:1024 + 2 * NTT * 16].bitcast(I32).rearrange("p (t j) -> p t j", t=NTT)
        angr = aTf[:, 2048:2048 + 2 * NTT * 16].bitcast(F32).rearrange("p (t j) -> p t j", t=NTT)
        AK = [("aT", i) for i in range(11)]
        P.op("pool", lambda e: e.iota(out=posi[:, 0:NTT - 1], pattern=[[128, NTT - 1]], base=0, channel_multiplier=1), (), ["posi"])
        P.op("pool", lambda e: e.iota(out=posi[:, NTT - 1:NTT], pattern=[[0, 1]], base=0, channel_multiplier=1), (), ["posi"])
        P.op("dve", lambda e: e.tensor_single_scalar(out=posi[:, NTT - 1:NTT], in_=posi[:, NTT - 1:NTT], scalar=15, op=ALU.bitwise_and),
             ["posi"], ["posi"])
        P.op("dve", lambda e: e.tensor_single_scalar(out=posi[:, NTT - 1:NTT], in_=posi[:, NTT - 1:NTT], scalar=PAST, op=ALU.add),
             ["posi"], ["posi"])
        CP(posf[:], posi[:], ["posi"], ["posf"])
        for j in range(16):
            MEMSET(inv[:, j:j + 1], float(np.float32(10000.0) ** np.float32(-(2 * j) / 32.0)), ["inv"])
        for t in range(NTT):
            TS(ang[:, t, :], inv[:], posf[:, t:t + 1], ALU.mult, ["inv", "posf"], ["ang"])
        angf = ang[:].rearrange("p t j -> p (t j)")
        angif = angi[:].rearrange("p t j -> p (t j)")
        angrf = angr[:].rearrange("p t j -> p (t j)")
        TS(angf, angf, 1.0 / (2 * math.pi), ALU.mult, ["ang"], ["ang"])
        CP(angif, angf, ["ang"], ["angi"])
        CP(angrf, angif, ["angi"], ["angr"])
        TT(angf, angf, angrf, ALU.subtract, ["ang", "angr"], ["ang"])
        for t in range(NTT):
            ACT(cs[:, t, 1, :], ang[:, t, :], AF.Sin, ["ang"], ["cs"], scale=2 * math.pi)
        TS(angf, angf, 0.25, ALU.add, ["ang"], ["ang"])
        TS(angrf, angf, 0.5, ALU.is_gt, ["ang"], ["angr"])
        TT(angf, angf, angrf, ALU.subtract, ["ang", "angr"], ["ang"])
        for t in range(NTT):
            ACT(cs[:, t, 0, :], ang[:, t, :], AF.Sin, ["ang"], ["cs"], scale=2 * math.pi)
        P.op("dve", lambda e: e.memset(aT[:, 0, 0:16], 0.0), ["ang", "angi", "angr"], AK + ["ang", "angi", "angr"])

        def wview(slot, off, kc, n):
            return wsc.ap()[slot, :, off:off + kc * n].rearrange("p (k n) -> p k n", k=kc)

        def cast_cols(slot, off, src, c0, n, kc):
            P.dma("pool", wview(slot, off, kc, n), src.ap()[:, c0:c0 + n].rearrange("(k p) n -> p k n", p=128),
                  writes=[("wsc", slot)])

        casted = set()

        def cast_slot(slot):
            if slot in casted:
                return
            casted.add(slot)
            if slot == SL_A:
                cast_cols(SL_A, 0, w_in, 0, 512, 8)
            elif slot == SL_V:
                cast_cols(SL_V, 0, w_in, 1568, 512, 8)
            elif slot == SL_KB:
                cast_cols(SL_KB, 0, w_in, 1056, 512, 8)
            elif slot == SL_QB:
                cast_cols(SL_QB, 0, w_in, 544, 512, 8)
            elif slot == SL_S:
                cast_cols(SL_S, S_KR, w_in, 512, 32, 8)
                cast_cols(SL_S, S_UQ, w_uq, 0, 768, 2)
                cast_cols(SL_S, S_UK, w_uk, 0, 512, 2)
                cast_cols(SL_S, S_UV, w_uv, 0, 512, 2)
            elif slot == SL_O0:
                cast_cols(SL_O0, 0, w_out, 0, 512, 8)
            elif slot == SL_O1:
                cast_cols(SL_O1, 0, w_out, 512, 512, 8)
            elif slot < SL_D0:
                gi = slot - SL_G0
                chunks = G_CHUNKS[gi]
                n = len(chunks)
                c0 = chunks[0] * 128
                dst = wsc.ap()[SL_G0 + gi, :, :].rearrange("p (k n) -> p k n", k=8)
                P.dma("pool", dst[:, :, 0:n * 128], w_gate.ap()[:, c0:c0 + n * 128].rearrange("(k p) n -> p k n", p=128),
                      writes=[("wsc", SL_G0 + gi)])
                P.dma("pool", dst[:, :, 256:256 + n * 128], w_up.ap()[:, c0:c0 + n * 128].rearrange("(k p) n -> p k n", p=128),
                      writes=[("wsc", SL_G0 + gi)])
            else:
                di = slot - SL_D0
                chunks = D_CHUNKS[di]
                f0 = chunks[0] * 128
                n = len(chunks)
                P.dma("pool", wsc.ap()[SL_D0 + di, :, 0:n * 1024].rearrange("p (c n) -> p c n", c=n),
                      w_down.ap()[f0:f0 + n * 128, :].rearrange("(c p) n -> p c n", p=128),
                      writes=[("wsc", SL_D0 + di)])

        NRING = 3
        ring = [sbt("ring%d" % i, [128, 4096], BF16) for i in range(NRING)]
        wq = {"seq": [], "loaded": 0, "used": 0}

        def slot_elems(slot):
            if slot == SL_S:
                return 3840
            if SL_G0 <= slot < SL_D0:
                return 8 * 512
            if slot >= SL_D0:
                return len(D_CHUNKS[slot - SL_D0]) * 1024
            return 4096

        def wload_upto(n):
            while wq["loaded"] < min(n, len(wq["seq"])):
                i = wq["loaded"]
                slot = wq["seq"][i]
                for j_ in range(i, min(i + 10, len(wq["seq"]), 25)):
                    cast_slot(wq["seq"][j_])
                ne = slot_elems(slot)
                if slot in (SL_G0 + 5, SL_G0 + 11):
                    P.dma("sp", ring[i % NRING][:, 0:4096].rearrange("p (k a n) -> p k a n", k=8, a=2)[:, :, :, 0:128],
                          wsc.ap()[slot, :, :].rearrange("p (k a n) -> p k a n", k=8, a=2)[:, :, :, 0:128],
                          reads=[("wsc", slot)], writes=[("wr", i % NRING)])
                else:
                    P.dma("sp", ring[i % NRING][:, 0:ne], wsc.ap()[slot, :, 0:ne],
                          reads=[("wsc", slot)], writes=[("wr", i % NRING)])
                wq["loaded"] += 1

        def wnext(slot, hold=0):
            i = wq["used"]
            assert wq["seq"][i] == slot, (i, wq["seq"][i], slot)
            wload_upto(i + NRING - hold)
            wq["used"] += 1
            return ring[i % NRING], ("wr", i % NRING)

        PASS_SLOTS = [SL_A, SL_V, SL_KB, SL_QB, SL_S, SL_O0, SL_O1]
        FFN_SLOTS = []
        for g in range(2):
            FFN_SLOTS += [SL_G0 + 6 * g + i for i in range(6)] + [SL_D0 + 3 * g + i for i in range(3)]
        n_pass = NSEQ * NST + (1 if NSS else 0)
        wq["seq"] = (PASS_SLOTS + FFN_SLOTS) * n_pass

        KT = carve(8 * S).rearrange("p (h s) -> p h s", h=8)
        VmF = carve(NT * 520 + 64)
        Vm = VmF[:, 0:NT * 520].rearrange("p (t h d) -> p t h d", t=NT, h=8)
        KbT = carve(4 * 1024).rearrange("p (j s) -> p j s", j=4)
        VbF = carve(8 * 520 + 64)
        Vb = VbF[:, 0:8 * 520].rearrange("p (t h d) -> p t h d", t=8, h=8)
        persist_end = apos[0]
        actT = carve(8 * 512).rearrange("p (c s) -> p c s", c=8)
        QT = carve(8 * 512).rearrange("p (h s) -> p h s", h=8)
        qb_off = apos[0]
        QbT = carve(4 * 512).rearrange("p (j s) -> p j s", j=4)
        cqnT = carve(2 * 512).rearrange("p (c s) -> p c s", c=2)
        ckvnT = carve(2 * 512).rearrange("p (c s) -> p c s", c=2)
        assert apos[0] == qb_off + 4096
        stgB = arena[:, qb_off:qb_off + 4096].bitcast(F32).rearrange("p (k d) -> p k d", k=2)
        xs = carve(1024)
        x1 = carve(4 * 1024, F32).rearrange("p (t d) -> p t d", t=4)
        oa = carve(4 * 1024).rearrange("p (t d) -> p t d", t=4)
        PTall = carve(2048)
        PT = [PTall[:, i * 512:(i + 1) * 512] for i in range(4)]
        xsb = [(xs, ["xs"]), (PTall[:, 0:1024], [("PT", 0), ("PT", 1)])]
        junk = PTall[:, 1024:2048]
        JK = [("PT", 2), ("PT", 3)]
        oaF = oa[:].rearrange("p t d -> p (t d)").bitcast(F32).rearrange("p (k d) -> p k d", k=2)
        okeys = lambda k: [("oa", 2 * k, 0), ("oa", 2 * k, 1), ("oa", 2 * k + 1, 0), ("oa", 2 * k + 1, 1)]

        def xstage(t, rows):
            if t < 2:
                return oaF[0:rows, t, :], okeys(t)
            if t == 2:
                return stgB[0:rows, 0, :], [("QbT", j) for j in range(4)]
            return stgB[0:rows, 1, :], [("cqnT", i) for i in range(4)] + [("ckvnT", i) for i in range(4)]
        stmp = [carve(512, F32) for _ in range(2)]
        oT = [carve(512, F32) for _ in range(2)]
        sg = [carve(512) for _ in range(2)]
        ostage = [carve(512, F32) for _ in range(2)]
        cq_bf = [carve(256) for _ in range(2)]
        ckv_f = [carve(256, F32) for _ in range(2)]
        ckv_bf = [carve(256) for _ in range(2)]
        kr_f = [carve(32, F32) for _ in range(2)]
        kr_t = [carve(64, F32) for _ in range(2)]
        krpad = [carve(96) for _ in range(2)]
        qf = [carve(768, F32).rearrange("p (h d) -> p h d", h=8) for _ in range(2)]
        q_bf = [carve(768).rearrange("p (h d) -> p h d", h=8) for _ in range(2)]
        q_t1 = carve(4 * 128, F32).rearrange("p (a h d) -> p a h d", a=4, h=8)
        q_t = [q_t1, q_t1]
        krT_sb = carve(512)
        prompt_end = apos[0]

        if NSEQ:
            MEMSET(VmF[:, :], 1.0, [("Vm", i) for i in range(NT)])
            MEMSET(KT[96:128, :, :].rearrange("p h s -> p (h s)"), 0.0, ["KTpad"])
        MEMSET(QT[96:128, :, :].rearrange("p h s -> p (h s)"), 0.0, ["QTpad"])
        MEMSET(VbF[:, :], 1.0, [("Vb", i) for i in range(8)])
        for i in range(2):
            MEMSET(krpad[i], 0.0, [("krpad", i)])

        cnt = {"tile": 0, "blk": 0, "ft": 0, "xs": 0, "st": 0}

        def rstd_from_ss(col, n, R, key_in, key_out):
            ACT(stat[0:R, col + 1:col + 2], stat[0:R, col:col + 1], AF.Ln, [key_in, "cst"], [key_out + "_ln"],
                scale=1.0 / n, bias=cst[0:R, 0:1])
            ACT(stat[0:R, col + 1:col + 2], stat[0:R, col + 1:col + 2], AF.Exp, [key_out + "_ln"], [key_out], scale=-0.5)

        def to_featT(src_bf, R, t, gcol, dstT, dkey, nchunk, rkeys, scale_ap):
            b = bank("misc")
            pv = psb(b)
            for c in range(nchunk):
                TR(pv[:, c * R:(c + 1) * R], src_bf[0:R, c * 128:(c + 1) * 128], ident[0:R, 0:R], rkeys + ["ident"], [("ps", b)])
            o = dstT[:, 0:nchunk, t * R:(t + 1) * R]
            i = pv[:, 0:nchunk * R].rearrange("p (c r) -> p c r", c=nchunk)
            if scale_ap is None:
                use_act = (cnt["ft"] % 2 == 1)
                cnt["ft"] += 1
                CP(o, i, [("ps", b)], [(dkey, t)], eng=("act" if use_act else "dve"))
            else:
                g = scale_ap[:, gcol:gcol + nchunk].unsqueeze(2).broadcast_to([128, nchunk, R])
                TT(o, i, g, ALU.mult, [("ps", b), "gains"], [(dkey, t)])

        def norm_to_featT(cx, t, src_f32, junk_ap, junk_keys, n, gains, rkeys, sc, kp="x"):
            R = cx["R"]
            xb, xk = xsb[cnt["xs"] % 2]
            cnt["xs"] += 1
            ACT(junk_ap, src_f32, AF.Square, rkeys, junk_keys + [("ss" + kp, t)], accum_out=stat[0:R, sc:sc + 1])
            rstd_from_ss(sc, n, R, ("ss" + kp, t), "rs%s%d" % (kp, t))
            TS(xb[0:R, :], src_f32, stat[0:R, sc + 1:sc + 2], ALU.mult, rkeys + ["rs%s%d" % (kp, t)], xk)
            to_featT(xb, R, t, 0, actT, "actT", 8, xk, gains)

        def phase_P(cx):
            R, T, C = cx["R"], cx["T"], cx["T"] * cx["R"]
            last = cx["last"]
            xdone = cx.get("xdone", set())
            for t in range(T):
                if t not in xdone:
                    P.dma("pool", x1[0:R, t, :], cx["x_src"](t), writes=[("x1", t)])
            W, wkA = wnext(SL_A)
            WvA = W[:, 0:4096].rearrange("p (k n) -> p k n", k=8)
            tinfo = []

            xn_done = set(xdone)

            def xnorm(t):
                if t not in xn_done:
                    xn_done.add(t)
                    norm_to_featT(cx, t, x1[0:R, t, :], junk[0:R, :], JK, D, gat, [("x1", t)], 2 * t)

            def A_tile(t):
                if t + 1 < T:
                    xnorm(t + 1)
                b = bank("mm")
                for c in range(8):
                    MM(psf(b, R), actT[:, c, t * R:(t + 1) * R], WvA[:, c, :], c == 0, c == 7, [("actT", t), wkA], [("ps", b)])
                i2 = cnt["tile"] % 2
                cnt["tile"] += 1
                tinfo.append(i2)
                if t == 0:
                    for tt in sorted(xdone):
                        sa, sk = xstage(tt, R)
                        CP(x1[0:R, tt, :], sa, sk, [("x1", tt)])
                s0 = 8 + 4 * t
                ACT(cq_bf[i2][0:R, :], psf(b, R)[:, 0:256], AF.Square, [("ps", b)], [("cq_bf", i2), ("ssq", t)],
                    accum_out=stat[0:R, s0:s0 + 1])
                ACT(ckv_bf[i2][0:R, :], psf(b, R)[:, 256:512], AF.Square, [("ps", b)], [("ckv_bf", i2), ("sskv", t)],
                    accum_out=stat[0:R, s0 + 2:s0 + 3])
                rstd_from_ss(s0, 256, R, ("ssq", t), "rsq%d" % t)
                rstd_from_ss(s0 + 2, 256, R, ("sskv", t), "rskv%d" % t)
                ACT(cq_bf[i2][0:R, :], psf(b, R)[:, 0:256], AF.Copy, [("ps", b), "rsq%d" % t], [("cq_bf", i2)],
                    scale=stat[0:R, s0 + 1:s0 + 2])
                ACT(ckv_f[i2][0:R, :], psf(b, R)[:, 256:512], AF.Copy, [("ps", b), "rskv%d" % t], [("ckv_f", i2)],
                    scale=stat[0:R, s0 + 3:s0 + 4])
                TT(ckv_f[i2][0:R, :], ckv_f[i2][0:R, :], gkv_bc[0:R, :], ALU.mult, [("ckv_f", i2), "gkv_bc"], [("ckv_f", i2)], eng="pool")
                P.dma("pool", cx["o_ckv"](t), ckv_f[i2][0:R, :], reads=[("ckv_f", i2)])
                CP(ckv_bf[i2][0:R, :], ckv_f[i2][0:R, :], [("ckv_f", i2)], [("ckv_bf", i2)], eng="pool")

            def TRcq(tt):
                j2 = tinfo[tt]
                to_featT(cq_bf[j2], R, tt, 0, cqnT, "cqnT", 2, [("cq_bf", j2)], gqT)
                to_featT(ckv_bf[j2], R, tt, 0, ckvnT, "ckvnT", 2, [("ckv_bf", j2)], None)

            xnorm(0)
            g1 = list(range(0, min(2, T)))
            g2 = list(range(2, T))
            for t in g1:
                A_tile(t)
            for t in range(T):
                xnorm(t)
            W, wk = wnext(SL_V, hold=1)
            Wv = W[:, 0:4096].rearrange("p (k n) -> p k n", k=8)
            for t in range(T):
                b = bank("mm")
                for c in range(8):
                    MM(psf(b, R), actT[:, c, t * R:(t + 1) * R], Wv[:, c, :], c == 0, c == 7, [("actT", t), wk], [("ps", b)])
                vdst, vkey = cx["vb_dst"](t)
                CP(vdst, psf(b, R).rearrange("p (h d) -> p h d", h=8), [("ps", b)], [vkey])
                if last:
                    i2 = cnt["tile"] % 2
                    cnt["tile"] += 1
                    CP(ostage[i2][0:R, :], psf(b, R), [("ps", b)], [("ostage", i2)], eng="act")
                    P.dma("pool", cx["o_bv"](t), ostage[i2][0:R, :], reads=[("ostage", i2)])
            for t in g1:
                TRcq(t)
            for t in g2:
                A_tile(t)
            W, wk = wnext(SL_KB)
            Wv = W[:, 0:4096].rearrange("p (k n) -> p k n", k=8)
            if last:
                for t in range(T):
                    b = bank("mm")
                    for c in range(8):
                        MM(psf(b, R), actT[:, c, t * R:(t + 1) * R], Wv[:, c, :], c == 0, c == 7, [("actT", t), wk], [("ps", b)])
                    i2 = cnt["tile"] % 2
                    cnt["tile"] += 1
                    CP(ostage[i2][0:R, :], psf(b, R), [("ps", b)], [("ostage", i2)], eng="act")
                    P.dma("pool", cx["o_bk"](t), ostage[i2][0:R, :], reads=[("ostage", i2)])
            for j in range(4):
                b = bank("mm")
                for c in range(8):
                    MM(psf(b)[:, 0:C], Wv[:, c, j * 128:(j + 1) * 128], actT[:, c, 0:C], c == 0, c == 7,
                       [("actT", t) for t in range(T)] + [wk], [("ps", b)])
                kdst, kkey = cx["kbT_dst"](j)
                if j % 2 == 0:
                    CP(kdst, psf(b)[:, 0:C], [("ps", b)], [kkey])
                else:
                    CP(kdst, psf(b)[:, 0:C], [("ps", b)], [kkey], eng="act")
            for t in g2:
                TRcq(t)
            W, wk = wnext(SL_QB)
            Wv = W[:, 0:4096].rearrange("p (k n) -> p k n", k=8)
            for j in range(4):
                b = bank("mm")
                for c in range(8):
                    MM(psf(b)[:, 0:C], Wv[:, c, j * 128:(j + 1) * 128], actT[:, c, 0:C], c == 0, c == 7,
                       [("actT", t) for t in range(T)] + [wk], [("ps", b)])
                if j % 2 == 0:
                    TS(QbT[:, j, 0:C], psf(b)[:, 0:C], 0.125, ALU.mult, [("ps", b)], [("QbT", j)])
                else:
                    ACT(QbT[:, j, 0:C], psf(b)[:, 0:C], AF.Copy, [("ps", b)], [("QbT", j)], scale=0.125)
            W, wk = wnext(SL_S)
            Wkr = W[:, S_KR:S_KR + 256].rearrange("p (k n) -> p k n", k=8)
            Wuq = W[:, S_UQ:S_UQ + 1536].rearrange("p (k n) -> p k n", k=2)
            Wuk = W[:, S_UK:S_UK + 1024].rearrange("p (k n) -> p k n", k=2)
            Wuv = W[:, S_UV:S_UV + 1024].rearrange("p (k n) -> p k n", k=2)
            sinfo = []
            for t in range(T):
                b = bank("mm")
                for c in range(2):
                    MM(psf(b, R), ckvnT[:, c, t * R:(t + 1) * R], Wuv[:, c, :], c == 0, c == 1, [("ckvnT", t), wk], [("ps", b)])
                vdst, vkey = cx["vm_dst"](t)
                CP(vdst, psf(b, R).rearrange("p (h d) -> p h d", h=8), [("ps", b)], [vkey], eng="act")
            for h in range(8):
                b = bank("mm")
                for c in range(2):
                    MM(psf(b, 64)[:, 0:C], Wuk[:, c, h * 64:(h + 1) * 64], ckvnT[:, c, 0:C], c == 0, c == 1,
                       [("ckvnT", t) for t in range(T)] + [wk], [("ps", b)])
                kd, kk = cx["ktn_dst"](h)
                if h % 2 == 0:
                    CP(kd, psf(b, 64)[:, 0:C], [("ps", b)], [kk])
                else:
                    CP(kd, psf(b, 64)[:, 0:C], [("ps", b)], [kk], eng="act")
            for t in range(T):
                ti = cx["pos_tile"](t)
                cosv = cs[0:R, ti, 0, :]
                sinv = cs[0:R, ti, 1, :]
                i2 = cnt["tile"] % 2
                cnt["tile"] += 1
                sinfo.append(i2)
                b = bank("mm")
                for c in range(8):
                    MM(psf(b, R)[:, 0:32], actT[:, c, t * R:(t + 1) * R], Wkr[:, c, :], c == 0, c == 7, [("actT", t), wk], [("ps", b)])
                CP(kr_t[i2][0:R, 0:32], psf(b, R)[:, 0:32], [("ps", b)], [("kr_t", i2)], eng="act")
                TT(kr_t[i2][0:R, 32:48], kr_t[i2][0:R, 0:16], cosv, ALU.mult, [("kr_t", i2), "cs"], [("kr_u", i2)])
                TT(kr_t[i2][0:R, 48:64], kr_t[i2][0:R, 16:32], sinv, ALU.mult, [("kr_t", i2), "cs"], [("kr_v", i2)])
                TT(kr_f[i2][0:R, 0:16], kr_t[i2][0:R, 32:48], kr_t[i2][0:R, 48:64], ALU.subtract,
                   [("kr_u", i2), ("kr_v", i2)], [("kr_f", i2)])
                TT(kr_t[i2][0:R, 32:48], kr_t[i2][0:R, 0:16], sinv, ALU.mult, [("kr_t", i2), "cs"], [("kr_u", i2)])
                TT(kr_t[i2][0:R, 48:64], kr_t[i2][0:R, 16:32], cosv, ALU.mult, [("kr_t", i2), "cs"], [("kr_v", i2)])
                TT(kr_f[i2][0:R, 16:32], kr_t[i2][0:R, 32:48], kr_t[i2][0:R, 48:64], ALU.add,
                   [("kr_u", i2), ("kr_v", i2)], [("kr_f", i2)])
                P.dma("pool", cx["o_kr"](t), kr_f[i2][0:R, :], reads=[("kr_f", i2)])
                CP(krpad[i2][0:R, 64:96], kr_f[i2][0:R, :], [("kr_f", i2)], [("krpad", i2)], eng="pool")
                qps = PS[0:R, 0:2, :].rearrange("p a n -> p (a n)")
                for c in range(2):
                    MM(qps[:, 0:512], cqnT[:, c, t * R:(t + 1) * R], Wuq[:, c, 0:512], c == 0, c == 1, [("cqnT", t), wk], [("ps", 0)])
                for c in range(2):
                    MM(qps[:, 512:768], cqnT[:, c, t * R:(t + 1) * R], Wuq[:, c, 512:768], c == 0, c == 1, [("cqnT", t), wk], [("ps", 1)])
                ACT(qf[i2][0:R].rearrange("p h d -> p (h d)"), qps[:, 0:768], AF.Copy, [("ps", 0), ("ps", 1)], [("qf", i2)],
                    scale=MLA_SCALE)
                CP(q_bf[i2][0:R, :, 0:64], qf[i2][0:R, :, 0:64], [("qf", i2)], [("q_bf", i2)], eng="pool")
                qa = qf[i2][0:R, :, 64:80]
                qb = qf[i2][0:R, :, 80:96]
                cb = cosv.unsqueeze(1).broadcast_to([R, 8, 16])
                sb_ = sinv.unsqueeze(1).broadcast_to([R, 8, 16])
                qt = q_t[i2]
                TT(qt[0:R, 0], qa, cb, ALU.mult, [("qf", i2), "cs"], ["q_t0"])
                TT(qt[0:R, 1], qb, sb_, ALU.mult, [("qf", i2), "cs"], ["q_t1"])
                TT(qt[0:R, 2], qa, sb_, ALU.mult, [("qf", i2), "cs"], ["q_t2"])
                TT(qt[0:R, 3], qb, cb, ALU.mult, [("qf", i2), "cs"], ["q_t3"])
                TT(q_bf[i2][0:R, :, 64:80], qt[0:R, 0], qt[0:R, 1], ALU.subtract, ["q_t0", "q_t1"], [("q_bf", i2)])
                TT(q_bf[i2][0:R, :, 80:96], qt[0:R, 2], qt[0:R, 3], ALU.add, ["q_t2", "q_t3"], [("q_bf", i2)])
                if t % 2 == 1 or t == T - 1:
                    for tt in range(t - (1 if t % 2 == 1 else 0), t + 1):
                        j2 = sinfo[tt]
                        bk_ = bank("misc")
                        TR(psb(bk_)[0:96, 0:R], krpad[j2][0:R, 0:96], ident[0:R, 0:R], [("krpad", j2), "ident"], [("ps", bk_)])
                        CP(krT_sb[64:96, tt * R:(tt + 1) * R], psb(bk_)[64:96, 0:R], [("ps", bk_)], ["krT_sb"])
                        bq = bank("misc")
                        for h in range(8):
                            TR(psb(bq)[0:96, h * R:(h + 1) * R], q_bf[j2][0:R, h, :], ident[0:R, 0:R], [("q_bf", j2), "ident"], [("ps", bq)])
                        CP(QT[0:96, :, tt * R:(tt + 1) * R], psb(bq)[0:96, 0:8 * R].rearrange("p (h r) -> p h r", h=8), [("ps", bq)],
                           [("QT", tt)])
            for h in range(8):
                kd, kk = cx["ktr_dst"](h)
                CP(kd, krT_sb[64:96, 0:C], ["krT_sb"], [kk], eng="pool")

        def finalize_head(cx, accb, h, mixer, tiles=None, split=False):
            R, T, C = cx["R"], cx["T"], cx["T"] * cx["R"]
            i2 = cnt["blk"] % 2
            cnt["blk"] += 1
            tl = list(range(T)) if tiles is None else tiles
            c0_, c1_ = tl[0] * R, (tl[-1] + 1) * R
            CP(oT[i2][0:65, c0_:c1_], psf(accb, 65)[:, c0_:c1_], [("ps", accb)], [("oT", i2)], eng=("dve" if mixer == 0 else "act"))

            def rest():
                finalize_rest(cx, h, mixer, tl, i2)

            if split:
                return rest
            rest()
            return None

        def finalize_rest(cx, h, mixer, tl, i2):
            R, T = cx["R"], cx["T"]
            b = bank("misc")
            for t in tl:
                TR(psf(b, R)[:, t * 65:(t + 1) * 65], oT[i2][0:65, t * R:(t + 1) * R], identf[0:65, 0:65], [("oT", i2), "identf"], [("ps", b)])
            pv = psf(b, R)[:, 0:T * 65].rearrange("p (t d) -> p t d", t=T)
            sc = 40 + 4 * i2
            t0_, t1_ = tl[0], tl[-1] + 1
            P.op("dve", lambda e: e.reciprocal(out=stat[0:R, sc + t0_:sc + t1_], in_=pv[:, t0_:t1_, 64]), [("ps", b)], [("rden", i2)])
            for t in tl:
                TS(oa[0:R, t, mixer * 512 + h * 64: mixer * 512 + (h + 1) * 64], pv[:, t, 0:64], stat[0:R, sc + t:sc + t + 1], ALU.mult,
                   [("ps", b), ("rden", i2)], [("oa", t, mixer)])

        def run_blocks(cx, blocks, LAG=2, FD=2):
            n = len(blocks)
            pend = []
            for i in range(n + LAG):
                if i < n:
                    blocks[i]["score"]()
                for p in [p for p in pend if p[0] <= i]:
                    p[1]()
                pend = [p for p in pend if p[0] > i]
                j = i - LAG
                if 0 <= j < n:
                    blocks[j]["pv"]()
                    if blocks[j].get("fin"):
                        rest = blocks[j]["fin"]()
                        if rest is not None:
                            pend.append((i + FD, rest))
            for p in pend:
                p[1]()

        def phase_M(cx, st_i):
            blocks = []
            for h in range(8):
                accb = bank("acc")
                nkt = 4 * st_i + 4
                for kt in range(nkt):
                    d = kt - 4 * st_i
                    q0 = 0 if d < 0 else 128 * d
                    blk = {}
                    st_b = {}

                    def score(h=h, kt=kt, q0=q0, st_b=st_b):
                        b = bank("sc")
                        pi = cnt["blk"] % 4
                        cnt["blk"] += 1
                        st_b["pi"] = pi
                        MM(psf(b)[:, q0:512], KT[:, h, kt * 128:(kt + 1) * 128], QT[:, h, q0:512], True, True,
                           [("KTn", h, kt // 4), ("KTr", h, kt // 4), "KTpad", "QTpad"] + [("QT", t) for t in range(4)], [("ps", b)])
                        ACT(PT[pi][:, q0:512], psf(b)[:, q0:512], AF.Exp, [("ps", b)], [("PT", pi)])
                        if kt >= 4 * st_i:
                            MEMSET(PT[pi][64:128, q0:q0 + 64], 0.0, [("PT", pi)])

                    def pv(h=h, kt=kt, d=d, q0=q0, accb=accb, st_b=st_b):
                        pi = st_b["pi"]
                        w0 = (kt * 8 + h) * 65
                        MM(psf(accb)[:, q0:512], VmF[:, w0:w0 + 128], PT[pi][:, q0:512], kt == 0, kt == 4 * st_i + 3,
                           [("Vm", kt), ("Vm", min(kt + 1, NT - 1)), ("PT", pi)], [("ps", accb)], sg=True)

                    blk["score"] = score
                    blk["pv"] = pv
                    if kt == nkt - 1:
                        blk["fin"] = (lambda h=h, accb=accb: finalize_head(cx, accb, h, 0, split=True))
                    blocks.append(blk)
            run_blocks(cx, blocks)

        def phase_B(cx, st_i):
            blocks = []
            late = {"fn": None}

            def run_late():
                f = late["fn"]
                late["fn"] = None
                if f is not None:
                    f()

            for h in range(8):
                accb = bank("acc")
                hp, hj = (h % 2) * 64, h // 2
                tiles = [t for t in range(8) if 4 * st_i - 4 + t >= 0]
                for n_, t in enumerate(tiles):
                    m = 4 * st_i - 4 + t
                    slot = (m // 4) % 2
                    kcol = slot * 512 + (m % 4) * 128
                    vt = slot * 4 + (m % 4)
                    (c0, c1), ex, exh = BAND_T[t]
                    u0, u1 = min(c0, ex), max(c1, ex)
                    qa, qb_ = 64 * u0, 64 * (u1 + 1)
                    r0 = 64 * (u0 + 8 - 2 * t)
                    blk = {}
                    st_b = {}

                    def score(h=h, hp=hp, hj=hj, kcol=kcol, qa=qa, qb_=qb_, r0=r0, slot=slot, st_b=st_b, ex=ex, exh=exh):
                        b = bank("sc")
                        pi = cnt["blk"] % 4
                        cnt["blk"] += 1
                        st_b["pi"] = pi
                        n = qb_ - qa
                        MM(psf(b)[:, qa:qb_], KbT[hp:hp + 64, hj, kcol:kcol + 128], QbT[hp:hp + 64, hj, qa:qb_], True, True,
                           [("KbT", hj, slot), ("QbT", hj)], [("ps", b)])
                        nb = max(0, min(384, r0 + n) - r0)
                        if nb < n:
                            ACT(PT[pi][:, qa + nb:qb_], psf(b)[:, qa + nb:qb_], AF.Exp, [("ps", b), "cbias"], [("PT", pi)],
                                bias=cbias[:, h:h + 1])
                        si = cnt["st"] % 2
                        if nb > 0:
                            cnt["st"] += 1
                            TT(stmp[si][:, 0:nb], psf(b)[:, qa:qa + nb], toep[:, h, r0:r0 + nb], ALU.add, [("ps", b), "toep"], [("stmp", si)])
                        run_late()

                        def mylate():
                            if nb > 0:
                                ACT(PT[pi][:, qa:qa + nb], stmp[si][:, 0:nb], AF.Exp, [("stmp", si)], [("PT", pi)])
                            zp = 64 * (1 - exh)
                            MEMSET(PT[pi][zp:zp + 64, 64 * ex:64 * (ex + 1)], 0.0, [("PT", pi)])
                            st_b["late_done"] = True

                        late["fn"] = mylate

                    def pv(h=h, vt=vt, c0=c0, c1=c1, ex=ex, exh=exh, accb=accb, first=(n_ == 0), st_b=st_b, slot=slot,
                           lastb=(n_ == len(tiles) - 1)):
                        if not st_b.get("late_done"):
                            run_late()
                        pi = st_b["pi"]
                        u0, u1 = min(c0, ex), max(c1, ex)
                        w0 = (vt * 8 + h) * 65
                        MM(psf(accb)[:, 64 * u0:64 * (u1 + 1)], VbF[:, w0:w0 + 128], PT[pi][:, 64 * u0:64 * (u1 + 1)], first, lastb,
                           [("Vb", vt), ("Vb", min(vt + 1, 7)), ("PT", pi)], [("ps", accb)], sg=True)

                    blk["score"] = score
                    blk["pv"] = pv
                    if n_ == len(tiles) - 1:
                        blk["fin"] = (lambda h=h, accb=accb: finalize_head(cx, accb, h, 1, split=True))
                    blocks.append(blk)
            run_blocks(cx, blocks)

        def phase_O(cx):
            R, T = cx["R"], cx["T"]
            for t in range(T):
                s0 = 24 + 4 * t
                xb, xk = xsb[cnt["xs"] % 2]
                cnt["xs"] += 1
                ACT(sg[0][0:R, 0:512], oa[0:R, t, 0:512], AF.Square, [("oa", t, 0)], [("sg", 0), ("ssa", t)], accum_out=stat[0:R, s0:s0 + 1])
                ACT(sg[1][0:R, 0:512], oa[0:R, t, 512:1024], AF.Square, [("oa", t, 1)], [("sg", 1), ("ssb", t)],
                    accum_out=stat[0:R, s0 + 2:s0 + 3])
                rstd_from_ss(s0, 512, R, ("ssa", t), "rsa%d" % t)
                rstd_from_ss(s0 + 2, 512, R, ("ssb", t), "rsb%d" % t)
                TS(xb[0:R, 0:512], oa[0:R, t, 0:512], stat[0:R, s0 + 1:s0 + 2], ALU.mult, [("oa", t, 0), "rsa%d" % t], xk)
                TS(xb[0:R, 512:1024], oa[0:R, t, 512:1024], stat[0:R, s0 + 3:s0 + 4], ALU.mult, [("oa", t, 1), "rsb%d" % t], xk)
                to_featT(xb, R, t, 0, actT, "actT", 8, xk, gmix)
            W0, wk0 = wnext(SL_O0)
            W1, wk1 = wnext(SL_O1, hold=1)
            Wvs = [(W0[:, 0:4096].rearrange("p (k n) -> p k n", k=8), wk0), (W1[:, 0:4096].rearrange("p (k n) -> p k n", k=8), wk1)]

            def wout(t):
                for half in range(2):
                    Wv, wk = Wvs[half]
                    b = bank("mm")
                    for c in range(8):
                        MM(psf(b, R), actT[:, c, t * R:(t + 1) * R], Wv[:, c, :], c == 0, c == 7, [("actT", t), wk], [("ps", b)])
                    xv = x1[0:R, t, half * 512:(half + 1) * 512]
                    TT(xv, psf(b, R), xv, ALU.add, [("ps", b), ("x1", t)], [("x1", t)])

            wout(0)
            for t in range(T):
                if t + 1 < T:
                    wout(t + 1)
                norm_to_featT(cx, t, x1[0:R, t, :], junk[0:R, :], JK, D, gffn, [("x1", t)], 2 * t)

        def phase_F(cx, nxt=None):
            R, T, C = cx["R"], cx["T"], cx["T"] * cx["R"]
            akeys = [("actT", t) for t in range(T)]
            if nxt is not None:
                for t in range(nxt["T"]):
                    sa, sk = xstage(t, nxt["R"])
                    P.dma("pool", sa, nxt["x_src"](t), writes=sk)
            for g in range(2):
                for gi in range(6):
                    W, wk = wnext(SL_G0 + 6 * g + gi)
                    Wv = W[:, 0:4096].rearrange("p (k n) -> p k n", k=8)
                    for ci, fc in enumerate(G_CHUNKS[6 * g + gi]):
                        lc = fc - 11 * g
                        bg = bank("mm")
                        bu = bank("mm")
                        for c in range(8):
                            MM(psf(bg)[:, 0:C], Wv[:, c, ci * 128:ci * 128 + 128], actT[:, c, 0:C], c == 0, c == 7, akeys + [wk], [("ps", bg)])
                        for c in range(8):
                            MM(psf(bu)[:, 0:C], Wv[:, c, 256 + ci * 128:256 + ci * 128 + 128], actT[:, c, 0:C], c == 0, c == 7, akeys + [wk], [("ps", bu)])
                        i2 = cnt["blk"] % 2
                        cnt["blk"] += 1
                        ACT(sg[i2][:, 0:C], psf(bg)[:, 0:C], AF.Silu, [("ps", bg)], [("sg", i2)])
                        TT(aT[:, lc, 0:C], psf(bu)[:, 0:C], sg[i2][:, 0:C], ALU.mult, [("ps", bu), ("sg", i2)], [("aT", lc)])
                if g == 1 and nxt is not None:
                    nxt["xdone"] = set()
                    for t in range(nxt["T"]):
                        sa, sk = xstage(t, nxt["R"])
                        norm_to_featT(nxt, t, sa, junk[0:nxt["R"], :], JK, D, gat, sk, 48 + 2 * t, kp="p")
                        nxt["xdone"].add(t)
                nacc = 2 * T
                for di in range(3):
                    W, wk = wnext(SL_D0 + 3 * g + di)
                    chunks = D_CHUNKS[3 * g + di]
                    Wv = W[:, 0:len(chunks) * 1024].rearrange("p (c n) -> p c n", c=len(chunks))
                    for t in range(T):
                        for half in range(2):
                            b = t * 2 + half
                            for ci, fc in enumerate(chunks):
                                lc = fc - 11 * g
                                MM(psf(b, R), aT[:, lc, t * R:(t + 1) * R], Wv[:, ci, half * 512:(half + 1) * 512],
                                   (di == 0 and ci == 0), (di == 2 and ci == len(chunks) - 1), [("aT", lc), wk], [("ps", b)])
                for t in range(T):
                    for half in range(2):
                        b = t * 2 + half
                        xv = x1[0:R, t, half * 512:(half + 1) * 512]
                        TT(xv, psf(b, R), xv, ALU.add, [("ps", b), ("x1", t)], [("x1", t)])

        def phase_Y(cx):
            R, T = cx["R"], cx["T"]
            for t in range(T):
                sc = 2 * t
                ACT(junk[0:R, :], x1[0:R, t, :], AF.Square, [("x1", t)], JK + [("ssx", t)], accum_out=stat[0:R, sc:sc + 1])
                rstd_from_ss(sc, D, R, ("ssx", t), "rsx%d" % t)
                P.op("dve", lambda e, t=t, sc=sc: e.scalar_tensor_tensor(
                    out=x1[0:R, t, :], in0=x1[0:R, t, :], scalar=stat[0:R, sc + 1:sc + 2], in1=gfin_bc[0:R, :],
                    op0=ALU.mult, op1=ALU.mult), [("x1", t), "rsx%d" % t, "gfin_bc"], [("x1", t)])
                P.dma("sp", cx["y_dst"](t), x1[0:R, t, :], reads=[("x1", t)])

        passes = []
        for sq in range(NSEQ):
            for st_i in range(NST):
                r0 = st_i * 512
                slot = st_i % 2
                cx = dict(
                    st_i=st_i,
                    R=128, T=4, last=(st_i == NST - 1),
                    x_src=lambda t, sq=sq, r0=r0: x_p.ap()[sq, r0 + t * 128:r0 + (t + 1) * 128, :],
                    o_ckv=lambda t, sq=sq, r0=r0: o_ckv_p.ap()[sq, r0 + t * 128:r0 + (t + 1) * 128, :],
                    o_kr=lambda t, sq=sq, r0=r0: o_kr_p.ap()[sq, r0 + t * 128:r0 + (t + 1) * 128, :],
                    o_bk=lambda t, sq=sq: o_bk_p.ap()[sq, t * 128:(t + 1) * 128, :],
                    o_bv=lambda t, sq=sq: o_bv_p.ap()[sq, t * 128:(t + 1) * 128, :],
                    y_dst=lambda t, sq=sq, r0=r0: y_p.ap()[sq, r0 + t * 128:r0 + (t + 1) * 128, :],
                    pos_tile=lambda t, st_i=st_i: st_i * 4 + t,
                    vb_dst=lambda t, slot=slot: (Vb[:, slot * 4 + t, :, 0:64], ("Vb", slot * 4 + t)),
                    vm_dst=lambda t, st_i=st_i: (Vm[:, st_i * 4 + t, :, 0:64], ("Vm", st_i * 4 + t)),
                    kbT_dst=lambda j, slot=slot: (KbT[:, j, slot * 512:(slot + 1) * 512], ("KbT", j, slot)),
                    ktr_dst=lambda h, r0=r0, st_i=st_i: (KT[64:96, h, r0:r0 + 512], ("KTr", h, st_i)),
                    ktn_dst=lambda h, r0=r0, st_i=st_i: (KT[0:64, h, r0:r0 + 512], ("KTn", h, st_i)),
                )
                passes.append(cx)
        for pi_, cx in enumerate(passes):
            phase_P(cx)
            if not toep_state["done"]:
                load_toeplitz()
                toep_state["done"] = True
            phase_M(cx, cx["st_i"])
            phase_B(cx, cx["st_i"])
            phase_O(cx)
            phase_F(cx, passes[pi_ + 1] if pi_ + 1 < len(passes) else None)
            phase_Y(cx)

        if NSS:
            P.barrier()
            apos[0] = 0
            R = 16
            ckvT = carve(2 * PAST).rearrange("p (c s) -> p c s", c=2)
            Vs_h = carve(NKT * 65).rearrange("p (t d) -> p t d", t=NKT)
            KTs = carve(PAST)[0:96, :]
            stg_f = carve(2048, F32)
            stg_b = carve(2048)
            krp_s = carve(NKT * 96).rearrange("p (t d) -> p t d", t=NKT)
            wukv = carve(2048)
            KTnew = carve(8 * 32)[0:96, :].rearrange("p (h s) -> p h s", h=8)
            Vnew = carve(NSS * 520).rearrange("p (t h d) -> p t h d", t=NSS, h=8)
            KbTnew = carve(4 * 32).rearrange("p (j s) -> p j s", j=4)
            Vbnew = carve(NSS * 520).rearrange("p (t h d) -> p t h d", t=NSS, h=8)
            KbTs = carve(4 * 512).rearrange("p (j s) -> p j s", j=4)
            Vbs = carve(4 * 520).rearrange("p (t h d) -> p t h d", t=4, h=8)
            assert apos[0] <= persist_end, (apos[0], persist_end)
            Wuk_s = wukv[:, 0:1024].rearrange("p (k n) -> p k n", k=2)
            Wuv_s = wukv[:, 1024:2048].rearrange("p (k n) -> p k n", k=2)
            if not toep_state["done"]:
                load_toeplitz()
                toep_state["done"] = True
            cast_slot(SL_S)
            P.dma("sp", wukv[:, 0:2048], wsc.ap()[SL_S, :, S_UK:S_UK + 2048], reads=[("wsc", SL_S)], writes=["wukv"])
            MEMSET(Vs_h[:].rearrange("p t d -> p (t d)"), 1.0, ["Vs_h"])
            MEMSET(Vnew[:].rearrange("p t h d -> p (t h d)"), 1.0, [("Vnew", t) for t in range(NSS)])
            MEMSET(Vbnew[:].rearrange("p t h d -> p (t h d)"), 1.0, [("Vbnew", t) for t in range(NSS)])
            MEMSET(Vbs[:].rearrange("p t h d -> p (t h d)"), 1.0, ["Vbs"])
            MEMSET(krp_s[:].rearrange("p t d -> p (t d)"), 0.0, ["krp_s"])
            cx = dict(
                R=16, T=NSS, last=True,
                x_src=lambda t: x_s.ap()[t * 16:(t + 1) * 16, :],
                o_ckv=lambda t: o_ckv_s.ap()[t * 16:(t + 1) * 16, :],
                o_kr=lambda t: o_kr_s.ap()[t * 16:(t + 1) * 16, :],
                o_bk=lambda t: o_bk_s.ap()[t * 16:(t + 1) * 16, :],
                o_bv=lambda t: o_bv_s.ap()[t * 16:(t + 1) * 16, :],
                y_dst=lambda t: y_s.ap()[t * 16:(t + 1) * 16, :],
                pos_tile=lambda t: NTT - 1,
                vb_dst=lambda t: (Vbnew[0:16, t, :, 0:64], ("Vbnew", t)),
                vm_dst=lambda t: (Vnew[0:16, t, :, 0:64], ("Vnew", t)),
                kbT_dst=lambda j: (KbTnew[:, j, 0:16 * NSS], ("KbTnew", j)),
                ktr_dst=lambda h: (KTnew[64:96, h, 0:16 * NSS], ("KTnew_r", h)),
                ktn_dst=lambda h: (KTnew[0:64, h, 0:16 * NSS], ("KTnew_n", h)),
            )
            phase_P(cx)
            NQ = 16
            KCH = min(8, NKT)
            for bsq in range(NSS):
                qc = slice(bsq * 16, (bsq + 1) * 16)
                for k0 in range(0, NKT, KCH):
                    sf = stg_f[:, 0:KCH * 256].rearrange("p (t c) -> p t c", t=KCH)
                    sbv = stg_b[:, 0:KCH * 256].rearrange("p (t c) -> p t c", t=KCH)
                    P.dma("sp", sf, c_ckv.ap()[bsq, k0 * 128:(k0 + KCH) * 128, :].rearrange("(t p) c -> p t c", p=128),
                          writes=["stg_f"])
                    CP(sbv, sf, ["stg_f"], ["stg_b"])
                    for k4 in range(0, KCH, 4):
                        b = bank("misc")
                        pv = psb(b)
                        for kk in range(4):
                            for c in range(2):
                                TR(pv[:, (kk * 2 + c) * 128:(kk * 2 + c + 1) * 128], sbv[:, k4 + kk, c * 128:(c + 1) * 128], ident[:],
                                   ["stg_b", "ident"], [("ps", b)])
                        pv4 = pv.rearrange("p (k c n) -> p k c n", k=4, c=2)
                        for c in range(2):
                            dst = ckvT[:, c, (k0 + k4) * 128:(k0 + k4 + 4) * 128].rearrange("p (k n) -> p k n", k=4)
                            CP(dst, pv4[:, :, c, :], [("ps", b)], [("ckvT", (k0 + k4) // 4)], eng=("act" if c else "dve"))
                krf = stg_f[:, 0:NKT * 32].rearrange("p (t c) -> p t c", t=NKT)
                P.dma("sp", krf, c_kr.ap()[bsq, :, :].rearrange("(t p) c -> p t c", p=128), writes=["stg_f"])
                CP(krp_s[:, :, 64:96], krf, ["stg_f"], ["krp_s"])
                for k8 in range(0, NKT, 8):
                    nk = min(8, NKT - k8)
                    b = bank("misc")
                    for kk in range(nk):
                        TR(psb(b)[0:96, kk * 128:(kk + 1) * 128], krp_s[:, k8 + kk, :], ident[:], ["krp_s", "ident"], [("ps", b)])
                    CP(KTs[64:96, k8 * 128:(k8 + nk) * 128], psb(b)[64:96, 0:nk * 128], [("ps", b)], ["KTs_r"])
                sf = stg_f[:, 0:2048].rearrange("p (t c) -> p t c", t=4)
                sbv = stg_b[:, 0:2048].rearrange("p (t c) -> p t c", t=4)
                P.dma("sp", sf, c_bk.ap()[bsq, :, :].rearrange("(t p) c -> p t c", p=128), writes=["stg_f"])
                CP(sbv, sf, ["stg_f"], ["stg_b"])
                for j2 in range(0, 4, 2):
                    b = bank("misc")
                    for jj in range(2):
                        for m in range(4):
                            TR(psb(b)[:, (jj * 4 + m) * 128:(jj * 4 + m + 1) * 128], sbv[:, m, (j2 + jj) * 128:(j2 + jj + 1) * 128], ident[:],
                               ["stg_b", "ident"], [("ps", b)])
                    CP(KbTs[:, j2:j2 + 2, :], psb(b).rearrange("p (j s) -> p j s", j=2), [("ps", b)], ["KbTs"])
                P.dma("sp", sf, c_bv.ap()[bsq, :, :].rearrange("(t p) c -> p t c", p=128), writes=["stg_f"])
                for m in range(4):
                    CP(Vbs[:, m, :, 0:64], sf[:, m, :].rearrange("p (h d) -> p h d", h=8), ["stg_f"], ["Vbs"], eng=("pool" if m % 2 else "dve"))
                for h in range(8):
                    hp, hj = (h % 2) * 64, h // 2
                    sb_ = bank("sc")
                    for m in range(4):
                        MM(psf(sb_)[:, m * 16:(m + 1) * 16], KbTs[hp:hp + 64, hj, m * 128:(m + 1) * 128], QbT[hp:hp + 64, hj, qc], True, True,
                           ["KbTs", ("QbT", hj)], [("ps", sb_)])
                    MM(psf(sb_, 16)[:, 64:80], KbTnew[hp:hp + 64, hj, qc], QbT[hp:hp + 64, hj, qc], True, True,
                       [("KbTnew", hj), ("QbT", hj)], [("ps", sb_)])
                    pi = cnt["blk"] % 4
                    cnt["blk"] += 1
                    si = cnt["blk"] % 2
                    ACT(PT[pi][:, 0:32], psf(sb_)[:, 0:32], AF.Exp, [("ps", sb_), "cbias"], [("PT", pi)], bias=cbias[:, h:h + 1])
                    TT(stmp[si][:, 0:16], psf(sb_)[:, 32:48], toep[:, h, 256:272], ALU.add, [("ps", sb_), "toep"], [("stmp", si)])
                    TT(stmp[si][:, 16:32], psf(sb_)[:, 48:64], toep[:, h, 128:144], ALU.add, [("ps", sb_), "toep"], [("stmp", si)])
                    TT(stmp[si][0:16, 32:48], psf(sb_, 16)[:, 64:80], toep[0:16, h, 0:16], ALU.add, [("ps", sb_), "toep"], [("stmp", si)])
                    ACT(PT[pi][:, 32:64], stmp[si][:, 0:32], AF.Exp, [("stmp", si)], [("PT", pi)])
                    ACT(PT[pi][0:16, 64:80], stmp[si][0:16, 32:48], AF.Exp, [("stmp", si)], [("PT", pi)])
                    accb = bank("acc")
                    for m in range(4):
                        MM(psf(accb, 65)[:, qc], Vbs[:, m, h, :], PT[pi][:, m * 16:(m + 1) * 16], m == 0, False,
                           ["Vbs", ("PT", pi)], [("ps", accb)], sg=True)
                    MM(psf(accb, 65)[:, qc], Vbnew[0:16, bsq, h, :], PT[pi][0:16, 64:80], False, True,
                       [("Vbnew", bsq), ("PT", pi)], [("ps", accb)], sg=True)
                    finalize_head(cx, accb, h, 1, tiles=[bsq])
                NSC = (NKT * 16 + 511) // 512
                for h in range(8):
                    for k0 in range(0, PAST, 512):
                        b = bank("mm")
                        for c in range(2):
                            MM(psf(b, 64), Wuk_s[:, c, h * 64:(h + 1) * 64], ckvT[:, c, k0:k0 + 512], c == 0, c == 1,
                               ["wukv", ("ckvT", k0 // 512)], [("ps", b)])
                        CP(KTs[0:64, k0:k0 + 512], psf(b, 64), [("ps", b)], ["KTs_n"], eng=("act" if (k0 // 512) % 2 else "dve"))
                    for k8 in range(0, NKT, 8):
                        nk = min(8, NKT - k8)
                        b = bank("mm")
                        for kk in range(nk):
                            for c in range(2):
                                MM(psf(b)[:, kk * 64:(kk + 1) * 64], ckvT[:, c, (k8 + kk) * 128:(k8 + kk + 1) * 128], Wuv_s[:, c, h * 64:(h + 1) * 64],
                                   c == 0, c == 1, ["wukv", ("ckvT", (k8 + kk) // 4)], [("ps", b)])
                        CP(Vs_h[:, k8:k8 + nk, 0:64], psf(b)[:, 0:nk * 64].rearrange("p (t d) -> p t d", t=nk), [("ps", b)], ["Vs_h"],
                           eng=("act" if (k8 // 8) % 2 else "dve"))
                    pts = []
                    for s_ in range(NSC):
                        sb_ = bank("sc")
                        kts = list(range(s_ * 32, min(NKT, (s_ + 1) * 32)))
                        for i_, kt in enumerate(kts):
                            MM(psf(sb_)[:, i_ * 16:(i_ + 1) * 16], KTs[:, kt * 128:(kt + 1) * 128], QT[0:96, h, qc], True, True,
                               ["KTs_n", "KTs_r"] + [("QT", t) for t in range(NSS)], [("ps", sb_)])
                        pi = cnt["blk"] % 4
                        cnt["blk"] += 1
                        ACT(PT[pi][:, 0:len(kts) * 16], psf(sb_)[:, 0:len(kts) * 16], AF.Exp, [("ps", sb_)], [("PT", pi)])
                        pts.append((pi, kts))
                    sb_ = bank("sc")
                    MM(psf(sb_, 16)[:, 0:16], KTnew[:, h, qc], QT[0:96, h, qc], True, True,
                       [("KTnew_n", h), ("KTnew_r", h)] + [("QT", t) for t in range(NSS)], [("ps", sb_)])
                    si = cnt["blk"] % 2
                    cnt["blk"] += 1
                    ACT(sg[si][0:16, 0:16], psf(sb_, 16)[:, 0:16], AF.Exp, [("ps", sb_)], [("sg", si)])
                    accb = bank("acc")
                    first = True
                    for pi, kts in pts:
                        for i_, kt in enumerate(kts):
                            MM(psf(accb, 65)[:, qc], Vs_h[:, kt, :], PT[pi][:, i_ * 16:(i_ + 1) * 16], first, False,
                               ["Vs_h", ("PT", pi)], [("ps", accb)], sg=True)
                            first = False
                    MM(psf(accb, 65)[:, qc], Vnew[0:16, bsq, h, :], sg[si][0:16, 0:16], False, True,
                       [("Vnew", bsq), ("sg", si)], [("ps", accb)], sg=True)
                    finalize_head(cx, accb, h, 0, tiles=[bsq])
            phase_O(cx)
            phase_F(cx)
            phase_Y(cx)

        P.finish()
    return nc


def core_inputs(inp, core, nseq, nss):
    f = lambda a: np.ascontiguousarray(a, dtype=np.float32)
    ps, ss = slice(core * nseq, (core + 1) * nseq), slice(core * nss, (core + 1) * nss)
    return dict(
        x_prompt=f(inp["x_prompt"][ps]),
        x_sample=f(inp["x_sample"][ss].reshape(nss * 16, 1024)),
        cache_mla_ckv=f(inp["cache_mla_ckv"][0, ss]),
        cache_mla_krope=f(inp["cache_mla_krope"][0, ss]),
        cache_band_k=f(inp["cache_band_k"][0, ss].reshape(nss, 512, 512)),
        cache_band_v=f(inp["cache_band_v"][0, ss].reshape(nss, 512, 512)),
        w_in=f(inp["w_in"][0]), g_attn=f(inp["g_attn"][0]), g_q=f(inp["g_q"][0]), w_uq=f(inp["w_uq"][0]),
        g_kv=f(inp["g_kv"][0]), w_uk=f(inp["w_uk"][0].reshape(256, 512)), w_uv=f(inp["w_uv"][0].reshape(256, 512)),
        rel_bias=f(inp["rel_bias"][0]), g_out_a=f(inp["g_out_a"][0]), g_out_b=f(inp["g_out_b"][0]),
        w_out=f(inp["w_out"][0]), g_ffn=f(inp["g_ffn"][0]), w_gate=f(inp["w_gate"][0]), w_up=f(inp["w_up"][0]),
        w_down=f(inp["w_down"][0]), g_final=f(inp["g_final"]),
    )


_NC_CACHE = {}


def kernel(**inputs):
    n_cores = 8
    nb, S = inputs["x_prompt"].shape[0], inputs["x_prompt"].shape[1]
    nsb = inputs["x_sample"].shape[0]
    past = inputs["cache_mla_ckv"].shape[2]
    nseq, nss = nb // n_cores, nsb // n_cores
    key = (nseq, S, nss, past)
    if key not in _NC_CACHE:
        _NC_CACHE[key] = build_program(nseq, S, nss, past)
    nc = _NC_CACHE[key]
    inp = {k: np.asarray(v) for k, v in inputs.items()}
    in_maps = [core_inputs(inp, c, nseq, nss) for c in range(n_cores)]
    res = run_bass_kernel_spmd(nc, in_maps, core_ids=list(range(n_cores)))
    r = res.results
    cat = lambda name: np.concatenate([np.asarray(r[c][name]) for c in range(n_cores)], axis=0)
    y_p = cat("y_prompt")
    y_s = cat("y_sample").reshape(nsb, 16, 1024)
    return (
        y_p, y_s,
        cat("new_ckv_prompt")[None], cat("new_kr_prompt")[None],
        cat("new_bk_prompt").reshape(nb, 512, 8, 64)[None], cat("new_bv_prompt").reshape(nb, 512, 8, 64)[None],
        cat("new_ckv_sample").reshape(nsb, 16, 256)[None], cat("new_kr_sample").reshape(nsb, 16, 32)[None],
        cat("new_bk_sample").reshape(nsb, 16, 8, 64)[None], cat("new_bv_sample").reshape(nsb, 16, 8, 64)[None],
    )
```
